# Optimizing a Trainium2 kernel written in Bass

```python
import math
import jax
import jax.numpy as jnp
from jax import lax
import numpy as np

D_MODEL = 2048
BATCH = 4
SEQ = 4096
DEPTH = 1

HEAD_DIM = 128
ROPE_DIM = HEAD_DIM // 4
ROPE_THETA = 500000.0
BLK = 128
RMS_EPS = 1e-6
DIL_GROUPS = ((128, 1), (512, 4), (2048, 16))
N_DIL_GROUPS = len(DIL_GROUPS)
DIL_HEADS = 4
DIL_WIDTH = N_DIL_GROUPS * DIL_HEADS * HEAD_DIM
DIL_OUT = DIL_HEADS * HEAD_DIM
DIFF_HEADS = 4
DIFF_QK = DIFF_HEADS * 2 * HEAD_DIM
DIFF_OUT = DIFF_HEADS * 2 * HEAD_DIM
IN_SPLITS = (DIL_WIDTH, DIL_WIDTH, DIL_WIDTH, DIFF_QK, DIFF_QK, DIFF_OUT, D_MODEL, D_MODEL)
D_IN = sum(IN_SPLITS)
D_FF = 5632

kernel_name = 'hybrid_dilated_diff_macaron_layer'


def rms_norm(x, g):
    xf = x.astype(jnp.float32)
    y = xf * lax.rsqrt(jnp.mean(xf * xf, axis=-1, keepdims=True) + RMS_EPS)
    return (y * g.astype(jnp.float32)).astype(x.dtype)


def rope_tables(positions):
    inv = ROPE_THETA ** (-jnp.arange(0, ROPE_DIM, 2, dtype=jnp.float32) / ROPE_DIM)
    ang = positions.astype(jnp.float32)[..., None] * inv
    return jnp.cos(ang), jnp.sin(ang)


def apply_rope(x, cos, sin):
    shape = cos.shape[:2] + (1,) * (x.ndim - 3) + (cos.shape[-1],)
    c = cos.reshape(shape)
    s = sin.reshape(shape)
    half = ROPE_DIM // 2
    xr = x[..., :ROPE_DIM].astype(jnp.float32)
    x1, x2 = xr[..., :half], xr[..., half:]
    rot = jnp.concatenate([x1 * c - x2 * s, x2 * c + x1 * s], axis=-1)
    return jnp.concatenate([rot.astype(x.dtype), x[..., ROPE_DIM:]], axis=-1)


def swiglu(h, w_gate, w_up, w_down):
    return (jax.nn.silu(h @ w_gate) * (h @ w_up)) @ w_down


def lambda_init(layer):
    return 0.8 - 0.6 * math.exp(-0.3 * layer)


def dilated_group(q, k, v, window, dilation):
    B, S, H, D = q.shape
    band = window // dilation
    L = S // dilation
    pad = (-L) % BLK
    nb = (L + pad) // BLK

    def strided(t):
        return t.reshape(B, L, dilation, H, D).transpose(0, 2, 1, 3, 4)

    qb = jnp.pad(strided(q), ((0, 0), (0, 0), (0, pad), (0, 0), (0, 0))).reshape(B, dilation, nb, BLK, H, D)

    def band_blocks(t):
        t = jnp.pad(strided(t), ((0, 0), (0, 0), (BLK, pad), (0, 0), (0, 0)))
        t = t.reshape(B, dilation, nb + 1, BLK, H, D)
        return jnp.concatenate([t[:, :, :-1], t[:, :, 1:]], axis=3)

    kb = band_blocks(k)
    vb = band_blocks(v)
    s = jnp.einsum('brnqhd,brnkhd->brnhqk', qb, kb).astype(jnp.float32) * (D ** -0.5)
    qi = jnp.arange(BLK)[:, None]
    kj = jnp.arange(2 * BLK)[None, :]
    dist = qi - kj + BLK
    kpos = jnp.arange(nb)[:, None, None] * BLK + kj[None] - BLK
    valid = (dist >= 0) & (dist <= band) & (kpos >= 0)
    s = jnp.where(valid[None, None, :, None], s, -jnp.inf)
    m = jnp.max(s, axis=-1, keepdims=True)
    p = jnp.exp(s - m)
    den = jnp.sum(p, axis=-1, keepdims=True)
    o = jnp.einsum('brnhqk,brnkhd->brnqhd', (p / den).astype(v.dtype), vb)
    lse = (m + jnp.log(den))[..., 0]
    o = o.reshape(B, dilation, nb * BLK, H, D)[:, :, :L].transpose(0, 2, 1, 3, 4).reshape(B, S, H, D)
    lse = lse.transpose(0, 1, 2, 4, 3).reshape(B, dilation, nb * BLK, H)[:, :, :L]
    lse = lse.transpose(0, 2, 1, 3).reshape(B, S, H)
    return o, lse


def diff_attention(q, k, v, lam):
    B, S, H, _, D = q.shape
    nb = S // BLK
    qb = q.reshape(B, nb, BLK, H, 2, D).transpose(1, 0, 2, 3, 4, 5)
    kpos = jnp.arange(S)

    def block(args):
        qblk, i = args
        s = jnp.einsum('bqhcd,bkhcd->bhcqk', qblk, k).astype(jnp.float32) * (D ** -0.5)
        qpos = i * BLK + jnp.arange(BLK)
        s = jnp.where(kpos[None, :] <= qpos[:, None], s, -jnp.inf)
        p = jax.nn.softmax(s, axis=-1)
        a = p[:, :, 0] - lam * p[:, :, 1]
        return jnp.einsum('bhqk,bkhe->bqhe', a.astype(v.dtype), v)

    o = lax.map(block, (qb, jnp.arange(nb)))
    return o.transpose(1, 0, 2, 3, 4).reshape(B, S, H, 2 * D)


def hybrid_layer(x, cos, sin, layer, ffn1_norm, ffn1_w_gate, ffn1_w_up, ffn1_w_down, mix_norm, w_in,
                 dil_q_norm, dil_k_norm, diff_q_norm, diff_k_norm, diff_lq1, diff_lk1, diff_lq2, diff_lk2,
                 diff_subln, w_dil_branch, w_diff_branch, w_out, ffn2_norm, ffn2_w_gate, ffn2_w_up, ffn2_w_down):
    B, S, _ = x.shape
    x = x + 0.5 * swiglu(rms_norm(x, ffn1_norm), ffn1_w_gate, ffn1_w_up, ffn1_w_down)

    h = rms_norm(x, mix_norm)
    proj = h @ w_in
    idx = np.cumsum(IN_SPLITS)[:-1].tolist()
    dq, dk, dv, fq, fk, fv, g_dil, g_diff = jnp.split(proj, idx, axis=-1)

    gshape = (B, S, N_DIL_GROUPS, DIL_HEADS, HEAD_DIM)
    dq = apply_rope(rms_norm(dq.reshape(gshape), dil_q_norm), cos, sin)
    dk = apply_rope(rms_norm(dk.reshape(gshape), dil_k_norm), cos, sin)
    dv = dv.reshape(gshape)
    outs = []
    lses = []
    for g, (window, dilation) in enumerate(DIL_GROUPS):
        o_g, lse_g = dilated_group(dq[:, :, g], dk[:, :, g], dv[:, :, g], window, dilation)
        outs.append(o_g)
        lses.append(lse_g)
    mix_w = jax.nn.softmax(jnp.stack(lses, axis=0), axis=0).astype(x.dtype)
    o_dil = jnp.einsum('gbsh,gbshd->bshd', mix_w, jnp.stack(outs, axis=0)).reshape(B, S, DIL_OUT)

    fshape = (B, S, DIFF_HEADS, 2, HEAD_DIM)
    fq = apply_rope(rms_norm(fq.reshape(fshape), diff_q_norm), cos, sin)
    fk = apply_rope(rms_norm(fk.reshape(fshape), diff_k_norm), cos, sin)
    fv = fv.reshape(B, S, DIFF_HEADS, 2 * HEAD_DIM)
    lam_init = lambda_init(layer)
    lam = (jnp.exp(jnp.sum(diff_lq1.astype(jnp.float32) * diff_lk1.astype(jnp.float32)))
           - jnp.exp(jnp.sum(diff_lq2.astype(jnp.float32) * diff_lk2.astype(jnp.float32))) + lam_init)
    o_diff = diff_attention(fq, fk, fv, lam)
    o_diff = (rms_norm(o_diff, diff_subln) * (1.0 - lam_init)).reshape(B, S, DIFF_OUT)

    y = jax.nn.sigmoid(g_dil) * (o_dil @ w_dil_branch) + jax.nn.sigmoid(g_diff) * (o_diff @ w_diff_branch)
    x = x + y @ w_out

    x = x + 0.5 * swiglu(rms_norm(x, ffn2_norm), ffn2_w_gate, ffn2_w_up, ffn2_w_down)
    return x


def setup_inputs(seed: int = 0) -> dict:
    key = jax.random.key(seed)
    ks = jax.random.split(key, 32)
    f32 = jnp.float32

    def dense(k, fan_in, fan_out):
        return jax.random.normal(k, (DEPTH, fan_in, fan_out), f32) * fan_in ** -0.5

    def gain(k, n):
        return 1.0 + 0.02 * jax.random.normal(k, (DEPTH, n), f32)

    def small(k, n):
        return 0.1 * jax.random.normal(k, (DEPTH, n), f32)

    x = jax.random.normal(ks[0], (BATCH, SEQ, D_MODEL), f32)
    positions = (jnp.arange(SEQ, dtype=jnp.int32)[None, :]
                 + jax.random.randint(ks[1], (BATCH, 1), 0, 1024, dtype=jnp.int32))
    return {
        'x': x,
        'positions': positions,
        'ffn1_norm': gain(ks[2], D_MODEL),
        'ffn1_w_gate': dense(ks[3], D_MODEL, D_FF),
        'ffn1_w_up': dense(ks[4], D_MODEL, D_FF),
        'ffn1_w_down': dense(ks[5], D_FF, D_MODEL),
        'mix_norm': gain(ks[6], D_MODEL),
        'w_in': dense(ks[7], D_MODEL, D_IN),
        'dil_q_norm': gain(ks[8], HEAD_DIM),
        'dil_k_norm': gain(ks[9], HEAD_DIM),
        'diff_q_norm': gain(ks[10], HEAD_DIM),
        'diff_k_norm': gain(ks[11], HEAD_DIM),
        'diff_lq1': small(ks[12], HEAD_DIM),
        'diff_lk1': small(ks[13], HEAD_DIM),
        'diff_lq2': small(ks[14], HEAD_DIM),
        'diff_lk2': small(ks[15], HEAD_DIM),
        'diff_subln': gain(ks[16], 2 * HEAD_DIM),
        'w_dil_branch': dense(ks[17], DIL_OUT, D_MODEL),
        'w_diff_branch': dense(ks[18], DIFF_OUT, D_MODEL),
        'w_out': dense(ks[19], D_MODEL, D_MODEL),
        'ffn2_norm': gain(ks[20], D_MODEL),
        'ffn2_w_gate': dense(ks[21], D_MODEL, D_FF),
        'ffn2_w_up': dense(ks[22], D_MODEL, D_FF),
        'ffn2_w_down': dense(ks[23], D_FF, D_MODEL),
    }


def reference(x, positions, ffn1_norm, ffn1_w_gate, ffn1_w_up, ffn1_w_down, mix_norm, w_in,
              dil_q_norm, dil_k_norm, diff_q_norm, diff_k_norm, diff_lq1, diff_lk1, diff_lq2, diff_lk2,
              diff_subln, w_dil_branch, w_diff_branch, w_out, ffn2_norm, ffn2_w_gate, ffn2_w_up, ffn2_w_down):
    cos, sin = rope_tables(positions)
    for l in range(DEPTH):
        x = hybrid_layer(x, cos, sin, l, ffn1_norm[l], ffn1_w_gate[l], ffn1_w_up[l], ffn1_w_down[l],
                         mix_norm[l], w_in[l], dil_q_norm[l], dil_k_norm[l], diff_q_norm[l], diff_k_norm[l],
                         diff_lq1[l], diff_lk1[l], diff_lq2[l], diff_lk2[l], diff_subln[l],
                         w_dil_branch[l], w_diff_branch[l], w_out[l], ffn2_norm[l], ffn2_w_gate[l],
                         ffn2_w_up[l], ffn2_w_down[l])
    return x
```

```python
import math
from contextlib import ExitStack

import numpy as np
import concourse.bass as bass
import concourse.mybir as mybir
from concourse.bass_utils import run_bass_kernel_spmd

F32 = mybir.dt.float32
BF16 = mybir.dt.bfloat16
I32 = mybir.dt.int32
AF = mybir.ActivationFunctionType
ALU = mybir.AluOpType
AX = mybir.AxisListType

D = 2048
DFF = 5632
DIN = 11776
TOWN = 2048
TFR = 4096
EPS = 1e-6
SCALE = 128.0 ** -0.5
PI = math.pi
ENGS = ("pe", "act", "dve", "pool", "sp")

C_INV2048, C_INV128, C_INV256, C_ONES, C_PSW, C_MDIL, C_M1, C_M2 = 0, 128, 256, 384, 512, 640, 896, 1408
NCB = 1920
F_G1, F_GM, F_G2, F_HG, F_SUB, F_INV, F_SGN, F_EPS = 0, 16, 32, 48, 52, 54, 55, 56
NCF = 57


class Op:
    __slots__ = ("eng", "fn", "deps", "kind", "ms", "val", "sem")

    def __init__(self, eng, fn, kind):
        self.eng, self.fn, self.kind = eng, fn, kind
        self.deps = []
        self.ms = False
        self.val = 0
        self.sem = None


class Buf:
    def __init__(self, name):
        self.name = name
        self.writes = {}
        self.reads = {}
        self.dsem = None
        self.dcnt = 0


class Prog:
    def __init__(self, nc, es, tag):
        self.nc, self.es, self.tag = nc, es, tag
        self.ops = {e: [] for e in ENGS}
        self.allsems = []
        self.csem = {e: self._sem(f"{tag}_{e}") for e in ENGS[:4]}
        self.dma_ops = []
        self.nb = 0

    def _sem(self, name):
        h = self.nc.alloc_semaphore(name=name)
        self.allsems.append(h)
        return h

    def sb(self, name, shape, dt):
        t = self.es.enter_context(self.nc.sbuf_tensor(f"{self.tag}_{name}", shape, dt))
        return t, Buf(name)

    def buf(self, name="b"):
        return Buf(name)

    @staticmethod
    def _key(op):
        return op.eng if op.kind == "c" else ("d", id(op.sem))

    def rec(self, eng, fn, reads=(), writes=(), kind="c", extra=()):
        op = Op(eng, fn, kind)
        deps = {}
        for b in reads:
            for k, d in b.writes.items():
                deps[id(d)] = d
        for b in writes:
            if b.reads:
                for k, d in b.reads.items():
                    deps[id(d)] = d
            else:
                for k, d in b.writes.items():
                    if not (d.kind == "c" and d.eng == eng):
                        deps[id(d)] = d
        for d in extra:
            deps[id(d)] = d
        if eng == "pe":
            deps = {i: d for i, d in deps.items() if not (d.kind == "c" and d.eng == "pe")}
        op.deps = list(deps.values())
        for d in op.deps:
            if d.kind == "c":
                d.ms = True
        self.ops[eng].append(op)
        return op

    def _commit(self, op, reads, writes):
        k = self._key(op)
        for b in reads:
            b.reads[k] = op
        for b in writes:
            if b.reads:
                b.reads = {}
                b.writes = {}
            b.writes[k] = op

    def c(self, eng, fn, reads=(), writes=()):
        op = self.rec(eng, fn, reads, writes)
        self._commit(op, reads, writes)
        return op

    def dma(self, q, out, in_, sbuf, load, reads=(), writes=()):
        if sbuf.dsem is None:
            self.nb += 1
            sbuf.dsem = self._sem(f"{self.tag}_d{self.nb}")
        r = list(reads) + ([] if load else [sbuf])
        w = list(writes) + ([sbuf] if load else [])
        op = self.rec(q, lambda e: e.dma_start(out=out, in_=in_), r, w, kind="d")
        sbuf.dcnt += 16
        op.sem = sbuf.dsem
        op.val = sbuf.dcnt
        self._commit(op, r, w)
        self.dma_ops.append(op)
        return op

    def run(self):
        for q in ("sp", "pool"):
            op = Op(q, None, "c")
            last = {}
            for d in self.dma_ops:
                last[id(d.sem)] = d
            op.deps = list(last.values())
            self.ops[q].append(op)
        for e in ENGS:
            cnt = 0
            for op in self.ops[e]:
                if op.kind == "c" and op.ms:
                    cnt += 1
                    op.val = cnt
        csem = self.csem

        def mk(e):
            ops = self.ops[e]

            def body(eng):
                waited = {}
                for op in ops:
                    for d in op.deps:
                        sem = d.sem if d.kind == "d" else csem[d.eng]
                        if waited.get(id(sem), 0) < d.val:
                            eng.wait_ge(sem, d.val)
                            waited[id(sem)] = d.val
                    if op.fn is not None:
                        ins = op.fn(eng)
                        if op.kind == "d":
                            ins.then_inc(op.sem, 16)
                        elif op.ms:
                            ins.then_inc(csem[e], 1)
            return body

        with self.nc.Block() as block:
            block.tensor(mk("pe"))
            block.scalar(mk("act"))
            block.vector(mk("dve"))
            block.gpsimd(mk("pool"))
            block.sync(mk("sp"))
        self.nc.clear_and_free_semaphores(self.allsems)
        self.nc.all_engine_barrier()


def reset_bufs(ps):
    for _, b in ps:
        b.reads, b.writes = {}, {}


class RR:
    def __init__(self, items):
        self.items, self.i = items, 0

    def next(self):
        x = self.items[self.i % len(self.items)]
        self.i += 1
        return x


def setup_phase(nc, K, ins):
    with ExitStack() as es:
        P = Prog(nc, es, "su")
        bcb, bcf, bfl, blam = Buf("cb"), Buf("cf"), Buf("fl"), Buf("lam")
        P.dma("pool", K["cb"][:], ins["cbf"], bcb, True)
        P.dma("pool", K["flag"][:], ins["flag"], bfl, True)
        P.dma("sp", K["cf"][:], ins["cf32"], bcf, True)
        lamt, blt = P.sb("lamt", [128, 4, 128], F32)
        pr, bpr = P.sb("pr", [128, 2, 128], F32)
        ss, bss = P.sb("ss", [128, 2], F32)
        ee, bee = P.sb("ee", [128, 2], F32)
        P.dma("sp", lamt[:], ins["lamv"], blt, True)
        P.c("dve", lambda e: e.tensor_tensor(out=pr[:, 0, :], in0=lamt[:, 0, :], in1=lamt[:, 1, :], op=ALU.mult), [blt], [bpr])
        P.c("dve", lambda e: e.tensor_tensor(out=pr[:, 1, :], in0=lamt[:, 2, :], in1=lamt[:, 3, :], op=ALU.mult), [blt], [bpr])
        P.c("dve", lambda e: e.reduce_sum(out=ss[:, 0:1], in_=pr[:, 0, :], axis=AX.X), [bpr], [bss])
        P.c("dve", lambda e: e.reduce_sum(out=ss[:, 1:2], in_=pr[:, 1, :], axis=AX.X), [bpr], [bss])
        P.c("act", lambda e: e.activation(out=ee[:], in_=ss[:], func=AF.Exp), [bss], [bee])
        P.c("dve", lambda e: e.tensor_tensor(out=K["nlam"][:], in0=ee[:, 1:2], in1=ee[:, 0:1], op=ALU.subtract), [bee], [blam])
        P.c("dve", lambda e: e.tensor_scalar_add(out=K["nlam"][:], in0=K["nlam"][:], scalar1=-0.2), [blam], [blam])
        P.c("dve", lambda e: e.tensor_scalar_mul(out=K["gs"][:], in0=K["cf"][:, F_SUB:F_SUB + 2], scalar1=0.8), [bcf], [blam])
        P.run()


def norm_prologue(P, ps, K, src, t0, gcol, hT, bhT, xin, sq, rstd, brstd):
    cb = K["cb"]
    for half in range(2):
        for c in range(16):
            xt, bx = xin.next()
            st, bs = sq.next()
            P.dma("sp", xt[:], src[c, :, t0 + half * 512:t0 + half * 512 + 512], bx, True)
            P.c("act", lambda e, st=st, xt=xt: e.activation(out=st[:], in_=xt[:], func=AF.Square), [bx], [bs])
            P.c("pe", lambda e, st=st, c=c, half=half: e.matmul(ps[6 + half][0][:], cb[:, C_INV2048:C_INV2048 + 128], st[:],
                                                                start=(c == 0), stop=(c == 15)), [bs], [ps[6 + half][1]])
        P.c("act", lambda e, half=half: e.activation(out=rstd[:, half * 512:half * 512 + 512], in_=ps[6 + half][0][:], func=AF.Sqrt,
                                                     bias=K["cf"][:, F_EPS:F_EPS + 1], scale=1.0), [ps[6 + half][1]], [brstd])
        P.c("dve", lambda e, half=half: e.reciprocal(out=rstd[:, half * 512:half * 512 + 512], in_=rstd[:, half * 512:half * 512 + 512]),
            [brstd], [brstd])
    for half in range(2):
        for c in range(16):
            xt, bx = xin.next()
            P.dma("sp", xt[:], src[c, :, t0 + half * 512:t0 + half * 512 + 512], bx, True)
            P.c("dve", lambda e, xt=xt, c=c, half=half: e.scalar_tensor_tensor(
                out=hT[:, c, half * 512:half * 512 + 512], in0=xt[:], scalar=K["cf"][:, gcol + c:gcol + c + 1],
                in1=rstd[:, half * 512:half * 512 + 512], op0=ALU.mult, op1=ALU.mult), [bx, brstd], [bhT])


def ffn_phase(nc, ps, K, src, dst, t0s, gcol, Wg, Wu, Wd, tag):
    with ExitStack() as es:
        reset_bufs(ps)
        P = Prog(nc, es, tag)
        hT, bhT = P.sb("hT", [128, 16, 1024], BF16)
        aT, baT = P.sb("aT", [128, 44, 1024], BF16)
        slots = RR([P.sb(f"w{i}", [128, 8192], BF16) for i in range(2)])
        xin = RR([P.sb(f"xin{i}", [128, 512], F32) for i in range(3)])
        sq = RR([P.sb(f"sq{i}", [128, 512], BF16) for i in range(2)])
        sil = RR([P.sb(f"sil{i}", [128, 512], F32) for i in range(2)])
        xout = RR([P.sb(f"xo{i}", [128, 512], F32) for i in range(2)])
        rstd, brstd = P.sb("rstd", [128, 1024], F32)
        Wgv = Wg.rearrange("(k p) n -> p k n", p=128)
        Wuv = Wu.rearrange("(k p) n -> p k n", p=128)
        Wdv = Wd.rearrange("(f p) n -> p f n", p=128)
        for t0 in t0s:
            norm_prologue(P, ps, K, src, t0, gcol, hT, bhT, xin, sq, rstd, brstd)
            u = 0
            for p in range(22):
                sl, bsl = slots.next()
                gv = sl[:, 0:4096].rearrange("p (k n) -> p k n", k=16)
                uv = sl[:, 4096:8192].rearrange("p (k n) -> p k n", k=16)
                P.dma("pool", gv, Wgv[:, :, p * 256:p * 256 + 256], bsl, True)
                P.dma("pool", uv, Wuv[:, :, p * 256:p * 256 + 256], bsl, True)
                for j in range(2):
                    for half in range(2):
                        bg, bu = ps[(u % 2) * 2], ps[(u % 2) * 2 + 1]
                        u += 1

                        def mm(e, gv=gv, uv=uv, j=j, half=half, bg=bg, bu=bu):
                            for k in range(16):
                                e.matmul(bg[0][:], gv[:, k, j * 128:j * 128 + 128], hT[:, k, half * 512:half * 512 + 512],
                                         start=(k == 0), stop=(k == 15))
                            for k in range(16):
                                ins = e.matmul(bu[0][:], uv[:, k, j * 128:j * 128 + 128], hT[:, k, half * 512:half * 512 + 512],
                                               start=(k == 0), stop=(k == 15))
                            return ins
                        P.c("pe", mm, [bsl, bhT], [bg[1], bu[1]])
                        st, bst = sil.next()
                        f = p * 2 + j
                        P.c("act", lambda e, st=st, bg=bg: e.activation(out=st[:], in_=bg[0][:], func=AF.Silu), [bg[1]], [bst])
                        P.c("dve", lambda e, st=st, bu=bu, f=f, half=half: e.tensor_tensor(
                            out=aT[:, f, half * 512:half * 512 + 512], in0=st[:], in1=bu[0][:], op=ALU.mult), [bst, bu[1]], [baT])
            for dp in range(8):
                base = 4 if dp % 2 == 0 else 0
                for fh in range(2):
                    sl, bsl = slots.next()
                    dv = sl[:, 0:5632].rearrange("p (f n) -> p f n", f=22)
                    P.dma("pool", dv, Wdv[:, fh * 22:fh * 22 + 22, dp * 256:dp * 256 + 256], bsl, True)
                    for j in range(2):
                        for half in range(2):
                            bk = ps[base + j * 2 + half]

                            def mm(e, dv=dv, j=j, half=half, bk=bk, fh=fh):
                                for f in range(22):
                                    ins = e.matmul(bk[0][:], dv[:, f, j * 128:j * 128 + 128],
                                                   aT[:, fh * 22 + f, half * 512:half * 512 + 512],
                                                   start=(fh == 0 and f == 0), stop=(fh == 1 and f == 21))
                                return ins
                            P.c("pe", mm, [bsl, baT], [bk[1]])
                for j in range(2):
                    for half in range(2):
                        bk = ps[base + j * 2 + half]
                        c = dp * 2 + j
                        xt, bx = xin.next()
                        xo, bxo = xout.next()
                        tt = t0 + half * 512
                        P.dma("sp", xt[:], src[c, :, tt:tt + 512], bx, True)
                        P.c("dve", lambda e, xo=xo, xt=xt, bk=bk: e.scalar_tensor_tensor(
                            out=xo[:], in0=bk[0][:], scalar=0.5, in1=xt[:], op0=ALU.mult, op1=ALU.add), [bk[1], bx], [bxo])
                        P.dma("sp", dst[c, :, tt:tt + 512], xo[:], bxo, False)
        P.run()


def proj_phase(nc, ps, K, x1T, win, posd, S, tiles=(0, 1, 2, 3), pfilter=None):
    with ExitStack() as es:
        reset_bufs(ps)
        P = Prog(nc, es, "pj")
        cb, cf = K["cb"], K["cf"]
        hT, bhT = P.sb("hT", [128, 16, 1024], BF16)
        slots = RR([P.sb(f"w{i}", [128, 8192], BF16) for i in range(3)])
        xin = RR([P.sb(f"xin{i}", [128, 512], F32) for i in range(3)])
        sq = RR([P.sb(f"sq{i}", [128, 512], BF16) for i in range(2)])
        qg = RR([P.sb(f"qg{i}", [128, 512], BF16) for i in range(2)])
        rs = RR([P.sb(f"rs{i}", [128, 512], F32) for i in range(2)])
        qo = RR([P.sb(f"qo{i}", [128, 512], BF16) for i in range(3)])
        t1 = RR([P.sb(f"t1{i}", [32, 512], F32) for i in range(2)])
        t2 = RR([P.sb(f"t2{i}", [32, 512], F32) for i in range(2)])
        vst = RR([P.sb(f"vs{i}", [128, 512], BF16) for i in range(3)])
        gst = RR([P.sb(f"gs{i}", [128, 512], F32) for i in range(3)])
        rstd, brstd = P.sb("rstd", [128, 1024], F32)
        posi, bposi = P.sb("posi", [32, 1024], I32)
        ang, bang = P.sb("ang", [32, 1024], F32)
        a1, ba1 = P.sb("a1", [32, 1024], F32)
        ki, bki = P.sb("ki", [32, 1024], I32)
        kf, bkf = P.sb("kf", [32, 1024], F32)
        Ct, bC = P.sb("C", [32, 1024], F32)
        St, bS = P.sb("S", [32, 1024], F32)
        npi, bnpi = P.sb("npi", [32, 1], F32)
        P.c("dve", lambda e: e.memset(npi[:], -PI), [], [bnpi])
        Wv = win.rearrange("(k p) n -> p k n", p=128)
        panels = []
        for i in range(3):
            panels.append((i * 512, "qk", ("Qd", i * 4, 0)))
        for i in range(3):
            panels.append((1536 + i * 512, "qk", ("Kd", i * 4, 1)))
        for i in range(3):
            panels.append((3072 + i * 512, "v", ("Vd", i * 512)))
        for i in range(2):
            panels.append((4608 + i * 512, "qk", ("Qf", i * 4, 2)))
        for i in range(2):
            panels.append((5632 + i * 512, "qk", ("Kf", i * 4, 3)))
        for i in range(2):
            panels.append((6656 + i * 512, "v", ("Vf", i * 512)))
        for i in range(8):
            panels.append((7680 + i * 512, "g", (i * 4,)))
        u = 0
        if pfilter is not None:
            panels = [p for i, p in enumerate(panels) if i in pfilter]
        for ti in tiles:
            t0 = ti * 1024
            own = ti >= 2
            norm_prologue(P, ps, K, x1T, t0, F_GM, hT, bhT, xin, sq, rstd, brstd)
            P.dma("sp", posi[:], posd[:, t0:t0 + 1024], bposi, True)
            P.c("dve", lambda e: e.tensor_copy(out=ang[:], in_=posi[:]), [bposi], [bang])
            P.c("dve", lambda e: e.tensor_scalar_mul(out=ang[:], in0=ang[:], scalar1=cf[0:32, F_INV:F_INV + 1]), [bang], [bang])
            for (off, Tt, bT) in ((0.75, Ct, bC), (0.5, St, bS)):
                P.c("dve", lambda e, off=off: e.tensor_scalar(out=a1[:], in0=ang[:], scalar1=1.0 / (2 * PI), scalar2=off, op0=ALU.mult,
                                                              op1=ALU.add), [bang], [ba1])
                P.c("dve", lambda e: e.tensor_copy(out=ki[:], in_=a1[:]), [ba1], [bki])
                P.c("dve", lambda e: e.tensor_copy(out=kf[:], in_=ki[:]), [bki], [bkf])
                P.c("dve", lambda e: e.tensor_tensor(out=a1[:], in0=a1[:], in1=kf[:], op=ALU.subtract), [ba1, bkf], [ba1])
                P.c("dve", lambda e: e.scalar_tensor_tensor(out=a1[:], in0=a1[:], scalar=0.0, in1=a1[:], op0=ALU.is_lt, op1=ALU.add),
                    [ba1], [ba1])
                P.c("act", lambda e, Tt=Tt: e.activation(out=Tt[:], in_=a1[:], func=AF.Sin, bias=npi[:], scale=2 * PI), [ba1, bnpi], [bT])
            P.c("dve", lambda e: e.tensor_scalar_mul(out=St[:], in0=St[:], scalar1=cf[0:32, F_SGN:F_SGN + 1]), [bS], [bS])
            for (c0, kind, info) in panels:
                if not own and (kind == "g" or info[0] in ("Qd", "Qf")):
                    continue
                sl, bsl = slots.next()
                wv = sl[:, :].rearrange("p (k n) -> p k n", k=16)
                P.dma("pool", wv, Wv[:, :, c0:c0 + 512], bsl, True)
                if kind == "v":
                    dstv = S[info[0]]
                    for tb in range(8):
                        bk = ps[u % 4]
                        u += 1

                        def mm(e, wv=wv, tb=tb, bk=bk):
                            for k in range(16):
                                ins = e.matmul(bk[0][:], hT[:, k, tb * 128:tb * 128 + 128], wv[:, k, :], start=(k == 0), stop=(k == 15))
                            return ins
                        P.c("pe", mm, [bsl, bhT], [bk[1]])
                        vt, bvt = vst.next()
                        P.c("act", lambda e, vt=vt, bk=bk: e.activation(out=vt[:], in_=bk[0][:], func=AF.Copy), [bk[1]], [bvt])
                        P.dma("sp", dstv[t0 + tb * 128:t0 + tb * 128 + 128, info[1]:info[1] + 512], vt[:], bvt, False)
                    continue
                for j in range(4):
                    for half in range(2):
                        bk = ps[u % 4]
                        u += 1
                        hs = slice(half * 512, half * 512 + 512)

                        def mm(e, wv=wv, j=j, hs=hs, bk=bk):
                            for k in range(16):
                                ins = e.matmul(bk[0][:], wv[:, k, j * 128:j * 128 + 128], hT[:, k, hs], start=(k == 0), stop=(k == 15))
                            return ins
                        P.c("pe", mm, [bsl, bhT], [bk[1]])
                        if kind == "g":
                            gt, bgt = gst.next()
                            P.c("act", lambda e, gt=gt, bk=bk: e.activation(out=gt[:], in_=bk[0][:], func=AF.Sigmoid), [bk[1]], [bgt])
                            tt = t0 - 2048 + half * 512
                            P.dma("sp", S["SG"][info[0] + j, :, tt:tt + 512], gt[:], bgt, False)
                            continue
                        name, ch0, gi = info
                        st, bs = sq.next()
                        qt, bq = qg.next()
                        rt, brt = rs.next()
                        ot, bo = qo.next()
                        x1, bx1 = t1.next()
                        x2, bx2 = t2.next()
                        bss, bsw = ps[4 + (u % 2)], ps[6 + (u % 2)]
                        P.c("act", lambda e, st=st, bk=bk: e.activation(out=st[:], in_=bk[0][:], func=AF.Square), [bk[1]], [bs])
                        P.c("act", lambda e, qt=qt, bk=bk, gi=gi: e.activation(out=qt[:], in_=bk[0][:], func=AF.Copy,
                                                                               scale=cf[:, F_HG + gi:F_HG + gi + 1]), [bk[1]], [bq])
                        P.c("pe", lambda e, st=st, bss=bss: e.matmul(bss[0][:], cb[:, C_INV128:C_INV128 + 128], st[:], start=True, stop=True),
                            [bs], [bss[1]])
                        P.c("pe", lambda e, qt=qt, bsw=bsw: e.matmul(bsw[0][:], cb[:, C_PSW:C_PSW + 128], qt[:], start=True, stop=True),
                            [bq], [bsw[1]])
                        P.c("act", lambda e, rt=rt, bss=bss: e.activation(out=rt[:], in_=bss[0][:], func=AF.Sqrt, bias=cf[:, F_EPS:F_EPS + 1],
                                                                         scale=1.0), [bss[1]], [brt])
                        P.c("dve", lambda e, rt=rt: e.reciprocal(out=rt[:], in_=rt[:]), [brt], [brt])
                        P.c("dve", lambda e, ot=ot, qt=qt, rt=rt: e.tensor_tensor(out=ot[:], in0=qt[:], in1=rt[:], op=ALU.mult), [bq, brt], [bo])
                        P.c("dve", lambda e, x1=x1, qt=qt, hs=hs: e.tensor_tensor(out=x1[:], in0=qt[0:32, :], in1=Ct[:, hs], op=ALU.mult),
                            [bq, bC], [bx1])
                        P.c("dve", lambda e, x2=x2, bsw=bsw, hs=hs: e.tensor_tensor(out=x2[:], in0=bsw[0][0:32, :], in1=St[:, hs], op=ALU.mult),
                            [bsw[1], bS], [bx2])
                        P.c("dve", lambda e, x1=x1, x2=x2: e.tensor_tensor(out=x1[:], in0=x1[:], in1=x2[:], op=ALU.add), [bx1, bx2], [bx1])
                        P.c("dve", lambda e, ot=ot, x1=x1, rt=rt: e.tensor_tensor(out=ot[0:32, :], in0=x1[:], in1=rt[0:32, :], op=ALU.mult),
                            [bx1, brt, bo], [bo])
                        if name[0] == "Q":
                            tt = t0 - 2048 + half * 512
                        else:
                            tt = t0 + half * 512
                        P.dma("sp", S[name][ch0 + j, :, tt:tt + 512], ot[:], bo, False)
        P.run()


def dil_phase(nc, ps, K, S):
    with ExitStack() as es:
        reset_bufs(ps)
        P = Prog(nc, es, "dl")
        cb = K["cb"]
        Vg, bVg = P.sb("Vg", [128, 32, 512], BF16)
        KT = RR([P.sb(f"KT{i}", [128, 4096], BF16) for i in range(2)])
        QT = RR([P.sb(f"QT{i}", [128, 2048], BF16) for i in range(2)])
        Ua = [P.sb(f"Ua{h}", [128, 2048], F32) for h in range(4)]
        Da = [P.sb(f"Da{h}", [128, 2048], F32) for h in range(4)]
        Pt = RR([P.sb(f"Pt{i}", [128, 256], BF16) for i in range(3)])
        Pm = RR([P.sb(f"Pm{i}", [128, 256], BF16) for i in range(3)])
        od = RR([P.sb(f"od{i}", [128, 2048], BF16) for i in range(2)])
        ones = cb[:, C_ONES:C_ONES + 128]
        flag = K["flag"][:, :]
        u = 0
        for g, r in enumerate((1, 4, 16)):
            nmb = 32 // r
            for mb in range(nmb):
                src = S["Vd"][mb * 128 * r:(mb + 1) * 128 * r, g * 512:(g + 1) * 512].rearrange("(p r) n -> p r n", r=r)
                P.dma("sp", Vg[:, mb * r:(mb + 1) * r, :], src, bVg, True)
            for h in range(4):
                kt, bkt = KT.next()
                qt, bqt = QT.next()
                P.dma("sp", kt[:], S["Kd"][g * 4 + h], bkt, True)
                P.dma("sp", qt[:], S["Qd"][g * 4 + h], bqt, True)
                ua, bua = Ua[h]
                da, bda = Da[h]
                for c in range(r):
                    for mb in range(nmb // 2, nmb):
                        bS_, bU_ = ps[u % 4], ps[4 + u % 4]
                        u += 1
                        q0 = mb * 128 * r + c - 2048
                        qsl = slice(q0, q0 + 127 * r + 1, r)
                        kp = slice((mb - 1) * 128 * r + c, (mb - 1) * 128 * r + c + 127 * r + 1, r)
                        kc = slice(mb * 128 * r + c, mb * 128 * r + c + 127 * r + 1, r)

                        def mm1(e, kt=kt, qt=qt, qsl=qsl, kp=kp, kc=kc, bS_=bS_):
                            e.matmul(bS_[0][:, 0:128], kt[:, kp], qt[:, qsl], start=True, stop=True)
                            return e.matmul(bS_[0][:, 128:256], kt[:, kc], qt[:, qsl], start=True, stop=True)
                        P.c("pe", mm1, [bkt, bqt], [bS_[1]])
                        pt, bpt = Pt.next()
                        pm, bpm = Pm.next()
                        P.c("act", lambda e, pt=pt, bS_=bS_: e.activation(out=pt[:], in_=bS_[0][:, 0:256], func=AF.Exp, scale=SCALE),
                            [bS_[1]], [bpt])
                        P.c("pool", lambda e, pt=pt, pm=pm: e.tensor_tensor(out=pm[:], in0=pt[:], in1=cb[:, C_MDIL:C_MDIL + 256], op=ALU.mult),
                            [bpt], [bpm])
                        prev_pref = (mb - 1) < nmb // 2
                        hc = slice(h * 128, h * 128 + 128)

                        def mm2(e, pm=pm, mb=mb, c=c, r=r, hc=hc, bU_=bU_, prev_pref=prev_pref):
                            e.matmul(bU_[0][:, 0:128], Vg[:, (mb - 1) * r + c, hc], pm[:, 0:128], start=True, stop=False)
                            e.matmul(bU_[0][:, 0:128], Vg[:, mb * r + c, hc], pm[:, 128:256], start=False, stop=True)
                            e.matmul(bU_[0][:, 128:256], flag if prev_pref else ones, pm[:, 0:128], start=True, stop=False)
                            return e.matmul(bU_[0][:, 128:256], ones, pm[:, 128:256], start=False, stop=True)
                        P.c("pe", mm2, [bpm, bVg], [bU_[1]])
                        if g == 0:
                            P.c("dve", lambda e, ua=ua, bU_=bU_, qsl=qsl: e.tensor_copy(out=ua[:, qsl], in_=bU_[0][:, 0:128]), [bU_[1]], [bua])
                            P.c("dve", lambda e, da=da, bU_=bU_, qsl=qsl: e.tensor_copy(out=da[:, qsl], in_=bU_[0][:, 128:256]), [bU_[1]], [bda])
                        else:
                            P.c("dve", lambda e, ua=ua, bU_=bU_, qsl=qsl: e.tensor_tensor(out=ua[:, qsl], in0=ua[:, qsl], in1=bU_[0][:, 0:128],
                                                                                         op=ALU.add), [bU_[1], bua], [bua])
                            P.c("dve", lambda e, da=da, bU_=bU_, qsl=qsl: e.tensor_tensor(out=da[:, qsl], in0=da[:, qsl], in1=bU_[0][:, 128:256],
                                                                                         op=ALU.add), [bU_[1], bda], [bda])
        for h in range(4):
            ua, bua = Ua[h]
            da, bda = Da[h]
            ot, bo = od.next()
            P.c("dve", lambda e, da=da: e.reciprocal(out=da[:], in_=da[:]), [bda], [bda])
            P.c("dve", lambda e, ot=ot, ua=ua, da=da: e.tensor_tensor(out=ot[:], in0=ua[:], in1=da[:], op=ALU.mult), [bua, bda], [bo])
            P.dma("sp", S["Od"][h], ot[:], bo, False)
        P.run()


def diff_phase(nc, ps, K, S):
    with ExitStack() as es:
        reset_bufs(ps)
        P = Prog(nc, es, "df")
        cb = K["cb"]
        KT = RR([P.sb(f"KT{i}", [128, 2, 4096], BF16) for i in range(2)])
        QT = RR([P.sb(f"QT{i}", [128, 2, 2048], BF16) for i in range(2)])
        Vh = RR([P.sb(f"Vh{i}", [128, 32, 256], BF16) for i in range(2)])
        Pt = RR([P.sb(f"Pt{i}", [128, 512], BF16) for i in range(3)])
        rd, brd = P.sb("rd", [128, 512], F32)
        o1, bo1 = P.sb("o1", [128, 512], F32)
        o2, bo2 = P.sb("o2", [128, 512], F32)
        sqt, bsq = P.sb("sq", [128, 512], BF16)
        rst, brs = P.sb("rs", [128, 256], F32)
        of = RR([P.sb(f"of{i}", [128, 512], BF16) for i in range(2)])
        ones = cb[:, C_ONES:C_ONES + 128]
        flag = K["flag"][:, :]
        nlam = K["nlam"]
        gs = K["gs"]
        u = 0
        for h in range(4):
            kt, bkt = KT.next()
            qt, bqt = QT.next()
            vh, bvh = Vh.next()
            for cpt in range(2):
                P.dma("sp", kt[:, cpt, :], S["Kf"][h * 2 + cpt], bkt, True)
                P.dma("sp", qt[:, cpt, :], S["Qf"][h * 2 + cpt], bqt, True)
            P.dma("sp", vh[:], S["Vf"][:, h * 256:(h + 1) * 256].rearrange("(kb p) n -> p kb n", p=128), bvh, True)
            for qi in range(8):
                nkb = 18 + 2 * qi
                bU = (ps[2], ps[3])
                bD = ps[4]
                qs = slice(qi * 256, qi * 256 + 256)
                for kb in range(nkb):
                    bS_ = ps[u % 2]
                    u += 1
                    ks = slice(kb * 128, kb * 128 + 128)

                    def mm1(e, kt=kt, qt=qt, ks=ks, qs=qs, bS_=bS_):
                        e.matmul(bS_[0][:, 0:256], kt[:, 0, ks], qt[:, 0, qs], start=True, stop=True)
                        return e.matmul(bS_[0][:, 256:512], kt[:, 1, ks], qt[:, 1, qs], start=True, stop=True)
                    P.c("pe", mm1, [bkt, bqt], [bS_[1]])
                    pt, bpt = Pt.next()
                    P.c("act", lambda e, pt=pt, bS_=bS_: e.activation(out=pt[:], in_=bS_[0][:], func=AF.Exp, scale=SCALE), [bS_[1]], [bpt])
                    if kb >= nkb - 2:
                        mc = C_M1 if kb == nkb - 2 else C_M2
                        P.c("pool", lambda e, pt=pt, mc=mc: e.tensor_tensor(out=pt[:], in0=pt[:], in1=cb[:, mc:mc + 512], op=ALU.mult),
                            [bpt], [bpt])
                    first, last = kb == 0, kb == nkb - 1

                    def mm2(e, pt=pt, vh=vh, kb=kb, first=first, last=last):
                        for cpt in range(2):
                            for dvc in range(2):
                                e.matmul(bU[cpt][0][:, dvc * 256:dvc * 256 + 256], vh[:, kb, dvc * 128:dvc * 128 + 128],
                                         pt[:, cpt * 256:cpt * 256 + 256], start=(first and dvc == 0), stop=last,
                                         skip_group_check=True)
                        for cpt in range(2):
                            ins = e.matmul(bD[0][:, cpt * 256:cpt * 256 + 256], flag if kb < 16 else ones,
                                           pt[:, cpt * 256:cpt * 256 + 256], start=(first and cpt == 0), stop=last,
                                           skip_group_check=True)
                        return ins
                    P.c("pe", mm2, [bpt, bvh], [bU[0][1], bU[1][1], bD[1]])
                P.c("dve", lambda e: e.reciprocal(out=rd[:], in_=bD[0][:]), [bD[1]], [brd])
                for dvc in range(2):
                    ds = slice(dvc * 256, dvc * 256 + 256)
                    P.c("dve", lambda e, ds=ds: e.tensor_tensor(out=o1[:, ds], in0=bU[0][0][:, ds], in1=rd[:, 0:256], op=ALU.mult),
                        [bU[0][1], brd], [bo1])
                    P.c("dve", lambda e, ds=ds: e.tensor_tensor(out=o2[:, ds], in0=bU[1][0][:, ds], in1=rd[:, 256:512], op=ALU.mult),
                        [bU[1][1], brd], [bo2])
                P.c("dve", lambda e: e.scalar_tensor_tensor(out=o1[:], in0=o2[:], scalar=nlam[:, 0:1], in1=o1[:], op0=ALU.mult, op1=ALU.add),
                    [bo1, bo2], [bo1])
                P.c("act", lambda e: e.activation(out=sqt[:], in_=o1[:], func=AF.Square), [bo1], [bsq])
                bN = ps[5]

                def mm3(e):
                    e.matmul(bN[0][:, 0:256], cb[:, C_INV256:C_INV256 + 128], sqt[:, 0:256], start=True, stop=False)
                    return e.matmul(bN[0][:, 0:256], cb[:, C_INV256:C_INV256 + 128], sqt[:, 256:512], start=False, stop=True)
                P.c("pe", mm3, [bsq], [bN[1]])
                P.c("act", lambda e: e.activation(out=rst[:], in_=bN[0][:, 0:256], func=AF.Sqrt, bias=K["cf"][:, F_EPS:F_EPS + 1], scale=1.0),
                    [bN[1]], [brs])
                P.c("dve", lambda e: e.reciprocal(out=rst[:], in_=rst[:]), [brs], [brs])
                ot, bot = of.next()
                for dvc in range(2):
                    ds = slice(dvc * 256, dvc * 256 + 256)
                    P.c("dve", lambda e, ot=ot, ds=ds, dvc=dvc: e.scalar_tensor_tensor(out=ot[:, ds], in0=o1[:, ds], scalar=gs[:, dvc:dvc + 1],
                                                                                      in1=rst[:], op0=ALU.mult, op1=ALU.mult), [bo1, brs], [bot])
                for dvc in range(2):
                    P.dma("sp", S["Of"][h * 2 + dvc, :, qi * 256:qi * 256 + 256], ot[:, dvc * 256:dvc * 256 + 256], bot, False)
        P.run()


def out_phase(nc, ps, K, S, WA, WB, WO):
    with ExitStack() as es:
        reset_bufs(ps)
        P = Prog(nc, es, "op")
        wa, bwa = P.sb("wa", [128, 4, 2048], BF16)
        wb, bwb = P.sb("wb", [128, 8, 2048], BF16)
        wo, bwo = P.sb("wo", [128, 16, 2048], BF16)
        odt = RR([P.sb(f"od{i}", [128, 4, 512], BF16) for i in range(1)])
        oft = RR([P.sb(f"of{i}", [128, 8, 512], BF16) for i in range(1)])
        yT = RR([P.sb(f"y{i}", [128, 16, 512], BF16) for i in range(1)])
        sgd = RR([P.sb(f"sgd{i}", [128, 512], F32) for i in range(2)])
        sgf = RR([P.sb(f"sgf{i}", [128, 512], F32) for i in range(2)])
        ta = RR([P.sb(f"ta{i}", [128, 512], F32) for i in range(2)])
        tb_ = RR([P.sb(f"tb{i}", [128, 512], F32) for i in range(2)])
        xin = RR([P.sb(f"xin{i}", [128, 512], F32) for i in range(2)])
        xout = RR([P.sb(f"xo{i}", [128, 512], F32) for i in range(2)])
        P.dma("pool", wa[:], WA.rearrange("(k p) n -> p k n", p=128), bwa, True)
        for i in range(2):
            P.dma("pool", wb[:, i * 4:i * 4 + 4, :], WB.rearrange("(k p) n -> p k n", p=128)[:, i * 4:i * 4 + 4, :], bwb, True)
        for i in range(4):
            P.dma("pool", wo[:, i * 4:i * 4 + 4, :], WO.rearrange("(k p) n -> p k n", p=128)[:, i * 4:i * 4 + 4, :], bwo, True)
        u = 0
        for ti in range(4):
            ts = slice(ti * 512, ti * 512 + 512)
            od, bod = odt.next()
            of, bof = oft.next()
            y, by = yT.next()
            for k in range(4):
                P.dma("sp", od[:, k, :], S["Od"][k, :, ts], bod, True)
            for k in range(8):
                P.dma("sp", of[:, k, :], S["Of"][k, :, ts], bof, True)
            for dc in range(16):
                bA, bB = ps[(u % 2) * 2], ps[(u % 2) * 2 + 1]
                u += 1
                cs = slice(dc * 128, dc * 128 + 128)

                def mm(e, od=od, of=of, cs=cs, bA=bA, bB=bB):
                    for k in range(4):
                        e.matmul(bA[0][:], wa[:, k, cs], od[:, k, :], start=(k == 0), stop=(k == 3))
                    for k in range(8):
                        ins = e.matmul(bB[0][:], wb[:, k, cs], of[:, k, :], start=(k == 0), stop=(k == 7))
                    return ins
                P.c("pe", mm, [bwa, bwb, bod, bof], [bA[1], bB[1]])
                g1, bg1 = sgd.next()
                g2, bg2 = sgf.next()
                a_, ba_ = ta.next()
                b_, bb_ = tb_.next()
                P.dma("sp", g1[:], S["SG"][dc, :, ts], bg1, True)
                P.dma("sp", g2[:], S["SG"][16 + dc, :, ts], bg2, True)
                P.c("dve", lambda e, a_=a_, g1=g1, bA=bA: e.tensor_tensor(out=a_[:], in0=g1[:], in1=bA[0][:], op=ALU.mult), [bg1, bA[1]], [ba_])
                P.c("dve", lambda e, b_=b_, g2=g2, bB=bB: e.tensor_tensor(out=b_[:], in0=g2[:], in1=bB[0][:], op=ALU.mult), [bg2, bB[1]], [bb_])
                P.c("pool", lambda e, y=y, dc=dc, a_=a_, b_=b_: e.tensor_tensor(out=y[:, dc, :], in0=a_[:], in1=b_[:], op=ALU.add),
                    [ba_, bb_], [by])
            for dc2 in range(16):
                bC_ = ps[4 + dc2 % 4]
                cs = slice(dc2 * 128, dc2 * 128 + 128)

                def mm(e, y=y, cs=cs, bC_=bC_):
                    for k in range(16):
                        ins = e.matmul(bC_[0][:], wo[:, k, cs], y[:, k, :], start=(k == 0), stop=(k == 15))
                    return ins
                P.c("pe", mm, [bwo, by], [bC_[1]])
                xt, bx = xin.next()
                xo, bxo = xout.next()
                P.dma("sp", xt[:], S["x1T"][dc2, :, 2048 + ti * 512:2048 + ti * 512 + 512], bx, True)
                P.c("dve", lambda e, xo=xo, xt=xt, bC_=bC_: e.tensor_tensor(out=xo[:], in0=bC_[0][:], in1=xt[:], op=ALU.add), [bC_[1], bx], [bxo])
                P.dma("sp", S["x2T"][dc2, :, ts], xo[:], bxo, False)
        P.run()


IN_SHAPES = {
    "xT": ([16, 128, TFR], F32), "pos": ([32, TFR], I32), "cbf": ([128, NCB], F32), "cf32": ([128, NCF], F32),
    "flag": ([128, 128], F32), "lamv": ([128, 4, 128], F32),
    "ffn1_wg": ([D, DFF], F32), "ffn1_wu": ([D, DFF], F32), "ffn1_wd": ([DFF, D], F32),
    "ffn2_wg": ([D, DFF], F32), "ffn2_wu": ([D, DFF], F32), "ffn2_wd": ([DFF, D], F32),
    "w_in": ([D, DIN], F32), "w_a": ([512, D], F32), "w_b": ([1024, D], F32), "w_o": ([D, D], F32),
}


class LazyIns(dict):
    def __init__(self, nc):
        super().__init__()
        self.nc = nc

    def __missing__(self, name):
        shape, dt = IN_SHAPES[name]
        ap = self.nc.dram_tensor(name, shape, dt, kind="ExternalInput").ap()
        self[name] = ap
        return ap


def build(debug=False, upto=7, dump=()):
    nc = bass.Bass("TRN2", target_bir_lowering=False)
    ins = LazyIns(nc)
    kind = "ExternalOutput" if debug else "Internal"
    S = {}

    def scr(name, shape, dt):
        S[name] = nc.dram_tensor(name, shape, dt, kind=("ExternalOutput" if name in dump else "Internal")).ap()
    scr("x1T", [16, 128, TFR], F32)
    scr("Qd", [12, 128, TOWN], BF16)
    scr("Kd", [12, 128, TFR], BF16)
    scr("Vd", [TFR, 1536], BF16)
    scr("Qf", [8, 128, TOWN], BF16)
    scr("Kf", [8, 128, TFR], BF16)
    scr("Vf", [TFR, 1024], BF16)
    scr("SG", [32, 128, TOWN], F32)
    scr("Od", [4, 128, TOWN], BF16)
    scr("Of", [8, 128, TOWN], BF16)
    scr("x2T", [16, 128, TOWN], F32)
    outT = nc.dram_tensor("outT", [16, 128, TOWN], F32, kind="ExternalOutput").ap()
    with ExitStack() as es:
        ps = []
        for i in range(8):
            t = es.enter_context(nc.psum_tensor(f"ps{i}", [128, 512], F32))
            ps.append((t, Buf(f"ps{i}")))
        K = {}
        K["cb"] = es.enter_context(nc.sbuf_tensor("K_cb", [128, NCB], BF16))
        K["cf"] = es.enter_context(nc.sbuf_tensor("K_cf", [128, NCF], F32))
        K["flag"] = es.enter_context(nc.sbuf_tensor("K_flag", [128, 128], BF16))
        K["nlam"] = es.enter_context(nc.sbuf_tensor("K_nlam", [128, 1], F32))
        K["gs"] = es.enter_context(nc.sbuf_tensor("K_gs", [128, 2], F32))
        setup_phase(nc, K, ins)
        if upto >= 2:
            ffn_phase(nc, ps, K, ins["xT"], S["x1T"], [0, 1024, 2048, 3072], F_G1, ins["ffn1_wg"], ins["ffn1_wu"], ins["ffn1_wd"], "f1")
        if upto >= 3:
            proj_phase(nc, ps, K, S["x1T"], ins["w_in"], ins["pos"], S)
        if upto >= 4:
            dil_phase(nc, ps, K, S)
        if upto >= 5:
            diff_phase(nc, ps, K, S)
        if upto >= 6:
            out_phase(nc, ps, K, S, ins["w_a"], ins["w_b"], ins["w_o"])
        if upto >= 7:
            ffn_phase(nc, ps, K, S["x2T"], outT, [0, 1024], F_G2, ins["ffn2_wg"], ins["ffn2_wu"], ins["ffn2_wd"], "f2")
    return nc, list(ins.keys())


def host_consts():
    cb = np.zeros((128, NCB), np.float32)
    cb[:, C_INV2048:C_INV2048 + 128] = 1.0 / 2048
    cb[:, C_INV128:C_INV128 + 128] = 1.0 / 128
    cb[:, C_INV256:C_INV256 + 128] = 1.0 / 256
    cb[:, C_ONES:C_ONES + 128] = 1.0
    for m in range(16):
        cb[m + 16, C_PSW + m] = 1.0
        cb[m, C_PSW + m + 16] = 1.0
    ik = np.arange(128)[:, None]
    iq = np.arange(128)[None, :]
    cb[:, C_MDIL:C_MDIL + 128] = (ik >= iq)
    cb[:, C_MDIL + 128:C_MDIL + 256] = (ik <= iq)
    tri = (ik <= iq).astype(np.float32)
    m1 = np.concatenate([tri, np.ones((128, 128), np.float32)], 1)
    m2 = np.concatenate([np.zeros((128, 128), np.float32), tri], 1)
    cb[:, C_M1:C_M1 + 512] = np.concatenate([m1, m1], 1)
    cb[:, C_M2:C_M2 + 512] = np.concatenate([m2, m2], 1)
    return cb


def make_in_maps(x, positions, ffn1_norm, ffn1_w_gate, ffn1_w_up, ffn1_w_down, mix_norm, w_in,
                 dil_q_norm, dil_k_norm, diff_q_norm, diff_k_norm, diff_lq1, diff_lk1, diff_lq2, diff_lk2,
                 diff_subln, w_dil_branch, w_diff_branch, w_out, ffn2_norm, ffn2_w_gate, ffn2_w_up, ffn2_w_down):
    f = lambda a: np.ascontiguousarray(np.asarray(a, dtype=np.float32))
    x = np.asarray(x, np.float32)
    positions = np.asarray(positions, np.int32)
    cb = host_consts()
    cf = np.zeros((128, NCF), np.float32)
    cf[:, F_G1:F_G1 + 16] = f(ffn1_norm)[0].reshape(16, 128).T
    cf[:, F_GM:F_GM + 16] = f(mix_norm)[0].reshape(16, 128).T
    cf[:, F_G2:F_G2 + 16] = f(ffn2_norm)[0].reshape(16, 128).T
    cf[:, F_HG + 0] = f(dil_q_norm)[0]
    cf[:, F_HG + 1] = f(dil_k_norm)[0]
    cf[:, F_HG + 2] = f(diff_q_norm)[0]
    cf[:, F_HG + 3] = f(diff_k_norm)[0]
    cf[:, F_SUB:F_SUB + 2] = f(diff_subln)[0].reshape(2, 128).T
    inv = (np.float32(500000.0) ** (-np.arange(0, 32, 2, dtype=np.float32) / np.float32(32))).astype(np.float32)
    cf[0:16, F_INV] = inv
    cf[16:32, F_INV] = inv
    cf[0:16, F_SGN] = -1.0
    cf[16:32, F_SGN] = 1.0
    cf[:, F_EPS] = EPS
    lamv = np.stack([f(diff_lq1)[0], f(diff_lk1)[0], f(diff_lq2)[0], f(diff_lk2)[0]], 0)
    lamv = np.ascontiguousarray(np.broadcast_to(lamv[None], (128, 4, 128)))
    shared = {
        "cbf": cb, "cf32": cf, "lamv": lamv,
        "ffn1_wg": f(ffn1_w_gate)[0], "ffn1_wu": f(ffn1_w_up)[0], "ffn1_wd": f(ffn1_w_down)[0],
        "ffn2_wg": f(ffn2_w_gate)[0], "ffn2_wu": f(ffn2_w_up)[0], "ffn2_wd": f(ffn2_w_down)[0],
        "w_in": f(w_in)[0], "w_a": f(w_dil_branch)[0], "w_b": f(w_diff_branch)[0], "w_o": f(w_out)[0],
    }
    maps = []
    for core in range(8):
        b, half = core // 2, core % 2
        xT = np.zeros((D, TFR), np.float32)
        pos = np.zeros((TFR,), np.int32)
        xT[:, 2048:] = x[b, half * 2048:(half + 1) * 2048, :].T
        pos[2048:] = positions[b, half * 2048:(half + 1) * 2048]
        if half == 1:
            xT[:, :2048] = x[b, 0:2048, :].T
            pos[:2048] = positions[b, 0:2048]
        m = dict(shared)
        m["xT"] = np.ascontiguousarray(xT.reshape(16, 128, TFR))
        m["pos"] = np.ascontiguousarray(np.broadcast_to(pos[None], (32, TFR)))
        m["flag"] = np.full((128, 128), float(half), np.float32)
        maps.append(m)
    return maps


def kernel(**inputs):
    maps = make_in_maps(**inputs)
    nc, names = build()
    maps = [{k: m[k] for k in names} for m in maps]
    res = run_bass_kernel_spmd(nc, maps, core_ids=list(range(8)))
    out = np.empty((4, 4096, D), np.float32)
    for core in range(8):
        b, half = core // 2, core % 2
        oT = np.asarray(res.results[core]["outT"], np.float32).reshape(D, TOWN)
        out[b, half * 2048:(half + 1) * 2048, :] = oT.T
    return out
```

```python
import math
from contextlib import ExitStack

import numpy as np
import concourse.bass as bass
import concourse.mybir as mybir
from concourse.bass_utils import run_bass_kernel_spmd

F32 = mybir.dt.float32
BF16 = mybir.dt.bfloat16
I32 = mybir.dt.int32
AF = mybir.ActivationFunctionType
ALU = mybir.AluOpType
AX = mybir.AxisListType

D = 2048
DFF = 5632
DIN = 11776
TOWN = 2048
TFR = 4096
EPS = 1e-6
SCALE = 128.0 ** -0.5
PI = math.pi
ENGS = ("pe", "act", "dve", "pool", "sp")

C_INV2048, C_INV128, C_INV256, C_ONES, C_PSW, C_MDIL, C_M1, C_M2 = 0, 128, 256, 384, 512, 640, 896, 1408
NCB = 1920
F_G1, F_GM, F_G2, F_HG, F_SUB, F_INV, F_SGN, F_EPS = 0, 16, 32, 48, 52, 54, 55, 56
NCF = 57


class Op:
    __slots__ = ("eng", "fn", "deps", "kind", "ms", "val", "sem")

    def __init__(self, eng, fn, kind):
        self.eng, self.fn, self.kind = eng, fn, kind
        self.deps = []
        self.ms = False
        self.val = 0
        self.sem = None


class Buf:
    def __init__(self, name):
        self.name = name
        self.writes = {}
        self.reads = {}
        self.dsem = None
        self.dcnt = 0


class Prog:
    def __init__(self, nc, es, tag):
        self.nc, self.es, self.tag = nc, es, tag
        self.ops = {e: [] for e in ENGS}
        self.allsems = []
        self.csem = {e: self._sem(f"{tag}_{e}") for e in ENGS[:4]}
        self.dma_ops = []
        self.nb = 0

    def _sem(self, name):
        h = self.nc.alloc_semaphore(name=name)
        self.allsems.append(h)
        return h

    def sb(self, name, shape, dt):
        t = self.es.enter_context(self.nc.sbuf_tensor(f"{self.tag}_{name}", shape, dt))
        return t, Buf(name)

    def buf(self, name="b"):
        return Buf(name)

    @staticmethod
    def _key(op):
        return op.eng if op.kind == "c" else ("d", id(op.sem))

    def rec(self, eng, fn, reads=(), writes=(), kind="c", extra=()):
        op = Op(eng, fn, kind)
        deps = {}
        for b in reads:
            for k, d in b.writes.items():
                deps[id(d)] = d
        for b in writes:
            if b.reads:
                for k, d in b.reads.items():
                    deps[id(d)] = d
            else:
                for k, d in b.writes.items():
                    if not (d.kind == "c" and d.eng == eng):
                        deps[id(d)] = d
        for d in extra:
            deps[id(d)] = d
        if eng == "pe":
            deps = {i: d for i, d in deps.items() if not (d.kind == "c" and d.eng == "pe")}
        op.deps = list(deps.values())
        for d in op.deps:
            if d.kind == "c":
                d.ms = True
        self.ops[eng].append(op)
        return op

    def _commit(self, op, reads, writes):
        k = self._key(op)
        for b in reads:
            b.reads[k] = op
        for b in writes:
            if b.reads:
                b.reads = {}
                b.writes = {}
            b.writes[k] = op

    def c(self, eng, fn, reads=(), writes=()):
        op = self.rec(eng, fn, reads, writes)
        self._commit(op, reads, writes)
        return op

    def dma(self, q, out, in_, sbuf, load, reads=(), writes=()):
        if sbuf.dsem is None:
            self.nb += 1
            sbuf.dsem = self._sem(f"{self.tag}_d{self.nb}")
        r = list(reads) + ([] if load else [sbuf])
        w = list(writes) + ([sbuf] if load else [])
        op = self.rec(q, lambda e: e.dma_start(out=out, in_=in_), r, w, kind="d")
        sbuf.dcnt += 16
        op.sem = sbuf.dsem
        op.val = sbuf.dcnt
        self._commit(op, r, w)
        self.dma_ops.append(op)
        return op

    def run(self):
        for q in ("sp", "pool"):
            op = Op(q, None, "c")
            last = {}
            for d in self.dma_ops:
                last[id(d.sem)] = d
            op.deps = list(last.values())
            self.ops[q].append(op)
        for e in ENGS:
            cnt = 0
            for op in self.ops[e]:
                if op.kind == "c" and op.ms:
                    cnt += 1
                    op.val = cnt
        csem = self.csem

        def mk(e):
            ops = self.ops[e]

            def body(eng):
                waited = {}
                for op in ops:
                    for d in op.deps:
                        sem = d.sem if d.kind == "d" else csem[d.eng]
                        if waited.get(id(sem), 0) < d.val:
                            eng.wait_ge(sem, d.val)
                            waited[id(sem)] = d.val
                    if op.fn is not None:
                        ins = op.fn(eng)
                        if op.kind == "d":
                            ins.then_inc(op.sem, 16)
                        elif op.ms:
                            ins.then_inc(csem[e], 1)
            return body

        with self.nc.Block() as block:
            block.tensor(mk("pe"))
            block.scalar(mk("act"))
            block.vector(mk("dve"))
            block.gpsimd(mk("pool"))
            block.sync(mk("sp"))
        self.nc.clear_and_free_semaphores(self.allsems)
        self.nc.all_engine_barrier()


def reset_bufs(ps):
    for _, b in ps:
        b.reads, b.writes = {}, {}


class RR:
    def __init__(self, items):
        self.items, self.i = items, 0

    def next(self):
        x = self.items[self.i % len(self.items)]
        self.i += 1
        return x


def setup_phase(nc, K, ins):
    with ExitStack() as es:
        P = Prog(nc, es, "su")
        bcb, bcf, bfl, blam = Buf("cb"), Buf("cf"), Buf("fl"), Buf("lam")
        P.dma("pool", K["cb"][:], ins["cbf"], bcb, True)
        P.dma("pool", K["flag"][:], ins["flag"], bfl, True)
        P.dma("sp", K["cf"][:], ins["cf32"], bcf, True)
        lamt, blt = P.sb("lamt", [128, 4, 128], F32)
        pr, bpr = P.sb("pr", [128, 2, 128], F32)
        ss, bss = P.sb("ss", [128, 2], F32)
        ee, bee = P.sb("ee", [128, 2], F32)
        P.dma("sp", lamt[:], ins["lamv"], blt, True)
        P.c("dve", lambda e: e.tensor_tensor(out=pr[:, 0, :], in0=lamt[:, 0, :], in1=lamt[:, 1, :], op=ALU.mult), [blt], [bpr])
        P.c("dve", lambda e: e.tensor_tensor(out=pr[:, 1, :], in0=lamt[:, 2, :], in1=lamt[:, 3, :], op=ALU.mult), [blt], [bpr])
        P.c("dve", lambda e: e.reduce_sum(out=ss[:, 0:1], in_=pr[:, 0, :], axis=AX.X), [bpr], [bss])
        P.c("dve", lambda e: e.reduce_sum(out=ss[:, 1:2], in_=pr[:, 1, :], axis=AX.X), [bpr], [bss])
        P.c("act", lambda e: e.activation(out=ee[:], in_=ss[:], func=AF.Exp), [bss], [bee])
        P.c("dve", lambda e: e.tensor_tensor(out=K["nlam"][:], in0=ee[:, 1:2], in1=ee[:, 0:1], op=ALU.subtract), [bee], [blam])
        P.c("dve", lambda e: e.tensor_scalar_add(out=K["nlam"][:], in0=K["nlam"][:], scalar1=-0.2), [blam], [blam])
        P.c("dve", lambda e: e.tensor_scalar_mul(out=K["gs"][:], in0=K["cf"][:, F_SUB:F_SUB + 2], scalar1=0.8), [bcf], [blam])
        P.run()


def norm_prologue(P, ps, K, src, t0, gcol, hT, bhT, xin, sq, rstd, brstd):
    cb = K["cb"]
    for half in range(2):
        for c in range(16):
            xt, bx = xin.next()
            st, bs = sq.next()
            P.dma("sp", xt[:], src[c, :, t0 + half * 512:t0 + half * 512 + 512], bx, True)
            P.c("act", lambda e, st=st, xt=xt: e.activation(out=st[:], in_=xt[:], func=AF.Square), [bx], [bs])
            P.c("pe", lambda e, st=st, c=c, half=half: e.matmul(ps[6 + half][0][:], cb[:, C_INV2048:C_INV2048 + 128], st[:],
                                                                start=(c == 0), stop=(c == 15)), [bs], [ps[6 + half][1]])
        P.c("act", lambda e, half=half: e.activation(out=rstd[:, half * 512:half * 512 + 512], in_=ps[6 + half][0][:], func=AF.Sqrt,
                                                     bias=K["cf"][:, F_EPS:F_EPS + 1], scale=1.0), [ps[6 + half][1]], [brstd])
        P.c("dve", lambda e, half=half: e.reciprocal(out=rstd[:, half * 512:half * 512 + 512], in_=rstd[:, half * 512:half * 512 + 512]),
            [brstd], [brstd])
    for half in range(2):
        for c in range(16):
            xt, bx = xin.next()
            P.dma("sp", xt[:], src[c, :, t0 + half * 512:t0 + half * 512 + 512], bx, True)
            P.c("dve", lambda e, xt=xt, c=c, half=half: e.scalar_tensor_tensor(
                out=hT[:, c, half * 512:half * 512 + 512], in0=xt[:], scalar=K["cf"][:, gcol + c:gcol + c + 1],
                in1=rstd[:, half * 512:half * 512 + 512], op0=ALU.mult, op1=ALU.mult), [bx, brstd], [bhT])


def ffn_phase(nc, ps, K, src, dst, t0s, gcol, Wg, Wu, Wd, tag):
    with ExitStack() as es:
        reset_bufs(ps)
        P = Prog(nc, es, tag)
        hT, bhT = P.sb("hT", [128, 16, 1024], BF16)
        aT, baT = P.sb("aT", [128, 44, 1024], BF16)
        slots = RR([P.sb(f"w{i}", [128, 8192], BF16) for i in range(2)])
        xin = RR([P.sb(f"xin{i}", [128, 512], F32) for i in range(3)])
        sq = RR([P.sb(f"sq{i}", [128, 512], BF16) for i in range(2)])
        sil = RR([P.sb(f"sil{i}", [128, 512], F32) for i in range(2)])
        xout = RR([P.sb(f"xo{i}", [128, 512], F32) for i in range(2)])
        rstd, brstd = P.sb("rstd", [128, 1024], F32)
        Wgv = Wg.rearrange("(k p) n -> p k n", p=128)
        Wuv = Wu.rearrange("(k p) n -> p k n", p=128)
        Wdv = Wd.rearrange("(f p) n -> p f n", p=128)
        for t0 in t0s:
            norm_prologue(P, ps, K, src, t0, gcol, hT, bhT, xin, sq, rstd, brstd)
            u = 0
            for p in range(22):
                sl, bsl = slots.next()
                gv = sl[:, 0:4096].rearrange("p (k n) -> p k n", k=16)
                uv = sl[:, 4096:8192].rearrange("p (k n) -> p k n", k=16)
                P.dma("pool", gv, Wgv[:, :, p * 256:p * 256 + 256], bsl, True)
                P.dma("pool", uv, Wuv[:, :, p * 256:p * 256 + 256], bsl, True)
                for j in range(2):
                    for half in range(2):
                        bg, bu = ps[(u % 2) * 2], ps[(u % 2) * 2 + 1]
                        u += 1

                        def mm(e, gv=gv, uv=uv, j=j, half=half, bg=bg, bu=bu):
                            for k in range(16):
                                e.matmul(bg[0][:], gv[:, k, j * 128:j * 128 + 128], hT[:, k, half * 512:half * 512 + 512],
                                         start=(k == 0), stop=(k == 15))
                            for k in range(16):
                                ins = e.matmul(bu[0][:], uv[:, k, j * 128:j * 128 + 128], hT[:, k, half * 512:half * 512 + 512],
                                               start=(k == 0), stop=(k == 15))
                            return ins
                        P.c("pe", mm, [bsl, bhT], [bg[1], bu[1]])
                        st, bst = sil.next()
                        f = p * 2 + j
                        P.c("act", lambda e, st=st, bg=bg: e.activation(out=st[:], in_=bg[0][:], func=AF.Silu), [bg[1]], [bst])
                        P.c("dve", lambda e, st=st, bu=bu, f=f, half=half: e.tensor_tensor(
                            out=aT[:, f, half * 512:half * 512 + 512], in0=st[:], in1=bu[0][:], op=ALU.mult), [bst, bu[1]], [baT])
            for dp in range(8):
                base = 4 if dp % 2 == 0 else 0
                for fh in range(2):
                    sl, bsl = slots.next()
                    dv = sl[:, 0:5632].rearrange("p (f n) -> p f n", f=22)
                    P.dma("pool", dv, Wdv[:, fh * 22:fh * 22 + 22, dp * 256:dp * 256 + 256], bsl, True)
                    for j in range(2):
                        for half in range(2):
                            bk = ps[base + j * 2 + half]

                            def mm(e, dv=dv, j=j, half=half, bk=bk, fh=fh):
                                for f in range(22):
                                    ins = e.matmul(bk[0][:], dv[:, f, j * 128:j * 128 + 128],
                                                   aT[:, fh * 22 + f, half * 512:half * 512 + 512],
                                                   start=(fh == 0 and f == 0), stop=(fh == 1 and f == 21))
                                return ins
                            P.c("pe", mm, [bsl, baT], [bk[1]])
                for j in range(2):
                    for half in range(2):
                        bk = ps[base + j * 2 + half]
                        c = dp * 2 + j
                        xt, bx = xin.next()
                        xo, bxo = xout.next()
                        tt = t0 + half * 512
                        P.dma("sp", xt[:], src[c, :, tt:tt + 512], bx, True)
                        P.c("dve", lambda e, xo=xo, xt=xt, bk=bk: e.scalar_tensor_tensor(
                            out=xo[:], in0=bk[0][:], scalar=0.5, in1=xt[:], op0=ALU.mult, op1=ALU.add), [bk[1], bx], [bxo])
                        P.dma("sp", dst[c, :, tt:tt + 512], xo[:], bxo, False)
        P.run()


def proj_phase(nc, ps, K, x1T, win, posd, S, tiles=(0, 1, 2, 3), pfilter=None):
    with ExitStack() as es:
        reset_bufs(ps)
        P = Prog(nc, es, "pj")
        cb, cf = K["cb"], K["cf"]
        hT, bhT = P.sb("hT", [128, 16, 1024], BF16)
        slots = RR([P.sb(f"w{i}", [128, 8192], BF16) for i in range(3)])
        xin = RR([P.sb(f"xin{i}", [128, 512], F32) for i in range(3)])
        sq = RR([P.sb(f"sq{i}", [128, 512], BF16) for i in range(2)])
        qg = RR([P.sb(f"qg{i}", [128, 512], BF16) for i in range(2)])
        rs = RR([P.sb(f"rs{i}", [128, 512], F32) for i in range(2)])
        qo = RR([P.sb(f"qo{i}", [128, 512], BF16) for i in range(3)])
        t1 = RR([P.sb(f"t1{i}", [32, 512], F32) for i in range(2)])
        t2 = RR([P.sb(f"t2{i}", [32, 512], F32) for i in range(2)])
        vst = RR([P.sb(f"vs{i}", [128, 512], BF16) for i in range(3)])
        gst = RR([P.sb(f"gs{i}", [128, 512], F32) for i in range(3)])
        rstd, brstd = P.sb("rstd", [128, 1024], F32)
        posi, bposi = P.sb("posi", [32, 1024], I32)
        ang, bang = P.sb("ang", [32, 1024], F32)
        a1, ba1 = P.sb("a1", [32, 1024], F32)
        ki, bki = P.sb("ki", [32, 1024], I32)
        kf, bkf = P.sb("kf", [32, 1024], F32)
        Ct, bC = P.sb("C", [32, 1024], F32)
        St, bS = P.sb("S", [32, 1024], F32)
        npi, bnpi = P.sb("npi", [32, 1], F32)
        P.c("dve", lambda e: e.memset(npi[:], -PI), [], [bnpi])
        Wv = win.rearrange("(k p) n -> p k n", p=128)
        panels = []
        for i in range(3):
            panels.append((i * 512, "qk", ("Qd", i * 4, 0)))
        for i in range(3):
            panels.append((1536 + i * 512, "qk", ("Kd", i * 4, 1)))
        for i in range(3):
            panels.append((3072 + i * 512, "v", ("Vd", i * 512)))
        for i in range(2):
            panels.append((4608 + i * 512, "qk", ("Qf", i * 4, 2)))
        for i in range(2):
            panels.append((5632 + i * 512, "qk", ("Kf", i * 4, 3)))
        for i in range(2):
            panels.append((6656 + i * 512, "v", ("Vf", i * 512)))
        for i in range(8):
            panels.append((7680 + i * 512, "g", (i * 4,)))
        u = 0
        if pfilter is not None:
            panels = [p for i, p in enumerate(panels) if i in pfilter]
        for ti in tiles:
            t0 = ti * 1024
            own = ti >= 2
            norm_prologue(P, ps, K, x1T, t0, F_GM, hT, bhT, xin, sq, rstd, brstd)
            P.dma("sp", posi[:], posd[:, t0:t0 + 1024], bposi, True)
            P.c("dve", lambda e: e.tensor_copy(out=ang[:], in_=posi[:]), [bposi], [bang])
            P.c("dve", lambda e: e.tensor_scalar_mul(out=ang[:], in0=ang[:], scalar1=cf[0:32, F_INV:F_INV + 1]), [bang], [bang])
            for (off, Tt, bT) in ((0.75, Ct, bC), (0.5, St, bS)):
                P.c("dve", lambda e, off=off: e.tensor_scalar(out=a1[:], in0=ang[:], scalar1=1.0 / (2 * PI), scalar2=off, op0=ALU.mult,
                                                              op1=ALU.add), [bang], [ba1])
                P.c("dve", lambda e: e.tensor_copy(out=ki[:], in_=a1[:]), [ba1], [bki])
                P.c("dve", lambda e: e.tensor_copy(out=kf[:], in_=ki[:]), [bki], [bkf])
                P.c("dve", lambda e: e.tensor_tensor(out=a1[:], in0=a1[:], in1=kf[:], op=ALU.subtract), [ba1, bkf], [ba1])
                P.c("dve", lambda e: e.scalar_tensor_tensor(out=a1[:], in0=a1[:], scalar=0.0, in1=a1[:], op0=ALU.is_lt, op1=ALU.add),
                    [ba1], [ba1])
                P.c("act", lambda e, Tt=Tt: e.activation(out=Tt[:], in_=a1[:], func=AF.Sin, bias=npi[:], scale=2 * PI), [ba1, bnpi], [bT])
            P.c("dve", lambda e: e.tensor_scalar_mul(out=St[:], in0=St[:], scalar1=cf[0:32, F_SGN:F_SGN + 1]), [bS], [bS])
            tile_units = []
            for (c0, kind, info) in panels:
                if not own and (kind == "g" or info[0] in ("Qd", "Qf")):
                    continue
                if kind == "v":
                    for tb in range(8):
                        tile_units.append((c0, kind, info, tb, 0, tb == 0))
                else:
                    for j in range(4):
                        for half in range(2):
                            tile_units.append((c0, kind, info, j, half, j == 0 and half == 0))
            stB = {}

            def stageA(idx, t0=t0):
                nonlocal u
                c0, kind, info, j, half, firstu = tile_units[idx]
                if firstu:
                    sl, bsl = slots.next()
                    wv = sl[:, :].rearrange("p (k n) -> p k n", k=16)
                    P.dma("pool", wv, Wv[:, :, c0:c0 + 512], bsl, True)
                    stageA.cur = (wv, bsl)
                wv, bsl = stageA.cur
                bk = ps[u % 4]
                u += 1
                if kind == "v":
                    tb = j

                    def mm(e):
                        for k in range(16):
                            ins = e.matmul(bk[0][:], hT[:, k, tb * 128:tb * 128 + 128], wv[:, k, :], start=(k == 0), stop=(k == 15))
                        return ins
                    P.c("pe", mm, [bsl, bhT], [bk[1]])
                    vt, bvt = vst.next()
                    P.c("act", lambda e: e.activation(out=vt[:], in_=bk[0][:], func=AF.Copy), [bk[1]], [bvt])
                    P.dma("sp", S[info[0]][t0 + tb * 128:t0 + tb * 128 + 128, info[1]:info[1] + 512], vt[:], bvt, False)
                    return
                hs = slice(half * 512, half * 512 + 512)

                def mm(e):
                    for k in range(16):
                        ins = e.matmul(bk[0][:], wv[:, k, j * 128:j * 128 + 128], hT[:, k, hs], start=(k == 0), stop=(k == 15))
                    return ins
                P.c("pe", mm, [bsl, bhT], [bk[1]])
                if kind == "g":
                    gt, bgt = gst.next()
                    P.c("act", lambda e: e.activation(out=gt[:], in_=bk[0][:], func=AF.Sigmoid), [bk[1]], [bgt])
                    tt = t0 - 2048 + half * 512
                    P.dma("sp", S["SG"][info[0] + j, :, tt:tt + 512], gt[:], bgt, False)
                    return
                name, ch0, gi = info
                st, bs = sq.next()
                qt, bq = qg.next()
                P.c("act", lambda e: e.activation(out=st[:], in_=bk[0][:], func=AF.Square), [bk[1]], [bs])
                P.c("act", lambda e: e.activation(out=qt[:], in_=bk[0][:], func=AF.Copy, scale=cf[:, F_HG + gi:F_HG + gi + 1]),
                    [bk[1]], [bq])
                stB[idx] = (st, bs, qt, bq, u)

            def stageB(idx, t0=t0):
                if idx not in stB:
                    return
                c0, kind, info, j, half, firstu = tile_units[idx]
                st, bs, qt, bq, uu = stB.pop(idx)
                name, ch0, gi = info
                hs = slice(half * 512, half * 512 + 512)
                rt, brt = rs.next()
                ot, bo = qo.next()
                x1, bx1 = t1.next()
                x2, bx2 = t2.next()
                bss, bsw = ps[4 + (uu % 2)], ps[6 + (uu % 2)]
                P.c("pe", lambda e: e.matmul(bss[0][:], cb[:, C_INV128:C_INV128 + 128], st[:], start=True, stop=True), [bs], [bss[1]])
                P.c("pe", lambda e: e.matmul(bsw[0][:], cb[:, C_PSW:C_PSW + 128], qt[:], start=True, stop=True), [bq], [bsw[1]])
                P.c("act", lambda e: e.activation(out=rt[:], in_=bss[0][:], func=AF.Sqrt, bias=cf[:, F_EPS:F_EPS + 1], scale=1.0),
                    [bss[1]], [brt])
                P.c("pool", lambda e: e.tensor_tensor(out=x1[:], in0=qt[0:32, :], in1=Ct[:, hs], op=ALU.mult), [bq, bC], [bx1])
                P.c("dve", lambda e: e.reciprocal(out=rt[:], in_=rt[:]), [brt], [brt])
                P.c("pool", lambda e: e.tensor_tensor(out=ot[:], in0=qt[:], in1=rt[:], op=ALU.mult), [bq, brt], [bo])
                P.c("dve", lambda e: e.tensor_tensor(out=x2[:], in0=bsw[0][0:32, :], in1=St[:, hs], op=ALU.mult), [bsw[1], bS], [bx2])
                P.c("dve", lambda e: e.tensor_tensor(out=x1[:], in0=x1[:], in1=x2[:], op=ALU.add), [bx1, bx2], [bx1])
                P.c("dve", lambda e: e.tensor_tensor(out=ot[0:32, :], in0=x1[:], in1=rt[0:32, :], op=ALU.mult), [bx1, brt, bo], [bo])
                tt = (t0 - 2048 + half * 512) if name[0] == "Q" else (t0 + half * 512)
                P.dma("sp", S[name][ch0 + j, :, tt:tt + 512], ot[:], bo, False)

            nu = len(tile_units)
            stageA(0)
            for idx in range(nu):
                if idx + 1 < nu:
                    stageA(idx + 1)
                stageB(idx)
        P.run()


def dil_phase(nc, ps, K, S, LOOK=2):
    with ExitStack() as es:
        reset_bufs(ps)
        P = Prog(nc, es, "dl")
        cb = K["cb"]
        Vg, bVg = P.sb("Vg", [128, 32, 512], BF16)
        KT = RR([P.sb(f"KT{i}", [128, 4096], BF16) for i in range(2)])
        QT = RR([P.sb(f"QT{i}", [128, 2048], BF16) for i in range(2)])
        Ua = [P.sb(f"Ua{h}", [128, 2048], F32) for h in range(4)]
        Da = [P.sb(f"Da{h}", [128, 2048], F32) for h in range(4)]
        Pt = RR([P.sb(f"Pt{i}", [128, 256], BF16) for i in range(4)])
        Pm = RR([P.sb(f"Pm{i}", [128, 256], BF16) for i in range(4)])
        od = RR([P.sb(f"od{i}", [128, 2048], BF16) for i in range(2)])
        ones = cb[:, C_ONES:C_ONES + 128]
        flag = K["flag"][:, :]
        units = []
        for g, r in enumerate((1, 4, 16)):
            nmb = 32 // r
            for h in range(4):
                for c in range(r):
                    for mb in range(nmb // 2, nmb):
                        units.append((g, r, h, c, mb))
        st = {}
        cur = {"gh": None, "g": None}

        def stage1(u):
            g, r, h, c, mb = units[u]
            if cur["gh"] != (g, h):
                cur["gh"] = (g, h)
                kt, bkt = KT.next()
                qt, bqt = QT.next()
                P.dma("sp", kt[:], S["Kd"][g * 4 + h], bkt, True)
                P.dma("sp", qt[:], S["Qd"][g * 4 + h], bqt, True)
                cur["kq"] = (kt, bkt, qt, bqt)
            kt, bkt, qt, bqt = cur["kq"]
            bS_ = ps[u % 4]
            q0 = mb * 128 * r + c - 2048
            qsl = slice(q0, q0 + 127 * r + 1, r)
            kp = slice((mb - 1) * 128 * r + c, (mb - 1) * 128 * r + c + 127 * r + 1, r)
            kc = slice(mb * 128 * r + c, mb * 128 * r + c + 127 * r + 1, r)

            def mm1(e):
                e.matmul(bS_[0][:, 0:128], kt[:, kp], qt[:, qsl], start=True, stop=True)
                return e.matmul(bS_[0][:, 128:256], kt[:, kc], qt[:, qsl], start=True, stop=True)
            P.c("pe", mm1, [bkt, bqt], [bS_[1]])
            pt, bpt = Pt.next()
            pm, bpm = Pm.next()
            P.c("act", lambda e: e.activation(out=pt[:], in_=bS_[0][:, 0:256], func=AF.Exp, scale=SCALE), [bS_[1]], [bpt])
            P.c("pool", lambda e: e.tensor_tensor(out=pm[:], in0=pt[:], in1=cb[:, C_MDIL:C_MDIL + 256], op=ALU.mult), [bpt], [bpm])
            st[u] = (pm, bpm, qsl)

        def stage2(u):
            g, r, h, c, mb = units[u]
            nmb = 32 // r
            if cur["g"] != g:
                cur["g"] = g
                for m2 in range(nmb):
                    src = S["Vd"][m2 * 128 * r:(m2 + 1) * 128 * r, g * 512:(g + 1) * 512].rearrange("(p r) n -> p r n", r=r)
                    P.dma("sp", Vg[:, m2 * r:(m2 + 1) * r, :], src, bVg, True)
            pm, bpm, qsl = st.pop(u)
            bU_ = ps[4 + u % 4]
            prev_pref = (mb - 1) < nmb // 2
            hc = slice(h * 128, h * 128 + 128)

            def mm2(e):
                e.matmul(bU_[0][:, 0:128], Vg[:, (mb - 1) * r + c, hc], pm[:, 0:128], start=True, stop=False)
                e.matmul(bU_[0][:, 0:128], Vg[:, mb * r + c, hc], pm[:, 128:256], start=False, stop=True)
                e.matmul(bU_[0][:, 128:256], flag if prev_pref else ones, pm[:, 0:128], start=True, stop=False)
                return e.matmul(bU_[0][:, 128:256], ones, pm[:, 128:256], start=False, stop=True)
            P.c("pe", mm2, [bpm, bVg], [bU_[1]])
            ua, bua = Ua[h]
            da, bda = Da[h]
            if g == 0:
                P.c("dve", lambda e: e.tensor_copy(out=ua[:, qsl], in_=bU_[0][:, 0:128]), [bU_[1]], [bua])
                P.c("dve", lambda e: e.tensor_copy(out=da[:, qsl], in_=bU_[0][:, 128:256]), [bU_[1]], [bda])
            else:
                P.c("dve", lambda e: e.tensor_tensor(out=ua[:, qsl], in0=ua[:, qsl], in1=bU_[0][:, 0:128], op=ALU.add), [bU_[1], bua], [bua])
                P.c("dve", lambda e: e.tensor_tensor(out=da[:, qsl], in0=da[:, qsl], in1=bU_[0][:, 128:256], op=ALU.add), [bU_[1], bda], [bda])

        n = len(units)
        for u in range(min(LOOK, n)):
            stage1(u)
        for u in range(n):
            if u + LOOK < n:
                stage1(u + LOOK)
            stage2(u)
        for h in range(4):
            ua, bua = Ua[h]
            da, bda = Da[h]
            ot, bo = od.next()
            P.c("dve", lambda e, da=da: e.reciprocal(out=da[:], in_=da[:]), [bda], [bda])
            P.c("dve", lambda e, ot=ot, ua=ua, da=da: e.tensor_tensor(out=ot[:], in0=ua[:], in1=da[:], op=ALU.mult), [bua, bda], [bo])
            P.dma("sp", S["Od"][h], ot[:], bo, False)
        P.run()


def diff_phase(nc, ps, K, S, LOOK=2):
    with ExitStack() as es:
        reset_bufs(ps)
        P = Prog(nc, es, "df")
        cb = K["cb"]
        KT = RR([P.sb(f"KT{i}", [128, 2, 4096], BF16) for i in range(2)])
        QT = RR([P.sb(f"QT{i}", [128, 2, 2048], BF16) for i in range(2)])
        Vh = RR([P.sb(f"Vh{i}", [128, 32, 256], BF16) for i in range(2)])
        Pt = RR([P.sb(f"Pt{i}", [128, 512], BF16) for i in range(4)])
        rd, brd = P.sb("rd", [128, 512], F32)
        o1, bo1 = P.sb("o1", [128, 512], F32)
        o2, bo2 = P.sb("o2", [128, 512], F32)
        sqt, bsq = P.sb("sq", [128, 512], BF16)
        rst, brs = P.sb("rs", [128, 256], F32)
        of = RR([P.sb(f"of{i}", [128, 512], BF16) for i in range(2)])
        ones = cb[:, C_ONES:C_ONES + 128]
        flag = K["flag"][:, :]
        nlam = K["nlam"]
        gs = K["gs"]
        units = []
        for h in range(4):
            for qi in range(8):
                nkb = 18 + 2 * qi
                for kb in range(nkb):
                    units.append((h, qi, kb, nkb))
        st = {}
        cur = {"h": None}
        bU = (ps[2], ps[3])
        bD = ps[4]
        bN = ps[5]

        def stage1(u):
            h, qi, kb, nkb = units[u]
            if cur["h"] != h:
                cur["h"] = h
                kt, bkt = KT.next()
                qt, bqt = QT.next()
                vh, bvh = Vh.next()
                for cpt in range(2):
                    P.dma("sp", kt[:, cpt, :], S["Kf"][h * 2 + cpt], bkt, True)
                    P.dma("sp", qt[:, cpt, :], S["Qf"][h * 2 + cpt], bqt, True)
                P.dma("sp", vh[:], S["Vf"][:, h * 256:(h + 1) * 256].rearrange("(kb p) n -> p kb n", p=128), bvh, True)
                cur["kqv"] = (kt, bkt, qt, bqt, vh, bvh)
            kt, bkt, qt, bqt, vh, bvh = cur["kqv"]
            bS_ = ps[(0, 1, 6)[u % 3]]
            qs = slice(qi * 256, qi * 256 + 256)
            ks = slice(kb * 128, kb * 128 + 128)

            def mm1(e):
                e.matmul(bS_[0][:, 0:256], kt[:, 0, ks], qt[:, 0, qs], start=True, stop=True)
                return e.matmul(bS_[0][:, 256:512], kt[:, 1, ks], qt[:, 1, qs], start=True, stop=True)
            P.c("pe", mm1, [bkt, bqt], [bS_[1]])
            pt, bpt = Pt.next()
            P.c("act", lambda e: e.activation(out=pt[:], in_=bS_[0][:], func=AF.Exp, scale=SCALE), [bS_[1]], [bpt])
            if kb >= nkb - 2:
                mc = C_M1 if kb == nkb - 2 else C_M2
                P.c("pool", lambda e: e.tensor_tensor(out=pt[:], in0=pt[:], in1=cb[:, mc:mc + 512], op=ALU.mult), [bpt], [bpt])
            st[u] = (pt, bpt, vh, bvh)

        def stage2(u):
            h, qi, kb, nkb = units[u]
            pt, bpt, vh, bvh = st.pop(u)
            first, last = kb == 0, kb == nkb - 1

            def mm2(e):
                for cpt in range(2):
                    for dvc in range(2):
                        e.matmul(bU[cpt][0][:, dvc * 256:dvc * 256 + 256], vh[:, kb, dvc * 128:dvc * 128 + 128],
                                 pt[:, cpt * 256:cpt * 256 + 256], start=(first and dvc == 0), stop=last, skip_group_check=True)
                for cpt in range(2):
                    ins = e.matmul(bD[0][:, cpt * 256:cpt * 256 + 256], flag if kb < 16 else ones,
                                   pt[:, cpt * 256:cpt * 256 + 256], start=(first and cpt == 0), stop=last, skip_group_check=True)
                return ins
            P.c("pe", mm2, [bpt, bvh], [bU[0][1], bU[1][1], bD[1]])
            if not last:
                return
            P.c("dve", lambda e: e.reciprocal(out=rd[:], in_=bD[0][:]), [bD[1]], [brd])
            for dvc in range(2):
                ds = slice(dvc * 256, dvc * 256 + 256)
                P.c("dve", lambda e, ds=ds: e.tensor_tensor(out=o1[:, ds], in0=bU[0][0][:, ds], in1=rd[:, 0:256], op=ALU.mult),
                    [bU[0][1], brd], [bo1])
                P.c("dve", lambda e, ds=ds: e.tensor_tensor(out=o2[:, ds], in0=bU[1][0][:, ds], in1=rd[:, 256:512], op=ALU.mult),
                    [bU[1][1], brd], [bo2])
            P.c("dve", lambda e: e.scalar_tensor_tensor(out=o1[:], in0=o2[:], scalar=nlam[:, 0:1], in1=o1[:], op0=ALU.mult, op1=ALU.add),
                [bo1, bo2], [bo1])
            P.c("act", lambda e: e.activation(out=sqt[:], in_=o1[:], func=AF.Square), [bo1], [bsq])

            def mm3(e):
                e.matmul(bN[0][:, 0:256], cb[:, C_INV256:C_INV256 + 128], sqt[:, 0:256], start=True, stop=False)
                return e.matmul(bN[0][:, 0:256], cb[:, C_INV256:C_INV256 + 128], sqt[:, 256:512], start=False, stop=True)
            P.c("pe", mm3, [bsq], [bN[1]])
            P.c("act", lambda e: e.activation(out=rst[:], in_=bN[0][:, 0:256], func=AF.Sqrt, bias=K["cf"][:, F_EPS:F_EPS + 1], scale=1.0),
                [bN[1]], [brs])
            P.c("dve", lambda e: e.reciprocal(out=rst[:], in_=rst[:]), [brs], [brs])
            ot, bot = of.next()
            for dvc in range(2):
                ds = slice(dvc * 256, dvc * 256 + 256)
                P.c("dve", lambda e, ds=ds, dvc=dvc: e.scalar_tensor_tensor(out=ot[:, ds], in0=o1[:, ds], scalar=gs[:, dvc:dvc + 1],
                                                                           in1=rst[:], op0=ALU.mult, op1=ALU.mult), [bo1, brs], [bot])
            for dvc in range(2):
                P.dma("sp", S["Of"][h * 2 + dvc, :, qi * 256:qi * 256 + 256], ot[:, dvc * 256:dvc * 256 + 256], bot, False)

        n = len(units)
        for u in range(min(LOOK, n)):
            stage1(u)
        for u in range(n):
            if u + LOOK < n:
                stage1(u + LOOK)
            stage2(u)
        P.run()


def out_phase(nc, ps, K, S, WA, WB, WO):
    with ExitStack() as es:
        reset_bufs(ps)
        P = Prog(nc, es, "op")
        wa, bwa = P.sb("wa", [128, 4, 2048], BF16)
        wb, bwb = P.sb("wb", [128, 8, 2048], BF16)
        wo, bwo = P.sb("wo", [128, 16, 2048], BF16)
        odt = RR([P.sb(f"od{i}", [128, 4, 512], BF16) for i in range(1)])
        oft = RR([P.sb(f"of{i}", [128, 8, 512], BF16) for i in range(1)])
        yT = RR([P.sb(f"y{i}", [128, 16, 512], BF16) for i in range(1)])
        sgd = RR([P.sb(f"sgd{i}", [128, 512], F32) for i in range(2)])
        sgf = RR([P.sb(f"sgf{i}", [128, 512], F32) for i in range(2)])
        ta = RR([P.sb(f"ta{i}", [128, 512], F32) for i in range(2)])
        tb_ = RR([P.sb(f"tb{i}", [128, 512], F32) for i in range(2)])
        xin = RR([P.sb(f"xin{i}", [128, 512], F32) for i in range(2)])
        xout = RR([P.sb(f"xo{i}", [128, 512], F32) for i in range(2)])
        P.dma("pool", wa[:], WA.rearrange("(k p) n -> p k n", p=128), bwa, True)
        for i in range(2):
            P.dma("pool", wb[:, i * 4:i * 4 + 4, :], WB.rearrange("(k p) n -> p k n", p=128)[:, i * 4:i * 4 + 4, :], bwb, True)
        for i in range(4):
            P.dma("pool", wo[:, i * 4:i * 4 + 4, :], WO.rearrange("(k p) n -> p k n", p=128)[:, i * 4:i * 4 + 4, :], bwo, True)
        u = 0
        for ti in range(4):
            ts = slice(ti * 512, ti * 512 + 512)
            od, bod = odt.next()
            of, bof = oft.next()
            y, by = yT.next()
            for k in range(4):
                P.dma("sp", od[:, k, :], S["Od"][k, :, ts], bod, True)
            for k in range(8):
                P.dma("sp", of[:, k, :], S["Of"][k, :, ts], bof, True)
            for dc in range(16):
                bA, bB = ps[(u % 2) * 2], ps[(u % 2) * 2 + 1]
                u += 1
                cs = slice(dc * 128, dc * 128 + 128)

                def mm(e, od=od, of=of, cs=cs, bA=bA, bB=bB):
                    for k in range(4):
                        e.matmul(bA[0][:], wa[:, k, cs], od[:, k, :], start=(k == 0), stop=(k == 3))
                    for k in range(8):
                        ins = e.matmul(bB[0][:], wb[:, k, cs], of[:, k, :], start=(k == 0), stop=(k == 7))
                    return ins
                P.c("pe", mm, [bwa, bwb, bod, bof], [bA[1], bB[1]])
                g1, bg1 = sgd.next()
                g2, bg2 = sgf.next()
                a_, ba_ = ta.next()
                b_, bb_ = tb_.next()
                P.dma("sp", g1[:], S["SG"][dc, :, ts], bg1, True)
                P.dma("sp", g2[:], S["SG"][16 + dc, :, ts], bg2, True)
                P.c("dve", lambda e, a_=a_, g1=g1, bA=bA: e.tensor_tensor(out=a_[:], in0=g1[:], in1=bA[0][:], op=ALU.mult), [bg1, bA[1]], [ba_])
                P.c("dve", lambda e, b_=b_, g2=g2, bB=bB: e.tensor_tensor(out=b_[:], in0=g2[:], in1=bB[0][:], op=ALU.mult), [bg2, bB[1]], [bb_])
                P.c("pool", lambda e, y=y, dc=dc, a_=a_, b_=b_: e.tensor_tensor(out=y[:, dc, :], in0=a_[:], in1=b_[:], op=ALU.add),
                    [ba_, bb_], [by])
            for dc2 in range(16):
                bC_ = ps[4 + dc2 % 4]
                cs = slice(dc2 * 128, dc2 * 128 + 128)

                def mm(e, y=y, cs=cs, bC_=bC_):
                    for k in range(16):
                        ins = e.matmul(bC_[0][:], wo[:, k, cs], y[:, k, :], start=(k == 0), stop=(k == 15))
                    return ins
                P.c("pe", mm, [bwo, by], [bC_[1]])
                xt, bx = xin.next()
                xo, bxo = xout.next()
                P.dma("sp", xt[:], S["x1T"][dc2, :, 2048 + ti * 512:2048 + ti * 512 + 512], bx, True)
                P.c("dve", lambda e, xo=xo, xt=xt, bC_=bC_: e.tensor_tensor(out=xo[:], in0=bC_[0][:], in1=xt[:], op=ALU.add), [bC_[1], bx], [bxo])
                P.dma("sp", S["x2T"][dc2, :, ts], xo[:], bxo, False)
        P.run()


IN_SHAPES = {
    "xT": ([16, 128, TFR], F32), "pos": ([32, TFR], I32), "cbf": ([128, NCB], F32), "cf32": ([128, NCF], F32),
    "flag": ([128, 128], F32), "lamv": ([128, 4, 128], F32),
    "ffn1_wg": ([D, DFF], F32), "ffn1_wu": ([D, DFF], F32), "ffn1_wd": ([DFF, D], F32),
    "ffn2_wg": ([D, DFF], F32), "ffn2_wu": ([D, DFF], F32), "ffn2_wd": ([DFF, D], F32),
    "w_in": ([D, DIN], F32), "w_a": ([512, D], F32), "w_b": ([1024, D], F32), "w_o": ([D, D], F32),
}


class LazyIns(dict):
    def __init__(self, nc):
        super().__init__()
        self.nc = nc

    def __missing__(self, name):
        shape, dt = IN_SHAPES[name]
        ap = self.nc.dram_tensor(name, shape, dt, kind="ExternalInput").ap()
        self[name] = ap
        return ap


def build(debug=False, upto=7, dump=()):
    nc = bass.Bass("TRN2", target_bir_lowering=False)
    ins = LazyIns(nc)
    kind = "ExternalOutput" if debug else "Internal"
    S = {}

    def scr(name, shape, dt):
        S[name] = nc.dram_tensor(name, shape, dt, kind=("ExternalOutput" if name in dump else "Internal")).ap()
    scr("x1T", [16, 128, TFR], F32)
    scr("Qd", [12, 128, TOWN], BF16)
    scr("Kd", [12, 128, TFR], BF16)
    scr("Vd", [TFR, 1536], BF16)
    scr("Qf", [8, 128, TOWN], BF16)
    scr("Kf", [8, 128, TFR], BF16)
    scr("Vf", [TFR, 1024], BF16)
    scr("SG", [32, 128, TOWN], F32)
    scr("Od", [4, 128, TOWN], BF16)
    scr("Of", [8, 128, TOWN], BF16)
    scr("x2T", [16, 128, TOWN], F32)
    outT = nc.dram_tensor("outT", [16, 128, TOWN], F32, kind="ExternalOutput").ap()
    with ExitStack() as es:
        ps = []
        for i in range(8):
            t = es.enter_context(nc.psum_tensor(f"ps{i}", [128, 512], F32))
            ps.append((t, Buf(f"ps{i}")))
        K = {}
        K["cb"] = es.enter_context(nc.sbuf_tensor("K_cb", [128, NCB], BF16))
        K["cf"] = es.enter_context(nc.sbuf_tensor("K_cf", [128, NCF], F32))
        K["flag"] = es.enter_context(nc.sbuf_tensor("K_flag", [128, 128], BF16))
        K["nlam"] = es.enter_context(nc.sbuf_tensor("K_nlam", [128, 1], F32))
        K["gs"] = es.enter_context(nc.sbuf_tensor("K_gs", [128, 2], F32))
        setup_phase(nc, K, ins)
        if upto >= 2:
            ffn_phase(nc, ps, K, ins["xT"], S["x1T"], [0, 1024, 2048, 3072], F_G1, ins["ffn1_wg"], ins["ffn1_wu"], ins["ffn1_wd"], "f1")
        if upto >= 3:
            proj_phase(nc, ps, K, S["x1T"], ins["w_in"], ins["pos"], S)
        if upto >= 4:
            dil_phase(nc, ps, K, S)
        if upto >= 5:
            diff_phase(nc, ps, K, S)
        if upto >= 6:
            out_phase(nc, ps, K, S, ins["w_a"], ins["w_b"], ins["w_o"])
        if upto >= 7:
            ffn_phase(nc, ps, K, S["x2T"], outT, [0, 1024], F_G2, ins["ffn2_wg"], ins["ffn2_wu"], ins["ffn2_wd"], "f2")
    return nc, list(ins.keys())


def host_consts():
    cb = np.zeros((128, NCB), np.float32)
    cb[:, C_INV2048:C_INV2048 + 128] = 1.0 / 2048
    cb[:, C_INV128:C_INV128 + 128] = 1.0 / 128
    cb[:, C_INV256:C_INV256 + 128] = 1.0 / 256
    cb[:, C_ONES:C_ONES + 128] = 1.0
    for m in range(16):
        cb[m + 16, C_PSW + m] = 1.0
        cb[m, C_PSW + m + 16] = 1.0
    ik = np.arange(128)[:, None]
    iq = np.arange(128)[None, :]
    cb[:, C_MDIL:C_MDIL + 128] = (ik >= iq)
    cb[:, C_MDIL + 128:C_MDIL + 256] = (ik <= iq)
    tri = (ik <= iq).astype(np.float32)
    m1 = np.concatenate([tri, np.ones((128, 128), np.float32)], 1)
    m2 = np.concatenate([np.zeros((128, 128), np.float32), tri], 1)
    cb[:, C_M1:C_M1 + 512] = np.concatenate([m1, m1], 1)
    cb[:, C_M2:C_M2 + 512] = np.concatenate([m2, m2], 1)
    return cb


def make_in_maps(x, positions, ffn1_norm, ffn1_w_gate, ffn1_w_up, ffn1_w_down, mix_norm, w_in,
                 dil_q_norm, dil_k_norm, diff_q_norm, diff_k_norm, diff_lq1, diff_lk1, diff_lq2, diff_lk2,
                 diff_subln, w_dil_branch, w_diff_branch, w_out, ffn2_norm, ffn2_w_gate, ffn2_w_up, ffn2_w_down):
    f = lambda a: np.ascontiguousarray(np.asarray(a, dtype=np.float32))
    x = np.asarray(x, np.float32)
    positions = np.asarray(positions, np.int32)
    cb = host_consts()
    cf = np.zeros((128, NCF), np.float32)
    cf[:, F_G1:F_G1 + 16] = f(ffn1_norm)[0].reshape(16, 128).T
    cf[:, F_GM:F_GM + 16] = f(mix_norm)[0].reshape(16, 128).T
    cf[:, F_G2:F_G2 + 16] = f(ffn2_norm)[0].reshape(16, 128).T
    cf[:, F_HG + 0] = f(dil_q_norm)[0]
    cf[:, F_HG + 1] = f(dil_k_norm)[0]
    cf[:, F_HG + 2] = f(diff_q_norm)[0]
    cf[:, F_HG + 3] = f(diff_k_norm)[0]
    cf[:, F_SUB:F_SUB + 2] = f(diff_subln)[0].reshape(2, 128).T
    inv = (np.float32(500000.0) ** (-np.arange(0, 32, 2, dtype=np.float32) / np.float32(32))).astype(np.float32)
    cf[0:16, F_INV] = inv
    cf[16:32, F_INV] = inv
    cf[0:16, F_SGN] = -1.0
    cf[16:32, F_SGN] = 1.0
    cf[:, F_EPS] = EPS
    lamv = np.stack([f(diff_lq1)[0], f(diff_lk1)[0], f(diff_lq2)[0], f(diff_lk2)[0]], 0)
    lamv = np.ascontiguousarray(np.broadcast_to(lamv[None], (128, 4, 128)))
    shared = {
        "cbf": cb, "cf32": cf, "lamv": lamv,
        "ffn1_wg": f(ffn1_w_gate)[0], "ffn1_wu": f(ffn1_w_up)[0], "ffn1_wd": f(ffn1_w_down)[0],
        "ffn2_wg": f(ffn2_w_gate)[0], "ffn2_wu": f(ffn2_w_up)[0], "ffn2_wd": f(ffn2_w_down)[0],
        "w_in": f(w_in)[0], "w_a": f(w_dil_branch)[0], "w_b": f(w_diff_branch)[0], "w_o": f(w_out)[0],
    }
    maps = []
    for core in range(8):
        b, half = core // 2, core % 2
        xT = np.zeros((D, TFR), np.float32)
        pos = np.zeros((TFR,), np.int32)
        xT[:, 2048:] = x[b, half * 2048:(half + 1) * 2048, :].T
        pos[2048:] = positions[b, half * 2048:(half + 1) * 2048]
        if half == 1:
            xT[:, :2048] = x[b, 0:2048, :].T
            pos[:2048] = positions[b, 0:2048]
        m = dict(shared)
        m["xT"] = np.ascontiguousarray(xT.reshape(16, 128, TFR))
        m["pos"] = np.ascontiguousarray(np.broadcast_to(pos[None], (32, TFR)))
        m["flag"] = np.full((128, 128), float(half), np.float32)
        maps.append(m)
    return maps


def kernel(**inputs):
    maps = make_in_maps(**inputs)
    nc, names = build()
    maps = [{k: m[k] for k in names} for m in maps]
    res = run_bass_kernel_spmd(nc, maps, core_ids=list(range(8)))
    out = np.empty((4, 4096, D), np.float32)
    for core in range(8):
        b, half = core // 2, core % 2
        oT = np.asarray(res.results[core]["outT"], np.float32).reshape(D, TOWN)
        out[b, half * 2048:(half + 1) * 2048, :] = oT.T
    return out
```

```python
import math
from contextlib import ExitStack

import numpy as np
import concourse.bass as bass
import concourse.mybir as mybir
from concourse.bass_utils import run_bass_kernel_spmd

F32 = mybir.dt.float32
BF16 = mybir.dt.bfloat16
I32 = mybir.dt.int32
AF = mybir.ActivationFunctionType
ALU = mybir.AluOpType
AX = mybir.AxisListType

D = 2048
DFF = 5632
DIN = 11776
TOWN = 2048
TFR = 4096
EPS = 1e-6
SCALE = 128.0 ** -0.5
PI = math.pi
ENGS = ("pe", "act", "dve", "pool", "sp")

C_INV2048, C_INV128, C_INV256, C_ONES, C_PSW, C_MDIL, C_M1, C_M2 = 0, 128, 256, 384, 512, 640, 896, 1408
NCB = 1920
F_G1, F_GM, F_G2, F_HG, F_SUB, F_INV, F_SGN, F_EPS = 0, 16, 32, 48, 52, 54, 55, 56
NCF = 57


class Op:
    __slots__ = ("eng", "fn", "deps", "kind", "ms", "val", "sem")

    def __init__(self, eng, fn, kind):
        self.eng, self.fn, self.kind = eng, fn, kind
        self.deps = []
        self.ms = False
        self.val = 0
        self.sem = None


class Buf:
    def __init__(self, name):
        self.name = name
        self.writes = {}
        self.reads = {}
        self.dsem = None
        self.dcnt = 0


class Prog:
    def __init__(self, nc, es, tag):
        self.nc, self.es, self.tag = nc, es, tag
        self.ops = {e: [] for e in ENGS}
        self.allsems = []
        self.csem = {e: self._sem(f"{tag}_{e}") for e in ENGS[:4]}
        self.dma_ops = []
        self.nb = 0

    def _sem(self, name):
        h = self.nc.alloc_semaphore(name=name)
        self.allsems.append(h)
        return h

    def sb(self, name, shape, dt):
        t = self.es.enter_context(self.nc.sbuf_tensor(f"{self.tag}_{name}", shape, dt))
        return t, Buf(name)

    def buf(self, name="b"):
        return Buf(name)

    @staticmethod
    def _key(op):
        return op.eng if op.kind == "c" else ("d", id(op.sem))

    def rec(self, eng, fn, reads=(), writes=(), kind="c", extra=()):
        op = Op(eng, fn, kind)
        deps = {}
        for b in reads:
            for k, d in b.writes.items():
                deps[id(d)] = d
        for b in writes:
            if b.reads:
                for k, d in b.reads.items():
                    deps[id(d)] = d
            else:
                for k, d in b.writes.items():
                    if not (d.kind == "c" and d.eng == eng):
                        deps[id(d)] = d
        for d in extra:
            deps[id(d)] = d
        if eng == "pe":
            deps = {i: d for i, d in deps.items() if not (d.kind == "c" and d.eng == "pe")}
        op.deps = list(deps.values())
        for d in op.deps:
            if d.kind == "c":
                d.ms = True
        self.ops[eng].append(op)
        return op

    def _commit(self, op, reads, writes):
        k = self._key(op)
        for b in reads:
            b.reads[k] = op
        for b in writes:
            if b.reads:
                b.reads = {}
                b.writes = {}
            b.writes[k] = op

    def c(self, eng, fn, reads=(), writes=()):
        op = self.rec(eng, fn, reads, writes)
        self._commit(op, reads, writes)
        return op

    def dma(self, q, out, in_, sbuf, load, reads=(), writes=()):
        if sbuf.dsem is None:
            self.nb += 1
            sbuf.dsem = self._sem(f"{self.tag}_d{self.nb}")
        r = list(reads) + ([] if load else [sbuf])
        w = list(writes) + ([sbuf] if load else [])
        op = self.rec(q, lambda e: e.dma_start(out=out, in_=in_), r, w, kind="d")
        sbuf.dcnt += 16
        op.sem = sbuf.dsem
        op.val = sbuf.dcnt
        self._commit(op, r, w)
        self.dma_ops.append(op)
        return op

    def run(self):
        for q in ("sp", "pool"):
            op = Op(q, None, "c")
            last = {}
            for d in self.dma_ops:
                last[id(d.sem)] = d
            op.deps = list(last.values())
            self.ops[q].append(op)
        for e in ENGS:
            cnt = 0
            for op in self.ops[e]:
                if op.kind == "c" and op.ms:
                    cnt += 1
                    op.val = cnt
        csem = self.csem

        def mk(e):
            ops = self.ops[e]

            def body(eng):
                waited = {}
                for op in ops:
                    for d in op.deps:
                        sem = d.sem if d.kind == "d" else csem[d.eng]
                        if waited.get(id(sem), 0) < d.val:
                            eng.wait_ge(sem, d.val)
                            waited[id(sem)] = d.val
                    if op.fn is not None:
                        ins = op.fn(eng)
                        if op.kind == "d":
                            ins.then_inc(op.sem, 16)
                        elif op.ms:
                            ins.then_inc(csem[e], 1)
            return body

        with self.nc.Block() as block:
            block.tensor(mk("pe"))
            block.scalar(mk("act"))
            block.vector(mk("dve"))
            block.gpsimd(mk("pool"))
            block.sync(mk("sp"))
        self.nc.clear_and_free_semaphores(self.allsems)
        self.nc.all_engine_barrier()


def reset_bufs(ps):
    for _, b in ps:
        b.reads, b.writes = {}, {}


class RR:
    def __init__(self, items):
        self.items, self.i = items, 0

    def next(self):
        x = self.items[self.i % len(self.items)]
        self.i += 1
        return x


def setup_phase(nc, K, ins):
    with ExitStack() as es:
        P = Prog(nc, es, "su")
        bcb, bcf, bfl, blam = Buf("cb"), Buf("cf"), Buf("fl"), Buf("lam")
        P.dma("pool", K["cb"][:], ins["cbf"], bcb, True)
        P.dma("pool", K["flag"][:], ins["flag"], bfl, True)
        P.dma("sp", K["cf"][:], ins["cf32"], bcf, True)
        lamt, blt = P.sb("lamt", [128, 4, 128], F32)
        pr, bpr = P.sb("pr", [128, 2, 128], F32)
        ss, bss = P.sb("ss", [128, 2], F32)
        ee, bee = P.sb("ee", [128, 2], F32)
        P.dma("sp", lamt[:], ins["lamv"], blt, True)
        P.c("dve", lambda e: e.tensor_tensor(out=pr[:, 0, :], in0=lamt[:, 0, :], in1=lamt[:, 1, :], op=ALU.mult), [blt], [bpr])
        P.c("dve", lambda e: e.tensor_tensor(out=pr[:, 1, :], in0=lamt[:, 2, :], in1=lamt[:, 3, :], op=ALU.mult), [blt], [bpr])
        P.c("dve", lambda e: e.reduce_sum(out=ss[:, 0:1], in_=pr[:, 0, :], axis=AX.X), [bpr], [bss])
        P.c("dve", lambda e: e.reduce_sum(out=ss[:, 1:2], in_=pr[:, 1, :], axis=AX.X), [bpr], [bss])
        P.c("act", lambda e: e.activation(out=ee[:], in_=ss[:], func=AF.Exp), [bss], [bee])
        P.c("dve", lambda e: e.tensor_tensor(out=K["nlam"][:], in0=ee[:, 1:2], in1=ee[:, 0:1], op=ALU.subtract), [bee], [blam])
        P.c("dve", lambda e: e.tensor_scalar_add(out=K["nlam"][:], in0=K["nlam"][:], scalar1=-0.2), [blam], [blam])
        P.c("dve", lambda e: e.tensor_scalar_mul(out=K["gs"][:], in0=K["cf"][:, F_SUB:F_SUB + 2], scalar1=0.8), [bcf], [blam])
        P.run()


def norm_prologue(P, ps, K, src, t0, gcol, hT, bhT, xin, sq, rstd, brstd):
    cb = K["cb"]
    for half in range(2):
        for c in range(16):
            xt, bx = xin.next()
            st, bs = sq.next()
            P.dma("sp", xt[:], src[c, :, t0 + half * 512:t0 + half * 512 + 512], bx, True)
            P.c("act", lambda e, st=st, xt=xt: e.activation(out=st[:], in_=xt[:], func=AF.Square), [bx], [bs])
            P.c("pe", lambda e, st=st, c=c, half=half: e.matmul(ps[6 + half][0][:], cb[:, C_INV2048:C_INV2048 + 128], st[:],
                                                                start=(c == 0), stop=(c == 15)), [bs], [ps[6 + half][1]])
        P.c("act", lambda e, half=half: e.activation(out=rstd[:, half * 512:half * 512 + 512], in_=ps[6 + half][0][:], func=AF.Sqrt,
                                                     bias=K["cf"][:, F_EPS:F_EPS + 1], scale=1.0), [ps[6 + half][1]], [brstd])
        P.c("dve", lambda e, half=half: e.reciprocal(out=rstd[:, half * 512:half * 512 + 512], in_=rstd[:, half * 512:half * 512 + 512]),
            [brstd], [brstd])
    for half in range(2):
        for c in range(16):
            xt, bx = xin.next()
            P.dma("sp", xt[:], src[c, :, t0 + half * 512:t0 + half * 512 + 512], bx, True)
            P.c("dve", lambda e, xt=xt, c=c, half=half: e.scalar_tensor_tensor(
                out=hT[:, c, half * 512:half * 512 + 512], in0=xt[:], scalar=K["cf"][:, gcol + c:gcol + c + 1],
                in1=rstd[:, half * 512:half * 512 + 512], op0=ALU.mult, op1=ALU.mult), [bx, brstd], [bhT])


def ffn_phase(nc, ps, K, src, dst, t0s, gcol, Wg, Wu, Wd, tag):
    with ExitStack() as es:
        reset_bufs(ps)
        P = Prog(nc, es, tag)
        hT, bhT = P.sb("hT", [128, 16, 1024], BF16)
        aT, baT = P.sb("aT", [128, 44, 1024], BF16)
        slots = RR([P.sb(f"w{i}", [128, 8192], BF16) for i in range(2)])
        xin = RR([P.sb(f"xin{i}", [128, 512], F32) for i in range(3)])
        sq = RR([P.sb(f"sq{i}", [128, 512], BF16) for i in range(2)])
        sil = RR([P.sb(f"sil{i}", [128, 512], F32) for i in range(2)])
        xout = RR([P.sb(f"xo{i}", [128, 512], F32) for i in range(2)])
        rstd, brstd = P.sb("rstd", [128, 1024], F32)
        Wgv = Wg.rearrange("(k p) n -> p k n", p=128)
        Wuv = Wu.rearrange("(k p) n -> p k n", p=128)
        Wdv = Wd.rearrange("(f p) n -> p f n", p=128)
        for it, t0 in enumerate(t0s):
            if it == 0:
                norm_prologue(P, ps, K, src, t0, gcol, hT, bhT, xin, sq, rstd, brstd)
            u = 0
            for p in range(22):
                sl, bsl = slots.next()
                gv = sl[:, 0:4096].rearrange("p (k n) -> p k n", k=16)
                uv = sl[:, 4096:8192].rearrange("p (k n) -> p k n", k=16)
                P.dma("pool", gv, Wgv[:, :, p * 256:p * 256 + 256], bsl, True)
                P.dma("pool", uv, Wuv[:, :, p * 256:p * 256 + 256], bsl, True)
                for j in range(2):
                    for half in range(2):
                        bg, bu = ps[(u % 2) * 2], ps[(u % 2) * 2 + 1]
                        u += 1

                        def mm(e, gv=gv, uv=uv, j=j, half=half, bg=bg, bu=bu):
                            for k in range(16):
                                e.matmul(bg[0][:], gv[:, k, j * 128:j * 128 + 128], hT[:, k, half * 512:half * 512 + 512],
                                         start=(k == 0), stop=(k == 15))
                            for k in range(16):
                                ins = e.matmul(bu[0][:], uv[:, k, j * 128:j * 128 + 128], hT[:, k, half * 512:half * 512 + 512],
                                               start=(k == 0), stop=(k == 15))
                            return ins
                        P.c("pe", mm, [bsl, bhT], [bg[1], bu[1]])
                        st, bst = sil.next()
                        f = p * 2 + j
                        P.c("act", lambda e, st=st, bg=bg: e.activation(out=st[:], in_=bg[0][:], func=AF.Silu), [bg[1]], [bst])
                        P.c("dve", lambda e, st=st, bu=bu, f=f, half=half: e.tensor_tensor(
                            out=aT[:, f, half * 512:half * 512 + 512], in0=st[:], in1=bu[0][:], op=ALU.mult), [bst, bu[1]], [baT])
            if it + 1 < len(t0s):
                norm_prologue(P, ps, K, src, t0s[it + 1], gcol, hT, bhT, xin, sq, rstd, brstd)
            for dp in range(8):
                base = 4 if dp % 2 == 0 else 0
                for fh in range(2):
                    sl, bsl = slots.next()
                    dv = sl[:, 0:5632].rearrange("p (f n) -> p f n", f=22)
                    P.dma("pool", dv, Wdv[:, fh * 22:fh * 22 + 22, dp * 256:dp * 256 + 256], bsl, True)
                    for j in range(2):
                        for half in range(2):
                            bk = ps[base + j * 2 + half]

                            def mm(e, dv=dv, j=j, half=half, bk=bk, fh=fh):
                                for f in range(22):
                                    ins = e.matmul(bk[0][:], dv[:, f, j * 128:j * 128 + 128],
                                                   aT[:, fh * 22 + f, half * 512:half * 512 + 512],
                                                   start=(fh == 0 and f == 0), stop=(fh == 1 and f == 21))
                                return ins
                            P.c("pe", mm, [bsl, baT], [bk[1]])
                for j in range(2):
                    for half in range(2):
                        bk = ps[base + j * 2 + half]
                        c = dp * 2 + j
                        xt, bx = xin.next()
                        xo, bxo = xout.next()
                        tt = t0 + half * 512
                        P.dma("sp", xt[:], src[c, :, tt:tt + 512], bx, True)
                        P.c("dve", lambda e, xo=xo, xt=xt, bk=bk: e.scalar_tensor_tensor(
                            out=xo[:], in0=bk[0][:], scalar=0.5, in1=xt[:], op0=ALU.mult, op1=ALU.add), [bk[1], bx], [bxo])
                        P.dma("sp", dst[c, :, tt:tt + 512], xo[:], bxo, False)
        P.run()


def proj_phase(nc, ps, K, x1T, win, posd, S, tiles=(0, 1, 2, 3), pfilter=None):
    with ExitStack() as es:
        reset_bufs(ps)
        P = Prog(nc, es, "pj")
        cb, cf = K["cb"], K["cf"]
        hT, bhT = P.sb("hT", [128, 16, 1024], BF16)
        slots = RR([P.sb(f"w{i}", [128, 8192], BF16) for i in range(3)])
        xin = RR([P.sb(f"xin{i}", [128, 512], F32) for i in range(3)])
        sq = RR([P.sb(f"sq{i}", [128, 512], BF16) for i in range(3)])
        qg = RR([P.sb(f"qg{i}", [128, 512], BF16) for i in range(3)])
        rs = RR([P.sb(f"rs{i}", [128, 512], F32) for i in range(3)])
        qo = RR([P.sb(f"qo{i}", [128, 512], BF16) for i in range(3)])
        t1 = RR([P.sb(f"t1{i}", [32, 512], F32) for i in range(3)])
        t2 = RR([P.sb(f"t2{i}", [32, 512], F32) for i in range(3)])
        vst = RR([P.sb(f"vs{i}", [128, 512], BF16) for i in range(3)])
        gst = RR([P.sb(f"gs{i}", [128, 512], F32) for i in range(3)])
        rstd, brstd = P.sb("rstd", [128, 1024], F32)
        posi, bposi = P.sb("posi", [32, 1024], I32)
        ang, bang = P.sb("ang", [32, 1024], F32)
        a1, ba1 = P.sb("a1", [32, 1024], F32)
        ki, bki = P.sb("ki", [32, 1024], I32)
        kf, bkf = P.sb("kf", [32, 1024], F32)
        Ct, bC = P.sb("C", [32, 1024], F32)
        St, bS = P.sb("S", [32, 1024], F32)
        npi, bnpi = P.sb("npi", [32, 1], F32)
        P.c("dve", lambda e: e.memset(npi[:], -PI), [], [bnpi])
        Wv = win.rearrange("(k p) n -> p k n", p=128)
        panels = []
        for i in range(3):
            panels.append((i * 512, "qk", ("Qd", i * 4, 0)))
        for i in range(3):
            panels.append((1536 + i * 512, "qk", ("Kd", i * 4, 1)))
        for i in range(3):
            panels.append((3072 + i * 512, "v", ("Vd", i * 512)))
        for i in range(2):
            panels.append((4608 + i * 512, "qk", ("Qf", i * 4, 2)))
        for i in range(2):
            panels.append((5632 + i * 512, "qk", ("Kf", i * 4, 3)))
        for i in range(2):
            panels.append((6656 + i * 512, "v", ("Vf", i * 512)))
        for i in range(8):
            panels.append((7680 + i * 512, "g", (i * 4,)))
        u = 0
        if pfilter is not None:
            panels = [p for i, p in enumerate(panels) if i in pfilter]
        for ti in tiles:
            t0 = ti * 1024
            own = ti >= 2
            norm_prologue(P, ps, K, x1T, t0, F_GM, hT, bhT, xin, sq, rstd, brstd)
            P.dma("sp", posi[:], posd[:, t0:t0 + 1024], bposi, True)
            P.c("dve", lambda e: e.tensor_copy(out=ang[:], in_=posi[:]), [bposi], [bang])
            P.c("dve", lambda e: e.tensor_scalar_mul(out=ang[:], in0=ang[:], scalar1=cf[0:32, F_INV:F_INV + 1]), [bang], [bang])
            for (off, Tt, bT) in ((0.75, Ct, bC), (0.5, St, bS)):
                P.c("dve", lambda e, off=off: e.tensor_scalar(out=a1[:], in0=ang[:], scalar1=1.0 / (2 * PI), scalar2=off, op0=ALU.mult,
                                                              op1=ALU.add), [bang], [ba1])
                P.c("dve", lambda e: e.tensor_copy(out=ki[:], in_=a1[:]), [ba1], [bki])
                P.c("dve", lambda e: e.tensor_copy(out=kf[:], in_=ki[:]), [bki], [bkf])
                P.c("dve", lambda e: e.tensor_tensor(out=a1[:], in0=a1[:], in1=kf[:], op=ALU.subtract), [ba1, bkf], [ba1])
                P.c("dve", lambda e: e.scalar_tensor_tensor(out=a1[:], in0=a1[:], scalar=0.0, in1=a1[:], op0=ALU.is_lt, op1=ALU.add),
                    [ba1], [ba1])
                P.c("act", lambda e, Tt=Tt: e.activation(out=Tt[:], in_=a1[:], func=AF.Sin, bias=npi[:], scale=2 * PI), [ba1, bnpi], [bT])
            P.c("dve", lambda e: e.tensor_scalar_mul(out=St[:], in0=St[:], scalar1=cf[0:32, F_SGN:F_SGN + 1]), [bS], [bS])
            tile_units = []
            for (c0, kind, info) in panels:
                if not own and (kind == "g" or info[0] in ("Qd", "Qf")):
                    continue
                grp = (info[1] // 4 if kind == "qk" else info[1] // 512) if info[0] in ("Kd", "Vd") else 2
                first_used = True
                if kind == "v":
                    for tb in range(8):
                        if not own and grp < 2 and (ti == 0 or tb < (7 if grp == 0 else 4)):
                            continue
                        tile_units.append((c0, kind, info, tb, 0, first_used))
                        first_used = False
                else:
                    for j in range(4):
                        for half in range(2):
                            if not own and grp < 2 and (ti == 0 or half == 0):
                                continue
                            tile_units.append((c0, kind, info, j, half, first_used))
                            first_used = False
            stB = {}

            def stageA(idx, t0=t0):
                nonlocal u
                c0, kind, info, j, half, firstu = tile_units[idx]
                if firstu:
                    sl, bsl = slots.next()
                    wv = sl[:, :].rearrange("p (k n) -> p k n", k=16)
                    P.dma("pool", wv, Wv[:, :, c0:c0 + 512], bsl, True)
                    stageA.cur = (wv, bsl)
                wv, bsl = stageA.cur
                bk = ps[u % 4]
                u += 1
                if kind == "v":
                    tb = j

                    def mm(e):
                        for k in range(16):
                            ins = e.matmul(bk[0][:], hT[:, k, tb * 128:tb * 128 + 128], wv[:, k, :], start=(k == 0), stop=(k == 15))
                        return ins
                    P.c("pe", mm, [bsl, bhT], [bk[1]])
                    vt, bvt = vst.next()
                    P.c("act", lambda e: e.activation(out=vt[:], in_=bk[0][:], func=AF.Copy), [bk[1]], [bvt])
                    P.dma("sp", S[info[0]][t0 + tb * 128:t0 + tb * 128 + 128, info[1]:info[1] + 512], vt[:], bvt, False)
                    return
                hs = slice(half * 512, half * 512 + 512)

                def mm(e):
                    for k in range(16):
                        ins = e.matmul(bk[0][:], wv[:, k, j * 128:j * 128 + 128], hT[:, k, hs], start=(k == 0), stop=(k == 15))
                    return ins
                P.c("pe", mm, [bsl, bhT], [bk[1]])
                if kind == "g":
                    gt, bgt = gst.next()
                    P.c("act", lambda e: e.activation(out=gt[:], in_=bk[0][:], func=AF.Sigmoid), [bk[1]], [bgt])
                    tt = t0 - 2048 + half * 512
                    P.dma("sp", S["SG"][info[0] + j, :, tt:tt + 512], gt[:], bgt, False)
                    return
                name, ch0, gi = info
                st, bs = sq.next()
                qt, bq = qg.next()
                P.c("act", lambda e: e.activation(out=st[:], in_=bk[0][:], func=AF.Square), [bk[1]], [bs])
                P.c("act", lambda e: e.activation(out=qt[:], in_=bk[0][:], func=AF.Copy, scale=cf[:, F_HG + gi:F_HG + gi + 1]),
                    [bk[1]], [bq])
                stB[idx] = (st, bs, qt, bq, u)

            def stageB(idx, t0=t0):
                if idx not in stB:
                    return
                c0, kind, info, j, half, firstu = tile_units[idx]
                st, bs, qt, bq, uu = stB.pop(idx)
                name, ch0, gi = info
                hs = slice(half * 512, half * 512 + 512)
                rt, brt = rs.next()
                ot, bo = qo.next()
                x1, bx1 = t1.next()
                x2, bx2 = t2.next()
                bss, bsw = ps[4 + (uu % 2)], ps[6 + (uu % 2)]
                P.c("pe", lambda e: e.matmul(bss[0][:], cb[:, C_INV128:C_INV128 + 128], st[:], start=True, stop=True), [bs], [bss[1]])
                P.c("pe", lambda e: e.matmul(bsw[0][:], cb[:, C_PSW:C_PSW + 128], qt[:], start=True, stop=True), [bq], [bsw[1]])
                P.c("act", lambda e: e.activation(out=rt[:], in_=bss[0][:], func=AF.Ln, bias=cf[:, F_EPS:F_EPS + 1], scale=1.0),
                    [bss[1]], [brt])
                P.c("act", lambda e: e.activation(out=rt[:], in_=rt[:], func=AF.Exp, scale=-0.5), [brt], [brt])
                P.c("dve", lambda e: e.tensor_tensor(out=x1[:], in0=qt[0:32, :], in1=Ct[:, hs], op=ALU.mult), [bq, bC], [bx1])
                P.c("dve", lambda e: e.tensor_tensor(out=ot[:], in0=qt[:], in1=rt[:], op=ALU.mult), [bq, brt], [bo])
                P.c("dve", lambda e: e.tensor_tensor(out=x2[:], in0=bsw[0][0:32, :], in1=St[:, hs], op=ALU.mult), [bsw[1], bS], [bx2])
                P.c("dve", lambda e: e.tensor_tensor(out=x1[:], in0=x1[:], in1=x2[:], op=ALU.add), [bx1, bx2], [bx1])
                P.c("dve", lambda e: e.tensor_tensor(out=ot[0:32, :], in0=x1[:], in1=rt[0:32, :], op=ALU.mult), [bx1, brt, bo], [bo])
                tt = (t0 - 2048 + half * 512) if name[0] == "Q" else (t0 + half * 512)
                P.dma("sp", S[name][ch0 + j, :, tt:tt + 512], ot[:], bo, False)

            nu = len(tile_units)
            stageA(0)
            for idx in range(nu):
                if idx + 1 < nu:
                    stageA(idx + 1)
                stageB(idx)
        P.run()


def dil_phase(nc, ps, K, S, LOOK=2):
    with ExitStack() as es:
        reset_bufs(ps)
        P = Prog(nc, es, "dl")
        cb = K["cb"]
        Vg, bVg = P.sb("Vg", [128, 32, 512], BF16)
        KT = RR([P.sb(f"KT{i}", [128, 4096], BF16) for i in range(2)])
        QT = RR([P.sb(f"QT{i}", [128, 2048], BF16) for i in range(2)])
        Ua = [P.sb(f"Ua{h}", [128, 2048], F32) for h in range(4)]
        Da = [P.sb(f"Da{h}", [128, 2048], F32) for h in range(4)]
        Pt = RR([P.sb(f"Pt{i}", [128, 256], BF16) for i in range(4)])
        Pm = RR([P.sb(f"Pm{i}", [128, 256], BF16) for i in range(4)])
        od = RR([P.sb(f"od{i}", [128, 2048], BF16) for i in range(2)])
        ones = cb[:, C_ONES:C_ONES + 128]
        flag = K["flag"][:, :]
        units = []
        for g, r in enumerate((1, 4, 16)):
            nmb = 32 // r
            for h in range(4):
                for c in range(r):
                    for mb in range(nmb // 2, nmb):
                        units.append((g, r, h, c, mb))
        st = {}
        cur = {"gh": None, "g": None}

        def stage1(u):
            g, r, h, c, mb = units[u]
            if cur["gh"] != (g, h):
                cur["gh"] = (g, h)
                kt, bkt = KT.next()
                qt, bqt = QT.next()
                P.dma("sp", kt[:], S["Kd"][g * 4 + h], bkt, True)
                P.dma("sp", qt[:], S["Qd"][g * 4 + h], bqt, True)
                cur["kq"] = (kt, bkt, qt, bqt)
            kt, bkt, qt, bqt = cur["kq"]
            bS_ = ps[u % 4]
            q0 = mb * 128 * r + c - 2048
            qsl = slice(q0, q0 + 127 * r + 1, r)
            kp = slice((mb - 1) * 128 * r + c, (mb - 1) * 128 * r + c + 127 * r + 1, r)
            kc = slice(mb * 128 * r + c, mb * 128 * r + c + 127 * r + 1, r)

            def mm1(e):
                e.matmul(bS_[0][:, 0:128], kt[:, kp], qt[:, qsl], start=True, stop=True)
                return e.matmul(bS_[0][:, 128:256], kt[:, kc], qt[:, qsl], start=True, stop=True)
            P.c("pe", mm1, [bkt, bqt], [bS_[1]])
            pt, bpt = Pt.next()
            pm, bpm = Pm.next()
            P.c("act", lambda e: e.activation(out=pt[:], in_=bS_[0][:, 0:256], func=AF.Exp, scale=SCALE), [bS_[1]], [bpt])
            P.c("pool", lambda e: e.tensor_tensor(out=pm[:], in0=pt[:], in1=cb[:, C_MDIL:C_MDIL + 256], op=ALU.mult), [bpt], [bpm])
            st[u] = (pm, bpm, qsl)

        def stage2(u):
            g, r, h, c, mb = units[u]
            nmb = 32 // r
            if cur["g"] != g:
                cur["g"] = g
                for m2 in range(nmb):
                    src = S["Vd"][m2 * 128 * r:(m2 + 1) * 128 * r, g * 512:(g + 1) * 512].rearrange("(p r) n -> p r n", r=r)
                    P.dma("sp", Vg[:, m2 * r:(m2 + 1) * r, :], src, bVg, True)
            pm, bpm, qsl = st.pop(u)
            bU_ = ps[4 + u % 4]
            prev_pref = (mb - 1) < nmb // 2
            hc = slice(h * 128, h * 128 + 128)

            def mm2(e):
                e.matmul(bU_[0][:, 0:128], Vg[:, (mb - 1) * r + c, hc], pm[:, 0:128], start=True, stop=False)
                e.matmul(bU_[0][:, 0:128], Vg[:, mb * r + c, hc], pm[:, 128:256], start=False, stop=True)
                e.matmul(bU_[0][:, 128:256], flag if prev_pref else ones, pm[:, 0:128], start=True, stop=False)
                return e.matmul(bU_[0][:, 128:256], ones, pm[:, 128:256], start=False, stop=True)
            P.c("pe", mm2, [bpm, bVg], [bU_[1]])
            ua, bua = Ua[h]
            da, bda = Da[h]
            if g == 0:
                P.c("dve", lambda e: e.tensor_copy(out=ua[:, qsl], in_=bU_[0][:, 0:128]), [bU_[1]], [bua])
                P.c("dve", lambda e: e.tensor_copy(out=da[:, qsl], in_=bU_[0][:, 128:256]), [bU_[1]], [bda])
            else:
                P.c("dve", lambda e: e.tensor_tensor(out=ua[:, qsl], in0=ua[:, qsl], in1=bU_[0][:, 0:128], op=ALU.add), [bU_[1], bua], [bua])
                P.c("dve", lambda e: e.tensor_tensor(out=da[:, qsl], in0=da[:, qsl], in1=bU_[0][:, 128:256], op=ALU.add), [bU_[1], bda], [bda])

        n = len(units)
        for u in range(min(LOOK, n)):
            stage1(u)
        for u in range(n):
            if u + LOOK < n:
                stage1(u + LOOK)
            stage2(u)
        for h in range(4):
            ua, bua = Ua[h]
            da, bda = Da[h]
            ot, bo = od.next()
            P.c("dve", lambda e, da=da: e.reciprocal(out=da[:], in_=da[:]), [bda], [bda])
            P.c("dve", lambda e, ot=ot, ua=ua, da=da: e.tensor_tensor(out=ot[:], in0=ua[:], in1=da[:], op=ALU.mult), [bua, bda], [bo])
            P.dma("sp", S["Od"][h], ot[:], bo, False)
        P.run()


def diff_phase(nc, ps, K, S, LOOK=2):
    with ExitStack() as es:
        reset_bufs(ps)
        P = Prog(nc, es, "df")
        cb = K["cb"]
        KT = RR([P.sb(f"KT{i}", [128, 2, 4096], BF16) for i in range(2)])
        QT = RR([P.sb(f"QT{i}", [128, 2, 2048], BF16) for i in range(2)])
        Vh = RR([P.sb(f"Vh{i}", [128, 32, 256], BF16) for i in range(2)])
        Pt = RR([P.sb(f"Pt{i}", [128, 512], BF16) for i in range(4)])
        rd, brd = P.sb("rd", [128, 512], F32)
        o1, bo1 = P.sb("o1", [128, 512], F32)
        o2, bo2 = P.sb("o2", [128, 512], F32)
        sqt, bsq = P.sb("sq", [128, 512], BF16)
        rst, brs = P.sb("rs", [128, 256], F32)
        of = RR([P.sb(f"of{i}", [128, 512], BF16) for i in range(2)])
        ones = cb[:, C_ONES:C_ONES + 128]
        flag = K["flag"][:, :]
        nlam = K["nlam"]
        gs = K["gs"]
        units = []
        for h in range(4):
            for qi in range(8):
                nkb = 18 + 2 * qi
                for kb in range(nkb):
                    units.append((h, qi, kb, nkb))
        st = {}
        cur = {"h": None}
        bU = (ps[2], ps[3])
        bD = ps[4]
        bN = ps[5]

        def stage1(u):
            h, qi, kb, nkb = units[u]
            if cur["h"] != h:
                cur["h"] = h
                kt, bkt = KT.next()
                qt, bqt = QT.next()
                vh, bvh = Vh.next()
                for cpt in range(2):
                    P.dma("sp", kt[:, cpt, :], S["Kf"][h * 2 + cpt], bkt, True)
                    P.dma("sp", qt[:, cpt, :], S["Qf"][h * 2 + cpt], bqt, True)
                P.dma("sp", vh[:], S["Vf"][:, h * 256:(h + 1) * 256].rearrange("(kb p) n -> p kb n", p=128), bvh, True)
                cur["kqv"] = (kt, bkt, qt, bqt, vh, bvh)
            kt, bkt, qt, bqt, vh, bvh = cur["kqv"]
            bS_ = ps[(0, 1, 6)[u % 3]]
            qs = slice(qi * 256, qi * 256 + 256)
            ks = slice(kb * 128, kb * 128 + 128)

            def mm1(e):
                e.matmul(bS_[0][:, 0:256], kt[:, 0, ks], qt[:, 0, qs], start=True, stop=True)
                return e.matmul(bS_[0][:, 256:512], kt[:, 1, ks], qt[:, 1, qs], start=True, stop=True)
            P.c("pe", mm1, [bkt, bqt], [bS_[1]])
            pt, bpt = Pt.next()
            P.c("act", lambda e: e.activation(out=pt[:], in_=bS_[0][:], func=AF.Exp, scale=SCALE), [bS_[1]], [bpt])
            if kb >= nkb - 2:
                mc = C_M1 if kb == nkb - 2 else C_M2
                P.c("pool", lambda e: e.tensor_tensor(out=pt[:], in0=pt[:], in1=cb[:, mc:mc + 512], op=ALU.mult), [bpt], [bpt])
            st[u] = (pt, bpt, vh, bvh)

        def stage2(u):
            h, qi, kb, nkb = units[u]
            pt, bpt, vh, bvh = st.pop(u)
            first, last = kb == 0, kb == nkb - 1

            def mm2(e):
                for cpt in range(2):
                    for dvc in range(2):
                        e.matmul(bU[cpt][0][:, dvc * 256:dvc * 256 + 256], vh[:, kb, dvc * 128:dvc * 128 + 128],
                                 pt[:, cpt * 256:cpt * 256 + 256], start=(first and dvc == 0), stop=last, skip_group_check=True)
                for cpt in range(2):
                    ins = e.matmul(bD[0][:, cpt * 256:cpt * 256 + 256], flag if kb < 16 else ones,
                                   pt[:, cpt * 256:cpt * 256 + 256], start=(first and cpt == 0), stop=last, skip_group_check=True)
                return ins
            P.c("pe", mm2, [bpt, bvh], [bU[0][1], bU[1][1], bD[1]])
            if not last:
                return
            P.c("dve", lambda e: e.reciprocal(out=rd[:], in_=bD[0][:]), [bD[1]], [brd])
            for dvc in range(2):
                ds = slice(dvc * 256, dvc * 256 + 256)
                P.c("dve", lambda e, ds=ds: e.tensor_tensor(out=o1[:, ds], in0=bU[0][0][:, ds], in1=rd[:, 0:256], op=ALU.mult),
                    [bU[0][1], brd], [bo1])
                P.c("dve", lambda e, ds=ds: e.tensor_tensor(out=o2[:, ds], in0=bU[1][0][:, ds], in1=rd[:, 256:512], op=ALU.mult),
                    [bU[1][1], brd], [bo2])
            P.c("dve", lambda e: e.scalar_tensor_tensor(out=o1[:], in0=o2[:], scalar=nlam[:, 0:1], in1=o1[:], op0=ALU.mult, op1=ALU.add),
                [bo1, bo2], [bo1])
            P.c("act", lambda e: e.activation(out=sqt[:], in_=o1[:], func=AF.Square), [bo1], [bsq])

            def mm3(e):
                e.matmul(bN[0][:, 0:256], cb[:, C_INV256:C_INV256 + 128], sqt[:, 0:256], start=True, stop=False)
                return e.matmul(bN[0][:, 0:256], cb[:, C_INV256:C_INV256 + 128], sqt[:, 256:512], start=False, stop=True)
            P.c("pe", mm3, [bsq], [bN[1]])
            P.c("act", lambda e: e.activation(out=rst[:], in_=bN[0][:, 0:256], func=AF.Sqrt, bias=K["cf"][:, F_EPS:F_EPS + 1], scale=1.0),
                [bN[1]], [brs])
            P.c("dve", lambda e: e.reciprocal(out=rst[:], in_=rst[:]), [brs], [brs])
            ot, bot = of.next()
            for dvc in range(2):
                ds = slice(dvc * 256, dvc * 256 + 256)
                P.c("dve", lambda e, ds=ds, dvc=dvc: e.scalar_tensor_tensor(out=ot[:, ds], in0=o1[:, ds], scalar=gs[:, dvc:dvc + 1],
                                                                           in1=rst[:], op0=ALU.mult, op1=ALU.mult), [bo1, brs], [bot])
            for dvc in range(2):
                P.dma("sp", S["Of"][h * 2 + dvc, :, qi * 256:qi * 256 + 256], ot[:, dvc * 256:dvc * 256 + 256], bot, False)

        n = len(units)
        for u in range(min(LOOK, n)):
            stage1(u)
        for u in range(n):
            if u + LOOK < n:
                stage1(u + LOOK)
            stage2(u)
        P.run()


def out_phase(nc, ps, K, S, WA, WB, WO):
    with ExitStack() as es:
        reset_bufs(ps)
        P = Prog(nc, es, "op")
        wa, bwa = P.sb("wa", [128, 4, 2048], BF16)
        wb, bwb = P.sb("wb", [128, 8, 2048], BF16)
        wo, bwo = P.sb("wo", [128, 16, 2048], BF16)
        odt = RR([P.sb(f"od{i}", [128, 4, 512], BF16) for i in range(1)])
        oft = RR([P.sb(f"of{i}", [128, 8, 512], BF16) for i in range(1)])
        yT = RR([P.sb(f"y{i}", [128, 16, 512], BF16) for i in range(1)])
        sgd = RR([P.sb(f"sgd{i}", [128, 512], F32) for i in range(2)])
        sgf = RR([P.sb(f"sgf{i}", [128, 512], F32) for i in range(2)])
        ta = RR([P.sb(f"ta{i}", [128, 512], F32) for i in range(2)])
        tb_ = RR([P.sb(f"tb{i}", [128, 512], F32) for i in range(2)])
        xin = RR([P.sb(f"xin{i}", [128, 512], F32) for i in range(2)])
        xout = RR([P.sb(f"xo{i}", [128, 512], F32) for i in range(2)])
        P.dma("pool", wa[:], WA.rearrange("(k p) n -> p k n", p=128), bwa, True)
        for i in range(2):
            P.dma("pool", wb[:, i * 4:i * 4 + 4, :], WB.rearrange("(k p) n -> p k n", p=128)[:, i * 4:i * 4 + 4, :], bwb, True)
        for i in range(4):
            P.dma("pool", wo[:, i * 4:i * 4 + 4, :], WO.rearrange("(k p) n -> p k n", p=128)[:, i * 4:i * 4 + 4, :], bwo, True)
        u = 0
        for ti in range(4):
            ts = slice(ti * 512, ti * 512 + 512)
            od, bod = odt.next()
            of, bof = oft.next()
            y, by = yT.next()
            for k in range(4):
                P.dma("sp", od[:, k, :], S["Od"][k, :, ts], bod, True)
            for k in range(8):
                P.dma("sp", of[:, k, :], S["Of"][k, :, ts], bof, True)
            for dc in range(16):
                bA, bB = ps[(u % 2) * 2], ps[(u % 2) * 2 + 1]
                u += 1
                cs = slice(dc * 128, dc * 128 + 128)

                def mm(e, od=od, of=of, cs=cs, bA=bA, bB=bB):
                    for k in range(4):
                        e.matmul(bA[0][:], wa[:, k, cs], od[:, k, :], start=(k == 0), stop=(k == 3))
                    for k in range(8):
                        ins = e.matmul(bB[0][:], wb[:, k, cs], of[:, k, :], start=(k == 0), stop=(k == 7))
                    return ins
                P.c("pe", mm, [bwa, bwb, bod, bof], [bA[1], bB[1]])
                g1, bg1 = sgd.next()
                g2, bg2 = sgf.next()
                a_, ba_ = ta.next()
                b_, bb_ = tb_.next()
                P.dma("sp", g1[:], S["SG"][dc, :, ts], bg1, True)
                P.dma("sp", g2[:], S["SG"][16 + dc, :, ts], bg2, True)
                P.c("dve", lambda e, a_=a_, g1=g1, bA=bA: e.tensor_tensor(out=a_[:], in0=g1[:], in1=bA[0][:], op=ALU.mult), [bg1, bA[1]], [ba_])
                P.c("dve", lambda e, b_=b_, g2=g2, bB=bB: e.tensor_tensor(out=b_[:], in0=g2[:], in1=bB[0][:], op=ALU.mult), [bg2, bB[1]], [bb_])
                P.c("dve", lambda e, y=y, dc=dc, a_=a_, b_=b_: e.tensor_tensor(out=y[:, dc, :], in0=a_[:], in1=b_[:], op=ALU.add),
                    [ba_, bb_], [by])
            for dc2 in range(16):
                bC_ = ps[4 + dc2 % 4]
                cs = slice(dc2 * 128, dc2 * 128 + 128)

                def mm(e, y=y, cs=cs, bC_=bC_):
                    for k in range(16):
                        ins = e.matmul(bC_[0][:], wo[:, k, cs], y[:, k, :], start=(k == 0), stop=(k == 15))
                    return ins
                P.c("pe", mm, [bwo, by], [bC_[1]])
                xt, bx = xin.next()
                xo, bxo = xout.next()
                P.dma("sp", xt[:], S["x1T"][dc2, :, 2048 + ti * 512:2048 + ti * 512 + 512], bx, True)
                P.c("dve", lambda e, xo=xo, xt=xt, bC_=bC_: e.tensor_tensor(out=xo[:], in0=bC_[0][:], in1=xt[:], op=ALU.add), [bC_[1], bx], [bxo])
                P.dma("sp", S["x2T"][dc2, :, ts], xo[:], bxo, False)
        P.run()


IN_SHAPES = {
    "xT": ([16, 128, TFR], F32), "pos": ([32, TFR], I32), "cbf": ([128, NCB], F32), "cf32": ([128, NCF], F32),
    "flag": ([128, 128], F32), "lamv": ([128, 4, 128], F32),
    "ffn1_wg": ([D, DFF], F32), "ffn1_wu": ([D, DFF], F32), "ffn1_wd": ([DFF, D], F32),
    "ffn2_wg": ([D, DFF], F32), "ffn2_wu": ([D, DFF], F32), "ffn2_wd": ([DFF, D], F32),
    "w_in": ([D, DIN], F32), "w_a": ([512, D], F32), "w_b": ([1024, D], F32), "w_o": ([D, D], F32),
}


class LazyIns(dict):
    def __init__(self, nc):
        super().__init__()
        self.nc = nc

    def __missing__(self, name):
        shape, dt = IN_SHAPES[name]
        ap = self.nc.dram_tensor(name, shape, dt, kind="ExternalInput").ap()
        self[name] = ap
        return ap


def build(debug=False, upto=7, dump=()):
    nc = bass.Bass("TRN2", target_bir_lowering=False)
    ins = LazyIns(nc)
    kind = "ExternalOutput" if debug else "Internal"
    S = {}

    def scr(name, shape, dt):
        S[name] = nc.dram_tensor(name, shape, dt, kind=("ExternalOutput" if name in dump else "Internal")).ap()
    scr("x1T", [16, 128, TFR], F32)
    scr("Qd", [12, 128, TOWN], BF16)
    scr("Kd", [12, 128, TFR], BF16)
    scr("Vd", [TFR, 1536], BF16)
    scr("Qf", [8, 128, TOWN], BF16)
    scr("Kf", [8, 128, TFR], BF16)
    scr("Vf", [TFR, 1024], BF16)
    scr("SG", [32, 128, TOWN], F32)
    scr("Od", [4, 128, TOWN], BF16)
    scr("Of", [8, 128, TOWN], BF16)
    scr("x2T", [16, 128, TOWN], F32)
    outT = nc.dram_tensor("outT", [16, 128, TOWN], F32, kind="ExternalOutput").ap()
    with ExitStack() as es:
        ps = []
        for i in range(8):
            t = es.enter_context(nc.psum_tensor(f"ps{i}", [128, 512], F32))
            ps.append((t, Buf(f"ps{i}")))
        K = {}
        K["cb"] = es.enter_context(nc.sbuf_tensor("K_cb", [128, NCB], BF16))
        K["cf"] = es.enter_context(nc.sbuf_tensor("K_cf", [128, NCF], F32))
        K["flag"] = es.enter_context(nc.sbuf_tensor("K_flag", [128, 128], BF16))
        K["nlam"] = es.enter_context(nc.sbuf_tensor("K_nlam", [128, 1], F32))
        K["gs"] = es.enter_context(nc.sbuf_tensor("K_gs", [128, 2], F32))
        setup_phase(nc, K, ins)
        if upto >= 2:
            ffn_phase(nc, ps, K, ins["xT"], S["x1T"], [0, 1024, 2048, 3072], F_G1, ins["ffn1_wg"], ins["ffn1_wu"], ins["ffn1_wd"], "f1")
        if upto >= 3:
            proj_phase(nc, ps, K, S["x1T"], ins["w_in"], ins["pos"], S)
        if upto >= 4:
            dil_phase(nc, ps, K, S)
        if upto >= 5:
            diff_phase(nc, ps, K, S)
        if upto >= 6:
            out_phase(nc, ps, K, S, ins["w_a"], ins["w_b"], ins["w_o"])
        if upto >= 7:
            ffn_phase(nc, ps, K, S["x2T"], outT, [0, 1024], F_G2, ins["ffn2_wg"], ins["ffn2_wu"], ins["ffn2_wd"], "f2")
    return nc, list(ins.keys())


def host_consts():
    cb = np.zeros((128, NCB), np.float32)
    cb[:, C_INV2048:C_INV2048 + 128] = 1.0 / 2048
    cb[:, C_INV128:C_INV128 + 128] = 1.0 / 128
    cb[:, C_INV256:C_INV256 + 128] = 1.0 / 256
    cb[:, C_ONES:C_ONES + 128] = 1.0
    for m in range(16):
        cb[m + 16, C_PSW + m] = 1.0
        cb[m, C_PSW + m + 16] = 1.0
    ik = np.arange(128)[:, None]
    iq = np.arange(128)[None, :]
    cb[:, C_MDIL:C_MDIL + 128] = (ik >= iq)
    cb[:, C_MDIL + 128:C_MDIL + 256] = (ik <= iq)
    tri = (ik <= iq).astype(np.float32)
    m1 = np.concatenate([tri, np.ones((128, 128), np.float32)], 1)
    m2 = np.concatenate([np.zeros((128, 128), np.float32), tri], 1)
    cb[:, C_M1:C_M1 + 512] = np.concatenate([m1, m1], 1)
    cb[:, C_M2:C_M2 + 512] = np.concatenate([m2, m2], 1)
    return cb


def make_in_maps(x, positions, ffn1_norm, ffn1_w_gate, ffn1_w_up, ffn1_w_down, mix_norm, w_in,
                 dil_q_norm, dil_k_norm, diff_q_norm, diff_k_norm, diff_lq1, diff_lk1, diff_lq2, diff_lk2,
                 diff_subln, w_dil_branch, w_diff_branch, w_out, ffn2_norm, ffn2_w_gate, ffn2_w_up, ffn2_w_down):
    f = lambda a: np.ascontiguousarray(np.asarray(a, dtype=np.float32))
    x = np.asarray(x, np.float32)
    positions = np.asarray(positions, np.int32)
    cb = host_consts()
    cf = np.zeros((128, NCF), np.float32)
    cf[:, F_G1:F_G1 + 16] = f(ffn1_norm)[0].reshape(16, 128).T
    cf[:, F_GM:F_GM + 16] = f(mix_norm)[0].reshape(16, 128).T
    cf[:, F_G2:F_G2 + 16] = f(ffn2_norm)[0].reshape(16, 128).T
    cf[:, F_HG + 0] = f(dil_q_norm)[0]
    cf[:, F_HG + 1] = f(dil_k_norm)[0]
    cf[:, F_HG + 2] = f(diff_q_norm)[0]
    cf[:, F_HG + 3] = f(diff_k_norm)[0]
    cf[:, F_SUB:F_SUB + 2] = f(diff_subln)[0].reshape(2, 128).T
    inv = (np.float32(500000.0) ** (-np.arange(0, 32, 2, dtype=np.float32) / np.float32(32))).astype(np.float32)
    cf[0:16, F_INV] = inv
    cf[16:32, F_INV] = inv
    cf[0:16, F_SGN] = -1.0
    cf[16:32, F_SGN] = 1.0
    cf[:, F_EPS] = EPS
    lamv = np.stack([f(diff_lq1)[0], f(diff_lk1)[0], f(diff_lq2)[0], f(diff_lk2)[0]], 0)
    lamv = np.ascontiguousarray(np.broadcast_to(lamv[None], (128, 4, 128)))
    shared = {
        "cbf": cb, "cf32": cf, "lamv": lamv,
        "ffn1_wg": f(ffn1_w_gate)[0], "ffn1_wu": f(ffn1_w_up)[0], "ffn1_wd": f(ffn1_w_down)[0],
        "ffn2_wg": f(ffn2_w_gate)[0], "ffn2_wu": f(ffn2_w_up)[0], "ffn2_wd": f(ffn2_w_down)[0],
        "w_in": f(w_in)[0], "w_a": f(w_dil_branch)[0], "w_b": f(w_diff_branch)[0], "w_o": f(w_out)[0],
    }
    maps = []
    for core in range(8):
        b, half = core // 2, core % 2
        xT = np.zeros((D, TFR), np.float32)
        pos = np.zeros((TFR,), np.int32)
        xT[:, 2048:] = x[b, half * 2048:(half + 1) * 2048, :].T
        pos[2048:] = positions[b, half * 2048:(half + 1) * 2048]
        if half == 1:
            xT[:, :2048] = x[b, 0:2048, :].T
            pos[:2048] = positions[b, 0:2048]
        m = dict(shared)
        m["xT"] = np.ascontiguousarray(xT.reshape(16, 128, TFR))
        m["pos"] = np.ascontiguousarray(np.broadcast_to(pos[None], (32, TFR)))
        m["flag"] = np.full((128, 128), float(half), np.float32)
        maps.append(m)
    return maps


def kernel(**inputs):
    maps = make_in_maps(**inputs)
    nc, names = build()
    maps = [{k: m[k] for k in names} for m in maps]
    res = run_bass_kernel_spmd(nc, maps, core_ids=list(range(8)))
    out = np.empty((4, 4096, D), np.float32)
    for core in range(8):
        b, half = core // 2, core % 2
        oT = np.asarray(res.results[core]["outT"], np.float32).reshape(D, TOWN)
        out[b, half * 2048:(half + 1) * 2048, :] = oT.T
    return out
```

```python
import math
from contextlib import ExitStack

import numpy as np
import concourse.bass as bass
import concourse.mybir as mybir
from concourse.bass_utils import run_bass_kernel_spmd

F32 = mybir.dt.float32
BF16 = mybir.dt.bfloat16
I32 = mybir.dt.int32
AF = mybir.ActivationFunctionType
ALU = mybir.AluOpType
AX = mybir.AxisListType

D = 2048
DFF = 5632
DIN = 11776
TOWN = 2048
TFR = 4096
EPS = 1e-6
SCALE = 128.0 ** -0.5
PI = math.pi
ENGS = ("pe", "act", "dve", "pool", "sp")

C_INV2048, C_INV128, C_INV256, C_ONES, C_PSW, C_MDIL, C_M1, C_M2 = 0, 128, 256, 384, 512, 640, 896, 1408
NCB = 1920
F_G1, F_GM, F_G2, F_HG, F_SUB, F_INV, F_SGN, F_EPS = 0, 16, 32, 48, 52, 54, 55, 56
NCF = 57


class Op:
    __slots__ = ("eng", "fn", "deps", "kind", "ms", "val", "sem")

    def __init__(self, eng, fn, kind):
        self.eng, self.fn, self.kind = eng, fn, kind
        self.deps = []
        self.ms = False
        self.val = 0
        self.sem = None


class Buf:
    def __init__(self, name):
        self.name = name
        self.writes = {}
        self.reads = {}
        self.dsem = None
        self.dcnt = 0


class Prog:
    def __init__(self, nc, es, tag):
        self.nc, self.es, self.tag = nc, es, tag
        self.ops = {e: [] for e in ENGS}
        self.allsems = []
        self.csem = {e: self._sem(f"{tag}_{e}") for e in ENGS[:4]}
        self.dma_ops = []
        self.nb = 0

    def _sem(self, name):
        h = self.nc.alloc_semaphore(name=name)
        self.allsems.append(h)
        return h

    def sb(self, name, shape, dt):
        t = self.es.enter_context(self.nc.sbuf_tensor(f"{self.tag}_{name}", shape, dt))
        return t, Buf(name)

    def buf(self, name="b"):
        return Buf(name)

    @staticmethod
    def _key(op):
        return op.eng if op.kind == "c" else ("d", id(op.sem))

    def rec(self, eng, fn, reads=(), writes=(), kind="c", extra=()):
        op = Op(eng, fn, kind)
        deps = {}
        for b in reads:
            for k, d in b.writes.items():
                deps[id(d)] = d
        for b in writes:
            if b.reads:
                for k, d in b.reads.items():
                    deps[id(d)] = d
            else:
                for k, d in b.writes.items():
                    if not (d.kind == "c" and d.eng == eng):
                        deps[id(d)] = d
        for d in extra:
            deps[id(d)] = d
        if eng == "pe":
            deps = {i: d for i, d in deps.items() if not (d.kind == "c" and d.eng == "pe")}
        op.deps = list(deps.values())
        for d in op.deps:
            if d.kind == "c":
                d.ms = True
        self.ops[eng].append(op)
        return op

    def _commit(self, op, reads, writes):
        k = self._key(op)
        for b in reads:
            b.reads[k] = op
        for b in writes:
            if b.reads:
                b.reads = {}
                b.writes = {}
            b.writes[k] = op

    def c(self, eng, fn, reads=(), writes=()):
        op = self.rec(eng, fn, reads, writes)
        self._commit(op, reads, writes)
        return op

    def dma(self, q, out, in_, sbuf, load, reads=(), writes=()):
        if sbuf.dsem is None:
            self.nb += 1
            sbuf.dsem = self._sem(f"{self.tag}_d{self.nb}")
        r = list(reads) + ([] if load else [sbuf])
        w = list(writes) + ([sbuf] if load else [])
        op = self.rec(q, lambda e: e.dma_start(out=out, in_=in_), r, w, kind="d")
        sbuf.dcnt += 16
        op.sem = sbuf.dsem
        op.val = sbuf.dcnt
        self._commit(op, r, w)
        self.dma_ops.append(op)
        return op

    def run(self):
        for q in ("sp", "pool"):
            op = Op(q, None, "c")
            last = {}
            for d in self.dma_ops:
                last[id(d.sem)] = d
            op.deps = list(last.values())
            self.ops[q].append(op)
        for e in ENGS:
            cnt = 0
            for op in self.ops[e]:
                if op.kind == "c" and op.ms:
                    cnt += 1
                    op.val = cnt
        csem = self.csem

        def mk(e):
            ops = self.ops[e]

            def body(eng):
                waited = {}
                for op in ops:
                    for d in op.deps:
                        sem = d.sem if d.kind == "d" else csem[d.eng]
                        if waited.get(id(sem), 0) < d.val:
                            eng.wait_ge(sem, d.val)
                            waited[id(sem)] = d.val
                    if op.fn is not None:
                        ins = op.fn(eng)
                        if op.kind == "d":
                            ins.then_inc(op.sem, 16)
                        elif op.ms:
                            ins.then_inc(csem[e], 1)
            return body

        with self.nc.Block() as block:
            block.tensor(mk("pe"))
            block.scalar(mk("act"))
            block.vector(mk("dve"))
            block.gpsimd(mk("pool"))
            block.sync(mk("sp"))
        self.nc.clear_and_free_semaphores(self.allsems)
        self.nc.all_engine_barrier()


def reset_bufs(ps):
    for _, b in ps:
        b.reads, b.writes = {}, {}


class RR:
    def __init__(self, items):
        self.items, self.i = items, 0

    def next(self):
        x = self.items[self.i % len(self.items)]
        self.i += 1
        return x


def setup_phase(nc, K, ins):
    with ExitStack() as es:
        P = Prog(nc, es, "su")
        bcb, bcf, bfl, blam = Buf("cb"), Buf("cf"), Buf("fl"), Buf("lam")
        P.dma("pool", K["cb"][:], ins["cbf"], bcb, True)
        P.dma("pool", K["flag"][:], ins["flag"], bfl, True)
        P.dma("sp", K["cf"][:], ins["cf32"], bcf, True)
        lamt, blt = P.sb("lamt", [128, 4, 128], F32)
        pr, bpr = P.sb("pr", [128, 2, 128], F32)
        ss, bss = P.sb("ss", [128, 2], F32)
        ee, bee = P.sb("ee", [128, 2], F32)
        P.dma("sp", lamt[:], ins["lamv"], blt, True)
        P.c("dve", lambda e: e.tensor_tensor(out=pr[:, 0, :], in0=lamt[:, 0, :], in1=lamt[:, 1, :], op=ALU.mult), [blt], [bpr])
        P.c("dve", lambda e: e.tensor_tensor(out=pr[:, 1, :], in0=lamt[:, 2, :], in1=lamt[:, 3, :], op=ALU.mult), [blt], [bpr])
        P.c("dve", lambda e: e.reduce_sum(out=ss[:, 0:1], in_=pr[:, 0, :], axis=AX.X), [bpr], [bss])
        P.c("dve", lambda e: e.reduce_sum(out=ss[:, 1:2], in_=pr[:, 1, :], axis=AX.X), [bpr], [bss])
        P.c("act", lambda e: e.activation(out=ee[:], in_=ss[:], func=AF.Exp), [bss], [bee])
        P.c("dve", lambda e: e.tensor_tensor(out=K["nlam"][:], in0=ee[:, 1:2], in1=ee[:, 0:1], op=ALU.subtract), [bee], [blam])
        P.c("dve", lambda e: e.tensor_scalar_add(out=K["nlam"][:], in0=K["nlam"][:], scalar1=-0.2), [blam], [blam])
        P.c("dve", lambda e: e.tensor_scalar_mul(out=K["gs"][:], in0=K["cf"][:, F_SUB:F_SUB + 2], scalar1=0.8), [bcf], [blam])
        P.run()


def norm_prologue(P, ps, K, src, t0, gcol, hT, bhT, xin, sq, rstd, brstd):
    cb = K["cb"]
    for half in range(2):
        for c in range(16):
            xt, bx = xin.next()
            st, bs = sq.next()
            P.dma("sp", xt[:], src[c, :, t0 + half * 512:t0 + half * 512 + 512], bx, True)
            P.c("act", lambda e, st=st, xt=xt: e.activation(out=st[:], in_=xt[:], func=AF.Square), [bx], [bs])
            P.c("pe", lambda e, st=st, c=c, half=half: e.matmul(ps[6 + half][0][:], cb[:, C_INV2048:C_INV2048 + 128], st[:],
                                                                start=(c == 0), stop=(c == 15)), [bs], [ps[6 + half][1]])
        P.c("act", lambda e, half=half: e.activation(out=rstd[:, half * 512:half * 512 + 512], in_=ps[6 + half][0][:], func=AF.Sqrt,
                                                     bias=K["cf"][:, F_EPS:F_EPS + 1], scale=1.0), [ps[6 + half][1]], [brstd])
        P.c("dve", lambda e, half=half: e.reciprocal(out=rstd[:, half * 512:half * 512 + 512], in_=rstd[:, half * 512:half * 512 + 512]),
            [brstd], [brstd])
    for half in range(2):
        for c in range(16):
            xt, bx = xin.next()
            P.dma("sp", xt[:], src[c, :, t0 + half * 512:t0 + half * 512 + 512], bx, True)
            P.c("dve", lambda e, xt=xt, c=c, half=half: e.scalar_tensor_tensor(
                out=hT[:, c, half * 512:half * 512 + 512], in0=xt[:], scalar=K["cf"][:, gcol + c:gcol + c + 1],
                in1=rstd[:, half * 512:half * 512 + 512], op0=ALU.mult, op1=ALU.mult), [bx, brstd], [bhT])


def ffn_phase(nc, ps, K, src, dst, t0s, gcol, Wg, Wu, Wd, tag):
    with ExitStack() as es:
        reset_bufs(ps)
        P = Prog(nc, es, tag)
        hT, bhT = P.sb("hT", [128, 16, 1024], BF16)
        aT, baT = P.sb("aT", [128, 44, 1024], BF16)
        slots = RR([P.sb(f"w{i}", [128, 8192], BF16) for i in range(2)])
        xin = RR([P.sb(f"xin{i}", [128, 512], F32) for i in range(3)])
        sq = RR([P.sb(f"sq{i}", [128, 512], BF16) for i in range(2)])
        sil = RR([P.sb(f"sil{i}", [128, 512], F32) for i in range(2)])
        xout = RR([P.sb(f"xo{i}", [128, 512], F32) for i in range(2)])
        rstd, brstd = P.sb("rstd", [128, 1024], F32)
        Wgv = Wg.rearrange("(k p) n -> p k n", p=128)
        Wuv = Wu.rearrange("(k p) n -> p k n", p=128)
        Wdv = Wd.rearrange("(f p) n -> p f n", p=128)
        for it, t0 in enumerate(t0s):
            if it == 0:
                norm_prologue(P, ps, K, src, t0, gcol, hT, bhT, xin, sq, rstd, brstd)
            u = 0
            for p in range(22):
                sl, bsl = slots.next()
                gv = sl[:, 0:4096].rearrange("p (k n) -> p k n", k=16)
                uv = sl[:, 4096:8192].rearrange("p (k n) -> p k n", k=16)
                P.dma("pool", gv, Wgv[:, :, p * 256:p * 256 + 256], bsl, True)
                P.dma("pool", uv, Wuv[:, :, p * 256:p * 256 + 256], bsl, True)
                for j in range(2):
                    for half in range(2):
                        bg, bu = ps[(u % 2) * 2], ps[(u % 2) * 2 + 1]
                        u += 1

                        def mm(e, gv=gv, uv=uv, j=j, half=half, bg=bg, bu=bu):
                            for k in range(16):
                                e.matmul(bg[0][:], gv[:, k, j * 128:j * 128 + 128], hT[:, k, half * 512:half * 512 + 512],
                                         start=(k == 0), stop=(k == 15))
                            for k in range(16):
                                ins = e.matmul(bu[0][:], uv[:, k, j * 128:j * 128 + 128], hT[:, k, half * 512:half * 512 + 512],
                                               start=(k == 0), stop=(k == 15))
                            return ins
                        P.c("pe", mm, [bsl, bhT], [bg[1], bu[1]])
                        st, bst = sil.next()
                        f = p * 2 + j
                        P.c("act", lambda e, st=st, bg=bg: e.activation(out=st[:], in_=bg[0][:], func=AF.Silu), [bg[1]], [bst])
                        P.c("dve", lambda e, st=st, bu=bu, f=f, half=half: e.tensor_tensor(
                            out=aT[:, f, half * 512:half * 512 + 512], in0=st[:], in1=bu[0][:], op=ALU.mult), [bst, bu[1]], [baT])
            if it + 1 < len(t0s):
                norm_prologue(P, ps, K, src, t0s[it + 1], gcol, hT, bhT, xin, sq, rstd, brstd)
            for dp in range(8):
                base = 4 if dp % 2 == 0 else 0
                for fh in range(2):
                    sl, bsl = slots.next()
                    dv = sl[:, 0:5632].rearrange("p (f n) -> p f n", f=22)
                    P.dma("pool", dv, Wdv[:, fh * 22:fh * 22 + 22, dp * 256:dp * 256 + 256], bsl, True)
                    for j in range(2):
                        for half in range(2):
                            bk = ps[base + j * 2 + half]

                            def mm(e, dv=dv, j=j, half=half, bk=bk, fh=fh):
                                for f in range(22):
                                    ins = e.matmul(bk[0][:], dv[:, f, j * 128:j * 128 + 128],
                                                   aT[:, fh * 22 + f, half * 512:half * 512 + 512],
                                                   start=(fh == 0 and f == 0), stop=(fh == 1 and f == 21))
                                return ins
                            P.c("pe", mm, [bsl, baT], [bk[1]])
                for j in range(2):
                    for half in range(2):
                        bk = ps[base + j * 2 + half]
                        c = dp * 2 + j
                        xt, bx = xin.next()
                        xo, bxo = xout.next()
                        tt = t0 + half * 512
                        P.dma("sp", xt[:], src[c, :, tt:tt + 512], bx, True)
                        P.c("dve", lambda e, xo=xo, xt=xt, bk=bk: e.scalar_tensor_tensor(
                            out=xo[:], in0=bk[0][:], scalar=0.5, in1=xt[:], op0=ALU.mult, op1=ALU.add), [bk[1], bx], [bxo])
                        P.dma("sp", dst[c, :, tt:tt + 512], xo[:], bxo, False)
        P.run()


def proj_phase(nc, ps, K, x1T, win, posd, S, tiles=(0, 1, 2, 3), pfilter=None):
    with ExitStack() as es:
        reset_bufs(ps)
        P = Prog(nc, es, "pj")
        cb, cf = K["cb"], K["cf"]
        hT, bhT = P.sb("hT", [128, 16, 1024], BF16)
        slots = RR([P.sb(f"w{i}", [128, 8192], BF16) for i in range(3)])
        xin = RR([P.sb(f"xin{i}", [128, 512], F32) for i in range(3)])
        sq = RR([P.sb(f"sq{i}", [128, 512], BF16) for i in range(3)])
        qg = RR([P.sb(f"qg{i}", [128, 512], BF16) for i in range(3)])
        rs = RR([P.sb(f"rs{i}", [128, 512], F32) for i in range(3)])
        qo = RR([P.sb(f"qo{i}", [128, 512], BF16) for i in range(3)])
        t1 = RR([P.sb(f"t1{i}", [32, 512], F32) for i in range(3)])
        t2 = RR([P.sb(f"t2{i}", [32, 512], F32) for i in range(3)])
        vst = RR([P.sb(f"vs{i}", [128, 512], BF16) for i in range(3)])
        gst = RR([P.sb(f"gs{i}", [128, 512], F32) for i in range(3)])
        rstd, brstd = P.sb("rstd", [128, 1024], F32)
        posi, bposi = P.sb("posi", [32, 1024], I32)
        ang, bang = P.sb("ang", [32, 1024], F32)
        a1, ba1 = P.sb("a1", [32, 1024], F32)
        ki, bki = P.sb("ki", [32, 1024], I32)
        kf, bkf = P.sb("kf", [32, 1024], F32)
        Ct, bC = P.sb("C", [32, 1024], F32)
        St, bS = P.sb("S", [32, 1024], F32)
        npi, bnpi = P.sb("npi", [32, 1], F32)
        P.c("dve", lambda e: e.memset(npi[:], -PI), [], [bnpi])
        Wv = win.rearrange("(k p) n -> p k n", p=128)
        panels = []
        for i in range(3):
            panels.append((i * 512, "qk", ("Qd", i * 4, 0)))
        for i in range(3):
            panels.append((1536 + i * 512, "qk", ("Kd", i * 4, 1)))
        for i in range(3):
            panels.append((3072 + i * 512, "v", ("Vd", i * 512)))
        for i in range(2):
            panels.append((4608 + i * 512, "qk", ("Qf", i * 4, 2)))
        for i in range(2):
            panels.append((5632 + i * 512, "qk", ("Kf", i * 4, 3)))
        for i in range(2):
            panels.append((6656 + i * 512, "v", ("Vf", i * 512)))
        for i in range(8):
            panels.append((7680 + i * 512, "g", (i * 4,)))
        u = 0
        if pfilter is not None:
            panels = [p for i, p in enumerate(panels) if i in pfilter]
        for ti in tiles:
            t0 = ti * 1024
            own = ti >= 2
            norm_prologue(P, ps, K, x1T, t0, F_GM, hT, bhT, xin, sq, rstd, brstd)
            P.dma("sp", posi[:], posd[:, t0:t0 + 1024], bposi, True)
            P.c("dve", lambda e: e.tensor_copy(out=ang[:], in_=posi[:]), [bposi], [bang])
            P.c("dve", lambda e: e.tensor_scalar_mul(out=ang[:], in0=ang[:], scalar1=cf[0:32, F_INV:F_INV + 1]), [bang], [bang])
            for (off, Tt, bT) in ((0.75, Ct, bC), (0.5, St, bS)):
                P.c("dve", lambda e, off=off: e.tensor_scalar(out=a1[:], in0=ang[:], scalar1=1.0 / (2 * PI), scalar2=off, op0=ALU.mult,
                                                              op1=ALU.add), [bang], [ba1])
                P.c("dve", lambda e: e.tensor_copy(out=ki[:], in_=a1[:]), [ba1], [bki])
                P.c("dve", lambda e: e.tensor_copy(out=kf[:], in_=ki[:]), [bki], [bkf])
                P.c("dve", lambda e: e.tensor_tensor(out=a1[:], in0=a1[:], in1=kf[:], op=ALU.subtract), [ba1, bkf], [ba1])
                P.c("dve", lambda e: e.scalar_tensor_tensor(out=a1[:], in0=a1[:], scalar=0.0, in1=a1[:], op0=ALU.is_lt, op1=ALU.add),
                    [ba1], [ba1])
                P.c("act", lambda e, Tt=Tt: e.activation(out=Tt[:], in_=a1[:], func=AF.Sin, bias=npi[:], scale=2 * PI), [ba1, bnpi], [bT])
            P.c("dve", lambda e: e.tensor_scalar_mul(out=St[:], in0=St[:], scalar1=cf[0:32, F_SGN:F_SGN + 1]), [bS], [bS])
            tile_units = []
            for (c0, kind, info) in panels:
                if not own and (kind == "g" or info[0] in ("Qd", "Qf")):
                    continue
                grp = (info[1] // 4 if kind == "qk" else info[1] // 512) if info[0] in ("Kd", "Vd") else 2
                first_used = True
                if kind == "v":
                    for tb in range(8):
                        if not own and grp < 2 and (ti == 0 or tb < (7 if grp == 0 else 4)):
                            continue
                        tile_units.append((c0, kind, info, tb, 0, first_used))
                        first_used = False
                else:
                    for j in range(4):
                        for half in range(2):
                            if not own and grp < 2 and (ti == 0 or half == 0):
                                continue
                            tile_units.append((c0, kind, info, j, half, first_used))
                            first_used = False
            stB = {}

            def stageA(idx, t0=t0):
                nonlocal u
                c0, kind, info, j, half, firstu = tile_units[idx]
                if firstu:
                    sl, bsl = slots.next()
                    wv = sl[:, :].rearrange("p (k n) -> p k n", k=16)
                    P.dma("pool", wv, Wv[:, :, c0:c0 + 512], bsl, True)
                    stageA.cur = (wv, bsl)
                wv, bsl = stageA.cur
                bk = ps[u % 4]
                u += 1
                if kind == "v":
                    tb = j

                    def mm(e):
                        for k in range(16):
                            ins = e.matmul(bk[0][:], hT[:, k, tb * 128:tb * 128 + 128], wv[:, k, :], start=(k == 0), stop=(k == 15))
                        return ins
                    P.c("pe", mm, [bsl, bhT], [bk[1]])
                    vt, bvt = vst.next()
                    P.c("act", lambda e: e.activation(out=vt[:], in_=bk[0][:], func=AF.Copy), [bk[1]], [bvt])
                    P.dma("sp", S[info[0]][t0 + tb * 128:t0 + tb * 128 + 128, info[1]:info[1] + 512], vt[:], bvt, False)
                    return
                hs = slice(half * 512, half * 512 + 512)

                def mm(e):
                    for k in range(16):
                        ins = e.matmul(bk[0][:], wv[:, k, j * 128:j * 128 + 128], hT[:, k, hs], start=(k == 0), stop=(k == 15))
                    return ins
                P.c("pe", mm, [bsl, bhT], [bk[1]])
                if kind == "g":
                    gt, bgt = gst.next()
                    P.c("act", lambda e: e.activation(out=gt[:], in_=bk[0][:], func=AF.Sigmoid), [bk[1]], [bgt])
                    tt = t0 - 2048 + half * 512
                    P.dma("sp", S["SG"][info[0] + j, :, tt:tt + 512], gt[:], bgt, False)
                    return
                name, ch0, gi = info
                st, bs = sq.next()
                qt, bq = qg.next()
                P.c("act", lambda e: e.activation(out=st[:], in_=bk[0][:], func=AF.Square), [bk[1]], [bs])
                P.c("act", lambda e: e.activation(out=qt[:], in_=bk[0][:], func=AF.Copy, scale=cf[:, F_HG + gi:F_HG + gi + 1]),
                    [bk[1]], [bq])
                stB[idx] = (st, bs, qt, bq, u)

            def stageB(idx, t0=t0):
                if idx not in stB:
                    return
                c0, kind, info, j, half, firstu = tile_units[idx]
                st, bs, qt, bq, uu = stB.pop(idx)
                name, ch0, gi = info
                hs = slice(half * 512, half * 512 + 512)
                rt, brt = rs.next()
                ot, bo = qo.next()
                x1, bx1 = t1.next()
                x2, bx2 = t2.next()
                bss, bsw = ps[4 + (uu % 2)], ps[6 + (uu % 2)]
                P.c("pe", lambda e: e.matmul(bss[0][:], cb[:, C_INV128:C_INV128 + 128], st[:], start=True, stop=True), [bs], [bss[1]])
                P.c("pe", lambda e: e.matmul(bsw[0][:], cb[:, C_PSW:C_PSW + 128], qt[:], start=True, stop=True), [bq], [bsw[1]])
                P.c("act", lambda e: e.activation(out=rt[:], in_=bss[0][:], func=AF.Ln, bias=cf[:, F_EPS:F_EPS + 1], scale=1.0),
                    [bss[1]], [brt])
                P.c("act", lambda e: e.activation(out=rt[:], in_=rt[:], func=AF.Exp, scale=-0.5), [brt], [brt])
                P.c("dve", lambda e: e.tensor_tensor(out=x1[:], in0=qt[0:32, :], in1=Ct[:, hs], op=ALU.mult), [bq, bC], [bx1])
                P.c("dve", lambda e: e.tensor_tensor(out=ot[:], in0=qt[:], in1=rt[:], op=ALU.mult), [bq, brt], [bo])
                P.c("dve", lambda e: e.tensor_tensor(out=x2[:], in0=bsw[0][0:32, :], in1=St[:, hs], op=ALU.mult), [bsw[1], bS], [bx2])
                P.c("dve", lambda e: e.tensor_tensor(out=x1[:], in0=x1[:], in1=x2[:], op=ALU.add), [bx1, bx2], [bx1])
                P.c("dve", lambda e: e.tensor_tensor(out=ot[0:32, :], in0=x1[:], in1=rt[0:32, :], op=ALU.mult), [bx1, brt, bo], [bo])
                tt = (t0 - 2048 + half * 512) if name[0] == "Q" else (t0 + half * 512)
                P.dma("sp", S[name][ch0 + j, :, tt:tt + 512], ot[:], bo, False)

            nu = len(tile_units)
            stageA(0)
            for idx in range(nu):
                if idx + 1 < nu:
                    stageA(idx + 1)
                stageB(idx)
        P.run()


def dil_phase(nc, ps, K, S, LOOK=3):
    with ExitStack() as es:
        reset_bufs(ps)
        P = Prog(nc, es, "dl")
        cb = K["cb"]
        Vg, bVg = P.sb("Vg", [128, 32, 512], BF16)
        KT = RR([P.sb(f"KT{i}", [128, 4096], BF16) for i in range(2)])
        QT = RR([P.sb(f"QT{i}", [128, 2048], BF16) for i in range(2)])
        Ua = [P.sb(f"Ua{h}", [128, 2048], F32) for h in range(4)]
        Da = [P.sb(f"Da{h}", [128, 2048], F32) for h in range(4)]
        Pt = RR([P.sb(f"Pt{i}", [128, 256], BF16) for i in range(4)])
        Pm = RR([P.sb(f"Pm{i}", [128, 256], BF16) for i in range(4)])
        od = RR([P.sb(f"od{i}", [128, 2048], BF16) for i in range(2)])
        ones = cb[:, C_ONES:C_ONES + 128]
        flag = K["flag"][:, :]
        units = []
        for g, r in enumerate((1, 4, 16)):
            nmb = 32 // r
            for h in range(4):
                for c in range(r):
                    for mb in range(nmb // 2, nmb):
                        units.append((g, r, h, c, mb))
        st = {}
        cur = {"gh": None, "g": None}

        def stage1(u):
            g, r, h, c, mb = units[u]
            if cur["gh"] != (g, h):
                cur["gh"] = (g, h)
                kt, bkt = KT.next()
                qt, bqt = QT.next()
                P.dma("sp", kt[:], S["Kd"][g * 4 + h], bkt, True)
                P.dma("sp", qt[:], S["Qd"][g * 4 + h], bqt, True)
                cur["kq"] = (kt, bkt, qt, bqt)
            kt, bkt, qt, bqt = cur["kq"]
            bS_ = ps[u % 4]
            q0 = mb * 128 * r + c - 2048
            qsl = slice(q0, q0 + 127 * r + 1, r)
            kp = slice((mb - 1) * 128 * r + c, (mb - 1) * 128 * r + c + 127 * r + 1, r)
            kc = slice(mb * 128 * r + c, mb * 128 * r + c + 127 * r + 1, r)

            def mm1(e):
                e.matmul(bS_[0][:, 0:128], kt[:, kp], qt[:, qsl], start=True, stop=True)
                return e.matmul(bS_[0][:, 128:256], kt[:, kc], qt[:, qsl], start=True, stop=True)
            P.c("pe", mm1, [bkt, bqt], [bS_[1]])
            pt, bpt = Pt.next()
            pm, bpm = Pm.next()
            P.c("act", lambda e: e.activation(out=pt[:], in_=bS_[0][:, 0:256], func=AF.Exp, scale=SCALE), [bS_[1]], [bpt])
            P.c("pool", lambda e: e.tensor_tensor(out=pm[:], in0=pt[:], in1=cb[:, C_MDIL:C_MDIL + 256], op=ALU.mult), [bpt], [bpm])
            st[u] = (pm, bpm, qsl)

        def stage2(u):
            g, r, h, c, mb = units[u]
            nmb = 32 // r
            if cur["g"] != g:
                cur["g"] = g
                for m2 in range(nmb):
                    src = S["Vd"][m2 * 128 * r:(m2 + 1) * 128 * r, g * 512:(g + 1) * 512].rearrange("(p r) n -> p r n", r=r)
                    P.dma("sp", Vg[:, m2 * r:(m2 + 1) * r, :], src, bVg, True)
            pm, bpm, qsl = st.pop(u)
            bU_ = ps[4 + u % 4]
            prev_pref = (mb - 1) < nmb // 2
            hc = slice(h * 128, h * 128 + 128)

            def mm2(e):
                e.matmul(bU_[0][:, 0:128], Vg[:, (mb - 1) * r + c, hc], pm[:, 0:128], start=True, stop=False)
                e.matmul(bU_[0][:, 0:128], Vg[:, mb * r + c, hc], pm[:, 128:256], start=False, stop=True)
                e.matmul(bU_[0][:, 128:256], flag if prev_pref else ones, pm[:, 0:128], start=True, stop=False)
                return e.matmul(bU_[0][:, 128:256], ones, pm[:, 128:256], start=False, stop=True)
            P.c("pe", mm2, [bpm, bVg], [bU_[1]])
            ua, bua = Ua[h]
            da, bda = Da[h]
            if g == 0:
                P.c("dve", lambda e: e.tensor_copy(out=ua[:, qsl], in_=bU_[0][:, 0:128]), [bU_[1]], [bua])
                P.c("dve", lambda e: e.tensor_copy(out=da[:, qsl], in_=bU_[0][:, 128:256]), [bU_[1]], [bda])
            else:
                P.c("dve", lambda e: e.tensor_tensor(out=ua[:, qsl], in0=ua[:, qsl], in1=bU_[0][:, 0:128], op=ALU.add), [bU_[1], bua], [bua])
                P.c("dve", lambda e: e.tensor_tensor(out=da[:, qsl], in0=da[:, qsl], in1=bU_[0][:, 128:256], op=ALU.add), [bU_[1], bda], [bda])

        n = len(units)
        for u in range(min(LOOK, n)):
            stage1(u)
        for u in range(n):
            if u + LOOK < n:
                stage1(u + LOOK)
            stage2(u)
        for h in range(4):
            ua, bua = Ua[h]
            da, bda = Da[h]
            ot, bo = od.next()
            P.c("dve", lambda e, da=da: e.reciprocal(out=da[:], in_=da[:]), [bda], [bda])
            P.c("dve", lambda e, ot=ot, ua=ua, da=da: e.tensor_tensor(out=ot[:], in0=ua[:], in1=da[:], op=ALU.mult), [bua, bda], [bo])
            P.dma("sp", S["Od"][h], ot[:], bo, False)
        P.run()


def diff_phase(nc, ps, K, S, LOOK=3):
    with ExitStack() as es:
        reset_bufs(ps)
        P = Prog(nc, es, "df")
        cb = K["cb"]
        KT = RR([P.sb(f"KT{i}", [128, 2, 4096], BF16) for i in range(2)])
        QT = RR([P.sb(f"QT{i}", [128, 2, 2048], BF16) for i in range(2)])
        Vh = RR([P.sb(f"Vh{i}", [128, 32, 256], BF16) for i in range(2)])
        Pt = RR([P.sb(f"Pt{i}", [128, 512], BF16) for i in range(5)])
        rd, brd = P.sb("rd", [128, 512], F32)
        o1, bo1 = P.sb("o1", [128, 512], F32)
        o2, bo2 = P.sb("o2", [128, 512], F32)
        sqt, bsq = P.sb("sq", [128, 512], BF16)
        rst, brs = P.sb("rs", [128, 256], F32)
        of = RR([P.sb(f"of{i}", [128, 512], BF16) for i in range(2)])
        ones = cb[:, C_ONES:C_ONES + 128]
        flag = K["flag"][:, :]
        nlam = K["nlam"]
        gs = K["gs"]
        units = []
        for h in range(4):
            for qi in range(8):
                nkb = 18 + 2 * qi
                for kb in range(nkb):
                    units.append((h, qi, kb, nkb))
        st = {}
        cur = {"h": None}
        bU = (ps[2], ps[3])
        bD = ps[4]
        bN = ps[5]

        def stage1(u):
            h, qi, kb, nkb = units[u]
            if cur["h"] != h:
                cur["h"] = h
                kt, bkt = KT.next()
                qt, bqt = QT.next()
                vh, bvh = Vh.next()
                for cpt in range(2):
                    P.dma("sp", kt[:, cpt, :], S["Kf"][h * 2 + cpt], bkt, True)
                    P.dma("sp", qt[:, cpt, :], S["Qf"][h * 2 + cpt], bqt, True)
                P.dma("sp", vh[:], S["Vf"][:, h * 256:(h + 1) * 256].rearrange("(kb p) n -> p kb n", p=128), bvh, True)
                cur["kqv"] = (kt, bkt, qt, bqt, vh, bvh)
            kt, bkt, qt, bqt, vh, bvh = cur["kqv"]
            bS_ = ps[(0, 1, 6, 7)[u % 4]]
            qs = slice(qi * 256, qi * 256 + 256)
            ks = slice(kb * 128, kb * 128 + 128)

            def mm1(e):
                e.matmul(bS_[0][:, 0:256], kt[:, 0, ks], qt[:, 0, qs], start=True, stop=True)
                return e.matmul(bS_[0][:, 256:512], kt[:, 1, ks], qt[:, 1, qs], start=True, stop=True)
            P.c("pe", mm1, [bkt, bqt], [bS_[1]])
            pt, bpt = Pt.next()
            P.c("act", lambda e: e.activation(out=pt[:], in_=bS_[0][:], func=AF.Exp, scale=SCALE), [bS_[1]], [bpt])
            if kb >= nkb - 2:
                mc = C_M1 if kb == nkb - 2 else C_M2
                P.c("pool", lambda e: e.tensor_tensor(out=pt[:], in0=pt[:], in1=cb[:, mc:mc + 512], op=ALU.mult), [bpt], [bpt])
            st[u] = (pt, bpt, vh, bvh)

        def stage2(u):
            h, qi, kb, nkb = units[u]
            pt, bpt, vh, bvh = st.pop(u)
            first, last = kb == 0, kb == nkb - 1

            def mm2(e):
                for cpt in range(2):
                    for dvc in range(2):
                        e.matmul(bU[cpt][0][:, dvc * 256:dvc * 256 + 256], vh[:, kb, dvc * 128:dvc * 128 + 128],
                                 pt[:, cpt * 256:cpt * 256 + 256], start=(first and dvc == 0), stop=last, skip_group_check=True)
                for cpt in range(2):
                    ins = e.matmul(bD[0][:, cpt * 256:cpt * 256 + 256], flag if kb < 16 else ones,
                                   pt[:, cpt * 256:cpt * 256 + 256], start=(first and cpt == 0), stop=last, skip_group_check=True)
                return ins
            P.c("pe", mm2, [bpt, bvh], [bU[0][1], bU[1][1], bD[1]])
            if not last:
                return
            P.c("dve", lambda e: e.reciprocal(out=rd[:], in_=bD[0][:]), [bD[1]], [brd])
            for dvc in range(2):
                ds = slice(dvc * 256, dvc * 256 + 256)
                P.c("dve", lambda e, ds=ds: e.tensor_tensor(out=o1[:, ds], in0=bU[0][0][:, ds], in1=rd[:, 0:256], op=ALU.mult),
                    [bU[0][1], brd], [bo1])
                P.c("dve", lambda e, ds=ds: e.tensor_tensor(out=o2[:, ds], in0=bU[1][0][:, ds], in1=rd[:, 256:512], op=ALU.mult),
                    [bU[1][1], brd], [bo2])
            P.c("dve", lambda e: e.scalar_tensor_tensor(out=o1[:], in0=o2[:], scalar=nlam[:, 0:1], in1=o1[:], op0=ALU.mult, op1=ALU.add),
                [bo1, bo2], [bo1])
            P.c("act", lambda e: e.activation(out=sqt[:], in_=o1[:], func=AF.Square), [bo1], [bsq])

            def mm3(e):
                e.matmul(bN[0][:, 0:256], cb[:, C_INV256:C_INV256 + 128], sqt[:, 0:256], start=True, stop=False)
                return e.matmul(bN[0][:, 0:256], cb[:, C_INV256:C_INV256 + 128], sqt[:, 256:512], start=False, stop=True)
            P.c("pe", mm3, [bsq], [bN[1]])
            P.c("act", lambda e: e.activation(out=rst[:], in_=bN[0][:, 0:256], func=AF.Sqrt, bias=K["cf"][:, F_EPS:F_EPS + 1], scale=1.0),
                [bN[1]], [brs])
            P.c("dve", lambda e: e.reciprocal(out=rst[:], in_=rst[:]), [brs], [brs])
            ot, bot = of.next()
            for dvc in range(2):
                ds = slice(dvc * 256, dvc * 256 + 256)
                P.c("dve", lambda e, ds=ds, dvc=dvc: e.scalar_tensor_tensor(out=ot[:, ds], in0=o1[:, ds], scalar=gs[:, dvc:dvc + 1],
                                                                           in1=rst[:], op0=ALU.mult, op1=ALU.mult), [bo1, brs], [bot])
            for dvc in range(2):
                P.dma("sp", S["Of"][h * 2 + dvc, :, qi * 256:qi * 256 + 256], ot[:, dvc * 256:dvc * 256 + 256], bot, False)

        n = len(units)
        for u in range(min(LOOK, n)):
            stage1(u)
        for u in range(n):
            if u + LOOK < n:
                stage1(u + LOOK)
            stage2(u)
        P.run()


def out_phase(nc, ps, K, S, WA, WB, WO):
    with ExitStack() as es:
        reset_bufs(ps)
        P = Prog(nc, es, "op")
        wa, bwa = P.sb("wa", [128, 4, 2048], BF16)
        wb, bwb = P.sb("wb", [128, 8, 2048], BF16)
        wo, bwo = P.sb("wo", [128, 16, 2048], BF16)
        odt = RR([P.sb(f"od{i}", [128, 4, 512], BF16) for i in range(2)])
        oft = RR([P.sb(f"of{i}", [128, 8, 512], BF16) for i in range(2)])
        yT = RR([P.sb(f"y{i}", [128, 16, 512], BF16) for i in range(1)])
        sgd = RR([P.sb(f"sgd{i}", [128, 512], F32) for i in range(2)])
        sgf = RR([P.sb(f"sgf{i}", [128, 512], F32) for i in range(2)])
        ta = RR([P.sb(f"ta{i}", [128, 512], F32) for i in range(2)])
        tb_ = RR([P.sb(f"tb{i}", [128, 512], F32) for i in range(2)])
        xin = RR([P.sb(f"xin{i}", [128, 512], F32) for i in range(2)])
        xout = RR([P.sb(f"xo{i}", [128, 512], F32) for i in range(2)])
        P.dma("pool", wa[:], WA.rearrange("(k p) n -> p k n", p=128), bwa, True)
        for i in range(2):
            P.dma("pool", wb[:, i * 4:i * 4 + 4, :], WB.rearrange("(k p) n -> p k n", p=128)[:, i * 4:i * 4 + 4, :], bwb, True)
        for i in range(4):
            P.dma("pool", wo[:, i * 4:i * 4 + 4, :], WO.rearrange("(k p) n -> p k n", p=128)[:, i * 4:i * 4 + 4, :], bwo, True)
        u = 0

        def load_o(ti):
            ts_ = slice(ti * 512, ti * 512 + 512)
            od, bod = odt.next()
            of, bof = oft.next()
            for k in range(4):
                P.dma("sp", od[:, k, :], S["Od"][k, :, ts_], bod, True)
            for k in range(8):
                P.dma("sp", of[:, k, :], S["Of"][k, :, ts_], bof, True)
            return od, bod, of, bof
        nxt = load_o(0)
        for ti in range(4):
            ts = slice(ti * 512, ti * 512 + 512)
            od, bod, of, bof = nxt
            y, by = yT.next()
            for dc in range(16):
                bA, bB = ps[(u % 2) * 2], ps[(u % 2) * 2 + 1]
                u += 1
                cs = slice(dc * 128, dc * 128 + 128)

                def mm(e, od=od, of=of, cs=cs, bA=bA, bB=bB):
                    for k in range(4):
                        e.matmul(bA[0][:], wa[:, k, cs], od[:, k, :], start=(k == 0), stop=(k == 3))
                    for k in range(8):
                        ins = e.matmul(bB[0][:], wb[:, k, cs], of[:, k, :], start=(k == 0), stop=(k == 7))
                    return ins
                P.c("pe", mm, [bwa, bwb, bod, bof], [bA[1], bB[1]])
                g1, bg1 = sgd.next()
                g2, bg2 = sgf.next()
                a_, ba_ = ta.next()
                b_, bb_ = tb_.next()
                P.dma("sp", g1[:], S["SG"][dc, :, ts], bg1, True)
                P.dma("sp", g2[:], S["SG"][16 + dc, :, ts], bg2, True)
                P.c("dve", lambda e, a_=a_, g1=g1, bA=bA: e.tensor_tensor(out=a_[:], in0=g1[:], in1=bA[0][:], op=ALU.mult), [bg1, bA[1]], [ba_])
                P.c("dve", lambda e, b_=b_, g2=g2, bB=bB: e.tensor_tensor(out=b_[:], in0=g2[:], in1=bB[0][:], op=ALU.mult), [bg2, bB[1]], [bb_])
                P.c("dve", lambda e, y=y, dc=dc, a_=a_, b_=b_: e.tensor_tensor(out=y[:, dc, :], in0=a_[:], in1=b_[:], op=ALU.add),
                    [ba_, bb_], [by])
            if ti + 1 < 4:
                nxt = load_o(ti + 1)
            for dc2 in range(16):
                bC_ = ps[4 + dc2 % 4]
                cs = slice(dc2 * 128, dc2 * 128 + 128)

                def mm(e, y=y, cs=cs, bC_=bC_):
                    for k in range(16):
                        ins = e.matmul(bC_[0][:], wo[:, k, cs], y[:, k, :], start=(k == 0), stop=(k == 15))
                    return ins
                P.c("pe", mm, [bwo, by], [bC_[1]])
                xt, bx = xin.next()
                xo, bxo = xout.next()
                P.dma("sp", xt[:], S["x1T"][dc2, :, 2048 + ti * 512:2048 + ti * 512 + 512], bx, True)
                P.c("dve", lambda e, xo=xo, xt=xt, bC_=bC_: e.tensor_tensor(out=xo[:], in0=bC_[0][:], in1=xt[:], op=ALU.add), [bC_[1], bx], [bxo])
                P.dma("sp", S["x2T"][dc2, :, ts], xo[:], bxo, False)
        P.run()


IN_SHAPES = {
    "xT": ([16, 128, TFR], F32), "pos": ([32, TFR], I32), "cbf": ([128, NCB], F32), "cf32": ([128, NCF], F32),
    "flag": ([128, 128], F32), "lamv": ([128, 4, 128], F32),
    "ffn1_wg": ([D, DFF], F32), "ffn1_wu": ([D, DFF], F32), "ffn1_wd": ([DFF, D], F32),
    "ffn2_wg": ([D, DFF], F32), "ffn2_wu": ([D, DFF], F32), "ffn2_wd": ([DFF, D], F32),
    "w_in": ([D, DIN], F32), "w_a": ([512, D], F32), "w_b": ([1024, D], F32), "w_o": ([D, D], F32),
}


class LazyIns(dict):
    def __init__(self, nc):
        super().__init__()
        self.nc = nc

    def __missing__(self, name):
        shape, dt = IN_SHAPES[name]
        ap = self.nc.dram_tensor(name, shape, dt, kind="ExternalInput").ap()
        self[name] = ap
        return ap


def build(debug=False, upto=7, dump=()):
    nc = bass.Bass("TRN2", target_bir_lowering=False)
    ins = LazyIns(nc)
    kind = "ExternalOutput" if debug else "Internal"
    S = {}

    def scr(name, shape, dt):
        S[name] = nc.dram_tensor(name, shape, dt, kind=("ExternalOutput" if name in dump else "Internal")).ap()
    scr("x1T", [16, 128, TFR], F32)
    scr("Qd", [12, 128, TOWN], BF16)
    scr("Kd", [12, 128, TFR], BF16)
    scr("Vd", [TFR, 1536], BF16)
    scr("Qf", [8, 128, TOWN], BF16)
    scr("Kf", [8, 128, TFR], BF16)
    scr("Vf", [TFR, 1024], BF16)
    scr("SG", [32, 128, TOWN], F32)
    scr("Od", [4, 128, TOWN], BF16)
    scr("Of", [8, 128, TOWN], BF16)
    scr("x2T", [16, 128, TOWN], F32)
    outT = nc.dram_tensor("outT", [16, 128, TOWN], F32, kind="ExternalOutput").ap()
    with ExitStack() as es:
        ps = []
        for i in range(8):
            t = es.enter_context(nc.psum_tensor(f"ps{i}", [128, 512], F32))
            ps.append((t, Buf(f"ps{i}")))
        K = {}
        K["cb"] = es.enter_context(nc.sbuf_tensor("K_cb", [128, NCB], BF16))
        K["cf"] = es.enter_context(nc.sbuf_tensor("K_cf", [128, NCF], F32))
        K["flag"] = es.enter_context(nc.sbuf_tensor("K_flag", [128, 128], BF16))
        K["nlam"] = es.enter_context(nc.sbuf_tensor("K_nlam", [128, 1], F32))
        K["gs"] = es.enter_context(nc.sbuf_tensor("K_gs", [128, 2], F32))
        setup_phase(nc, K, ins)
        if upto >= 2:
            ffn_phase(nc, ps, K, ins["xT"], S["x1T"], [0, 1024, 2048, 3072], F_G1, ins["ffn1_wg"], ins["ffn1_wu"], ins["ffn1_wd"], "f1")
        if upto >= 3:
            proj_phase(nc, ps, K, S["x1T"], ins["w_in"], ins["pos"], S)
        if upto >= 4:
            dil_phase(nc, ps, K, S)
        if upto >= 5:
            diff_phase(nc, ps, K, S)
        if upto >= 6:
            out_phase(nc, ps, K, S, ins["w_a"], ins["w_b"], ins["w_o"])
        if upto >= 7:
            ffn_phase(nc, ps, K, S["x2T"], outT, [0, 1024], F_G2, ins["ffn2_wg"], ins["ffn2_wu"], ins["ffn2_wd"], "f2")
    return nc, list(ins.keys())


def host_consts():
    cb = np.zeros((128, NCB), np.float32)
    cb[:, C_INV2048:C_INV2048 + 128] = 1.0 / 2048
    cb[:, C_INV128:C_INV128 + 128] = 1.0 / 128
    cb[:, C_INV256:C_INV256 + 128] = 1.0 / 256
    cb[:, C_ONES:C_ONES + 128] = 1.0
    for m in range(16):
        cb[m + 16, C_PSW + m] = 1.0
        cb[m, C_PSW + m + 16] = 1.0
    ik = np.arange(128)[:, None]
    iq = np.arange(128)[None, :]
    cb[:, C_MDIL:C_MDIL + 128] = (ik >= iq)
    cb[:, C_MDIL + 128:C_MDIL + 256] = (ik <= iq)
    tri = (ik <= iq).astype(np.float32)
    m1 = np.concatenate([tri, np.ones((128, 128), np.float32)], 1)
    m2 = np.concatenate([np.zeros((128, 128), np.float32), tri], 1)
    cb[:, C_M1:C_M1 + 512] = np.concatenate([m1, m1], 1)
    cb[:, C_M2:C_M2 + 512] = np.concatenate([m2, m2], 1)
    return cb


def make_in_maps(x, positions, ffn1_norm, ffn1_w_gate, ffn1_w_up, ffn1_w_down, mix_norm, w_in,
                 dil_q_norm, dil_k_norm, diff_q_norm, diff_k_norm, diff_lq1, diff_lk1, diff_lq2, diff_lk2,
                 diff_subln, w_dil_branch, w_diff_branch, w_out, ffn2_norm, ffn2_w_gate, ffn2_w_up, ffn2_w_down):
    f = lambda a: np.ascontiguousarray(np.asarray(a, dtype=np.float32))
    x = np.asarray(x, np.float32)
    positions = np.asarray(positions, np.int32)
    cb = host_consts()
    cf = np.zeros((128, NCF), np.float32)
    cf[:, F_G1:F_G1 + 16] = f(ffn1_norm)[0].reshape(16, 128).T
    cf[:, F_GM:F_GM + 16] = f(mix_norm)[0].reshape(16, 128).T
    cf[:, F_G2:F_G2 + 16] = f(ffn2_norm)[0].reshape(16, 128).T
    cf[:, F_HG + 0] = f(dil_q_norm)[0]
    cf[:, F_HG + 1] = f(dil_k_norm)[0]
    cf[:, F_HG + 2] = f(diff_q_norm)[0]
    cf[:, F_HG + 3] = f(diff_k_norm)[0]
    cf[:, F_SUB:F_SUB + 2] = f(diff_subln)[0].reshape(2, 128).T
    inv = (np.float32(500000.0) ** (-np.arange(0, 32, 2, dtype=np.float32) / np.float32(32))).astype(np.float32)
    cf[0:16, F_INV] = inv
    cf[16:32, F_INV] = inv
    cf[0:16, F_SGN] = -1.0
    cf[16:32, F_SGN] = 1.0
    cf[:, F_EPS] = EPS
    lamv = np.stack([f(diff_lq1)[0], f(diff_lk1)[0], f(diff_lq2)[0], f(diff_lk2)[0]], 0)
    lamv = np.ascontiguousarray(np.broadcast_to(lamv[None], (128, 4, 128)))
    shared = {
        "cbf": cb, "cf32": cf, "lamv": lamv,
        "ffn1_wg": f(ffn1_w_gate)[0], "ffn1_wu": f(ffn1_w_up)[0], "ffn1_wd": f(ffn1_w_down)[0],
        "ffn2_wg": f(ffn2_w_gate)[0], "ffn2_wu": f(ffn2_w_up)[0], "ffn2_wd": f(ffn2_w_down)[0],
        "w_in": f(w_in)[0], "w_a": f(w_dil_branch)[0], "w_b": f(w_diff_branch)[0], "w_o": f(w_out)[0],
    }
    maps = []
    for core in range(8):
        b, half = core // 2, core % 2
        xT = np.zeros((D, TFR), np.float32)
        pos = np.zeros((TFR,), np.int32)
        xT[:, 2048:] = x[b, half * 2048:(half + 1) * 2048, :].T
        pos[2048:] = positions[b, half * 2048:(half + 1) * 2048]
        if half == 1:
            xT[:, :2048] = x[b, 0:2048, :].T
            pos[:2048] = positions[b, 0:2048]
        m = dict(shared)
        m["xT"] = np.ascontiguousarray(xT.reshape(16, 128, TFR))
        m["pos"] = np.ascontiguousarray(np.broadcast_to(pos[None], (32, TFR)))
        m["flag"] = np.full((128, 128), float(half), np.float32)
        maps.append(m)
    return maps


def kernel(**inputs):
    maps = make_in_maps(**inputs)
    nc, names = build()
    maps = [{k: m[k] for k in names} for m in maps]
    res = run_bass_kernel_spmd(nc, maps, core_ids=list(range(8)))
    out = np.empty((4, 4096, D), np.float32)
    for core in range(8):
        b, half = core // 2, core % 2
        oT = np.asarray(res.results[core]["outT"], np.float32).reshape(D, TOWN)
        out[b, half * 2048:(half + 1) * 2048, :] = oT.T
    return out
```

```python
import math
from contextlib import ExitStack

import numpy as np
import concourse.bass as bass
import concourse.mybir as mybir
from concourse.bass_utils import run_bass_kernel_spmd

F32 = mybir.dt.float32
BF16 = mybir.dt.bfloat16
I32 = mybir.dt.int32
AF = mybir.ActivationFunctionType
ALU = mybir.AluOpType
AX = mybir.AxisListType

D = 2048
DFF = 5632
DIN = 11776
TOWN = 2048
TFR = 4096
EPS = 1e-6
SCALE = 128.0 ** -0.5
PI = math.pi
ENGS = ("pe", "act", "dve", "pool", "sp")

C_INV2048, C_INV128, C_INV256, C_ONES, C_PSW, C_MDIL, C_M1, C_M2 = 0, 128, 256, 384, 512, 640, 896, 1408
NCB = 1920
F_G1, F_GM, F_G2, F_HG, F_SUB, F_INV, F_SGN, F_EPS = 0, 16, 32, 48, 52, 54, 55, 56
NCF = 57


class Op:
    __slots__ = ("eng", "fn", "deps", "kind", "ms", "val", "sem")

    def __init__(self, eng, fn, kind):
        self.eng, self.fn, self.kind = eng, fn, kind
        self.deps = []
        self.ms = False
        self.val = 0
        self.sem = None


class Buf:
    def __init__(self, name):
        self.name = name
        self.writes = {}
        self.reads = {}
        self.dsem = None
        self.dcnt = 0


class Prog:
    def __init__(self, nc, es, tag):
        self.nc, self.es, self.tag = nc, es, tag
        self.ops = {e: [] for e in ENGS}
        self.allsems = []
        self.csem = {e: self._sem(f"{tag}_{e}") for e in ENGS[:4]}
        self.dma_ops = []
        self.nb = 0

    def _sem(self, name):
        h = self.nc.alloc_semaphore(name=name)
        self.allsems.append(h)
        return h

    def sb(self, name, shape, dt):
        t = self.es.enter_context(self.nc.sbuf_tensor(f"{self.tag}_{name}", shape, dt))
        return t, Buf(name)

    def buf(self, name="b"):
        return Buf(name)

    @staticmethod
    def _key(op):
        return op.eng if op.kind == "c" else ("d", id(op.sem))

    def rec(self, eng, fn, reads=(), writes=(), kind="c", extra=()):
        op = Op(eng, fn, kind)
        deps = {}
        for b in reads:
            for k, d in b.writes.items():
                deps[id(d)] = d
        for b in writes:
            if b.reads:
                for k, d in b.reads.items():
                    deps[id(d)] = d
            else:
                for k, d in b.writes.items():
                    if not (d.kind == "c" and d.eng == eng):
                        deps[id(d)] = d
        for d in extra:
            deps[id(d)] = d
        if eng == "pe":
            deps = {i: d for i, d in deps.items() if not (d.kind == "c" and d.eng == "pe")}
        op.deps = list(deps.values())
        for d in op.deps:
            if d.kind == "c":
                d.ms = True
        self.ops[eng].append(op)
        return op

    def _commit(self, op, reads, writes):
        k = self._key(op)
        for b in reads:
            b.reads[k] = op
        for b in writes:
            if b.reads:
                b.reads = {}
                b.writes = {}
            b.writes[k] = op

    def c(self, eng, fn, reads=(), writes=()):
        op = self.rec(eng, fn, reads, writes)
        self._commit(op, reads, writes)
        return op

    def dma(self, q, out, in_, sbuf, load, reads=(), writes=()):
        if sbuf.dsem is None:
            self.nb += 1
            sbuf.dsem = self._sem(f"{self.tag}_d{self.nb}")
        r = list(reads) + ([] if load else [sbuf])
        w = list(writes) + ([sbuf] if load else [])
        op = self.rec(q, lambda e: e.dma_start(out=out, in_=in_), r, w, kind="d")
        sbuf.dcnt += 16
        op.sem = sbuf.dsem
        op.val = sbuf.dcnt
        self._commit(op, r, w)
        self.dma_ops.append(op)
        return op

    def run(self):
        for q in ("sp", "pool"):
            op = Op(q, None, "c")
            last = {}
            for d in self.dma_ops:
                last[id(d.sem)] = d
            op.deps = list(last.values())
            self.ops[q].append(op)
        for e in ENGS:
            cnt = 0
            for op in self.ops[e]:
                if op.kind == "c" and op.ms:
                    cnt += 1
                    op.val = cnt
        csem = self.csem

        def mk(e):
            ops = self.ops[e]

            def body(eng):
                waited = {}
                for op in ops:
                    for d in op.deps:
                        sem = d.sem if d.kind == "d" else csem[d.eng]
                        if waited.get(id(sem), 0) < d.val:
                            eng.wait_ge(sem, d.val)
                            waited[id(sem)] = d.val
                    if op.fn is not None:
                        ins = op.fn(eng)
                        if op.kind == "d":
                            ins.then_inc(op.sem, 16)
                        elif op.ms:
                            ins.then_inc(csem[e], 1)
            return body

        with self.nc.Block() as block:
            block.tensor(mk("pe"))
            block.scalar(mk("act"))
            block.vector(mk("dve"))
            block.gpsimd(mk("pool"))
            block.sync(mk("sp"))
        self.nc.clear_and_free_semaphores(self.allsems)
        self.nc.all_engine_barrier()


def reset_bufs(ps):
    for _, b in ps:
        b.reads, b.writes = {}, {}


class RR:
    def __init__(self, items):
        self.items, self.i = items, 0

    def next(self):
        x = self.items[self.i % len(self.items)]
        self.i += 1
        return x


def setup_phase(nc, K, ins):
    with ExitStack() as es:
        P = Prog(nc, es, "su")
        bcb, bcf, bfl, blam = Buf("cb"), Buf("cf"), Buf("fl"), Buf("lam")
        P.dma("pool", K["cb"][:], ins["cbf"], bcb, True)
        P.dma("pool", K["flag"][:], ins["flag"], bfl, True)
        P.dma("sp", K["cf"][:], ins["cf32"], bcf, True)
        lamt, blt = P.sb("lamt", [128, 4, 128], F32)
        pr, bpr = P.sb("pr", [128, 2, 128], F32)
        ss, bss = P.sb("ss", [128, 2], F32)
        ee, bee = P.sb("ee", [128, 2], F32)
        P.dma("sp", lamt[:], ins["lamv"], blt, True)
        P.c("dve", lambda e: e.tensor_tensor(out=pr[:, 0, :], in0=lamt[:, 0, :], in1=lamt[:, 1, :], op=ALU.mult), [blt], [bpr])
        P.c("dve", lambda e: e.tensor_tensor(out=pr[:, 1, :], in0=lamt[:, 2, :], in1=lamt[:, 3, :], op=ALU.mult), [blt], [bpr])
        P.c("dve", lambda e: e.reduce_sum(out=ss[:, 0:1], in_=pr[:, 0, :], axis=AX.X), [bpr], [bss])
        P.c("dve", lambda e: e.reduce_sum(out=ss[:, 1:2], in_=pr[:, 1, :], axis=AX.X), [bpr], [bss])
        P.c("act", lambda e: e.activation(out=ee[:], in_=ss[:], func=AF.Exp), [bss], [bee])
        P.c("dve", lambda e: e.tensor_tensor(out=K["nlam"][:], in0=ee[:, 1:2], in1=ee[:, 0:1], op=ALU.subtract), [bee], [blam])
        P.c("dve", lambda e: e.tensor_scalar_add(out=K["nlam"][:], in0=K["nlam"][:], scalar1=-0.2), [blam], [blam])
        P.c("dve", lambda e: e.tensor_scalar_mul(out=K["gs"][:], in0=K["cf"][:, F_SUB:F_SUB + 2], scalar1=0.8), [bcf], [blam])
        P.run()


def norm_prologue(P, ps, K, src, t0, gcol, hT, bhT, xin, sq, rstd, brstd):
    cb = K["cb"]
    for half in range(2):
        for c in range(16):
            xt, bx = xin.next()
            st, bs = sq.next()
            P.dma("sp", xt[:], src[c, :, t0 + half * 512:t0 + half * 512 + 512], bx, True)
            P.c("act", lambda e, st=st, xt=xt: e.activation(out=st[:], in_=xt[:], func=AF.Square), [bx], [bs])
            P.c("pe", lambda e, st=st, c=c, half=half: e.matmul(ps[6 + half][0][:], cb[:, C_INV2048:C_INV2048 + 128], st[:],
                                                                start=(c == 0), stop=(c == 15)), [bs], [ps[6 + half][1]])
        P.c("act", lambda e, half=half: e.activation(out=rstd[:, half * 512:half * 512 + 512], in_=ps[6 + half][0][:], func=AF.Sqrt,
                                                     bias=K["cf"][:, F_EPS:F_EPS + 1], scale=1.0), [ps[6 + half][1]], [brstd])
        P.c("dve", lambda e, half=half: e.reciprocal(out=rstd[:, half * 512:half * 512 + 512], in_=rstd[:, half * 512:half * 512 + 512]),
            [brstd], [brstd])
    for half in range(2):
        for c in range(16):
            xt, bx = xin.next()
            P.dma("sp", xt[:], src[c, :, t0 + half * 512:t0 + half * 512 + 512], bx, True)
            P.c("dve", lambda e, xt=xt, c=c, half=half: e.scalar_tensor_tensor(
                out=hT[:, c, half * 512:half * 512 + 512], in0=xt[:], scalar=K["cf"][:, gcol + c:gcol + c + 1],
                in1=rstd[:, half * 512:half * 512 + 512], op0=ALU.mult, op1=ALU.mult), [bx, brstd], [bhT])


def ffn_phase(nc, ps, K, src, dst, t0s, gcol, Wg, Wu, Wd, tag):
    with ExitStack() as es:
        reset_bufs(ps)
        P = Prog(nc, es, tag)
        hT, bhT = P.sb("hT", [128, 16, 1024], BF16)
        aT, baT = P.sb("aT", [128, 44, 1024], BF16)
        slots = RR([P.sb(f"w{i}", [128, 8192], BF16) for i in range(2)])
        xin = RR([P.sb(f"xin{i}", [128, 512], F32) for i in range(3)])
        sq = RR([P.sb(f"sq{i}", [128, 512], BF16) for i in range(2)])
        sil = RR([P.sb(f"sil{i}", [128, 512], F32) for i in range(2)])
        xout = RR([P.sb(f"xo{i}", [128, 512], F32) for i in range(2)])
        rstd, brstd = P.sb("rstd", [128, 1024], F32)
        Wgv = Wg.rearrange("(k p) n -> p k n", p=128)
        Wuv = Wu.rearrange("(k p) n -> p k n", p=128)
        Wdv = Wd.rearrange("(f p) n -> p f n", p=128)
        for it, t0 in enumerate(t0s):
            if it == 0:
                norm_prologue(P, ps, K, src, t0, gcol, hT, bhT, xin, sq, rstd, brstd)
            u = 0
            for p in range(22):
                sl, bsl = slots.next()
                gv = sl[:, 0:4096].rearrange("p (k n) -> p k n", k=16)
                uv = sl[:, 4096:8192].rearrange("p (k n) -> p k n", k=16)
                P.dma("pool", gv, Wgv[:, :, p * 256:p * 256 + 256], bsl, True)
                P.dma("pool", uv, Wuv[:, :, p * 256:p * 256 + 256], bsl, True)
                for j in range(2):
                    for half in range(2):
                        bg, bu = ps[(u % 2) * 2], ps[(u % 2) * 2 + 1]
                        u += 1

                        def mm(e, gv=gv, uv=uv, j=j, half=half, bg=bg, bu=bu):
                            for k in range(16):
                                e.matmul(bg[0][:], gv[:, k, j * 128:j * 128 + 128], hT[:, k, half * 512:half * 512 + 512],
                                         start=(k == 0), stop=(k == 15))
                            for k in range(16):
                                ins = e.matmul(bu[0][:], uv[:, k, j * 128:j * 128 + 128], hT[:, k, half * 512:half * 512 + 512],
                                               start=(k == 0), stop=(k == 15))
                            return ins
                        P.c("pe", mm, [bsl, bhT], [bg[1], bu[1]])
                        st, bst = sil.next()
                        f = p * 2 + j
                        P.c("act", lambda e, st=st, bg=bg: e.activation(out=st[:], in_=bg[0][:], func=AF.Silu), [bg[1]], [bst])
                        P.c("dve", lambda e, st=st, bu=bu, f=f, half=half: e.tensor_tensor(
                            out=aT[:, f, half * 512:half * 512 + 512], in0=st[:], in1=bu[0][:], op=ALU.mult), [bst, bu[1]], [baT])
            if it + 1 < len(t0s):
                norm_prologue(P, ps, K, src, t0s[it + 1], gcol, hT, bhT, xin, sq, rstd, brstd)
            for dp in range(8):
                base = 4 if dp % 2 == 0 else 0
                for fh in range(2):
                    sl, bsl = slots.next()
                    dv = sl[:, 0:5632].rearrange("p (f n) -> p f n", f=22)
                    P.dma("pool", dv, Wdv[:, fh * 22:fh * 22 + 22, dp * 256:dp * 256 + 256], bsl, True)
                    for j in range(2):
                        for half in range(2):
                            bk = ps[base + j * 2 + half]

                            def mm(e, dv=dv, j=j, half=half, bk=bk, fh=fh):
                                for f in range(22):
                                    ins = e.matmul(bk[0][:], dv[:, f, j * 128:j * 128 + 128],
                                                   aT[:, fh * 22 + f, half * 512:half * 512 + 512],
                                                   start=(fh == 0 and f == 0), stop=(fh == 1 and f == 21))
                                return ins
                            P.c("pe", mm, [bsl, baT], [bk[1]])
                for j in range(2):
                    for half in range(2):
                        bk = ps[base + j * 2 + half]
                        c = dp * 2 + j
                        xt, bx = xin.next()
                        xo, bxo = xout.next()
                        tt = t0 + half * 512
                        P.dma("sp", xt[:], src[c, :, tt:tt + 512], bx, True)
                        P.c("dve", lambda e, xo=xo, xt=xt, bk=bk: e.scalar_tensor_tensor(
                            out=xo[:], in0=bk[0][:], scalar=0.5, in1=xt[:], op0=ALU.mult, op1=ALU.add), [bk[1], bx], [bxo])
                        P.dma("sp", dst[c, :, tt:tt + 512], xo[:], bxo, False)
        P.run()


def proj_phase(nc, ps, K, x1T, win, posd, S, tiles=(0, 1, 2, 3), pfilter=None):
    with ExitStack() as es:
        reset_bufs(ps)
        P = Prog(nc, es, "pj")
        cb, cf = K["cb"], K["cf"]
        hT, bhT = P.sb("hT", [128, 16, 1024], BF16)
        slots = RR([P.sb(f"w{i}", [128, 8192], BF16) for i in range(3)])
        xin = RR([P.sb(f"xin{i}", [128, 512], F32) for i in range(3)])
        sq = RR([P.sb(f"sq{i}", [128, 512], BF16) for i in range(3)])
        qg = RR([P.sb(f"qg{i}", [128, 512], BF16) for i in range(3)])
        rs = RR([P.sb(f"rs{i}", [128, 512], F32) for i in range(3)])
        qo = RR([P.sb(f"qo{i}", [128, 512], BF16) for i in range(3)])
        t1 = RR([P.sb(f"t1{i}", [32, 512], F32) for i in range(3)])
        t2 = RR([P.sb(f"t2{i}", [32, 512], F32) for i in range(3)])
        vst = RR([P.sb(f"vs{i}", [128, 512], BF16) for i in range(3)])
        gst = RR([P.sb(f"gs{i}", [128, 512], F32) for i in range(3)])
        rstd, brstd = P.sb("rstd", [128, 1024], F32)
        posi, bposi = P.sb("posi", [32, 1024], I32)
        ang, bang = P.sb("ang", [32, 1024], F32)
        a1, ba1 = P.sb("a1", [32, 1024], F32)
        ki, bki = P.sb("ki", [32, 1024], I32)
        kf, bkf = P.sb("kf", [32, 1024], F32)
        Ct, bC = P.sb("C", [32, 1024], F32)
        St, bS = P.sb("S", [32, 1024], F32)
        npi, bnpi = P.sb("npi", [32, 1], F32)
        P.c("dve", lambda e: e.memset(npi[:], -PI), [], [bnpi])
        Wv = win.rearrange("(k p) n -> p k n", p=128)
        panels = []
        for i in range(3):
            panels.append((i * 512, "qk", ("Qd", i * 4, 0)))
        for i in range(3):
            panels.append((1536 + i * 512, "qk", ("Kd", i * 4, 1)))
        for i in range(3):
            panels.append((3072 + i * 512, "v", ("Vd", i * 512)))
        for i in range(2):
            panels.append((4608 + i * 512, "qk", ("Qf", i * 4, 2)))
        for i in range(2):
            panels.append((5632 + i * 512, "qk", ("Kf", i * 4, 3)))
        for i in range(2):
            panels.append((6656 + i * 512, "v", ("Vf", i * 512)))
        for i in range(8):
            panels.append((7680 + i * 512, "g", (i * 4,)))
        u = 0
        if pfilter is not None:
            panels = [p for i, p in enumerate(panels) if i in pfilter]
        for ti in tiles:
            t0 = ti * 1024
            own = ti >= 2
            norm_prologue(P, ps, K, x1T, t0, F_GM, hT, bhT, xin, sq, rstd, brstd)
            P.dma("sp", posi[:], posd[:, t0:t0 + 1024], bposi, True)
            P.c("dve", lambda e: e.tensor_copy(out=ang[:], in_=posi[:]), [bposi], [bang])
            P.c("dve", lambda e: e.tensor_scalar_mul(out=ang[:], in0=ang[:], scalar1=cf[0:32, F_INV:F_INV + 1]), [bang], [bang])
            for (off, Tt, bT) in ((0.75, Ct, bC), (0.5, St, bS)):
                P.c("dve", lambda e, off=off: e.tensor_scalar(out=a1[:], in0=ang[:], scalar1=1.0 / (2 * PI), scalar2=off, op0=ALU.mult,
                                                              op1=ALU.add), [bang], [ba1])
                P.c("dve", lambda e: e.tensor_copy(out=ki[:], in_=a1[:]), [ba1], [bki])
                P.c("dve", lambda e: e.tensor_copy(out=kf[:], in_=ki[:]), [bki], [bkf])
                P.c("dve", lambda e: e.tensor_tensor(out=a1[:], in0=a1[:], in1=kf[:], op=ALU.subtract), [ba1, bkf], [ba1])
                P.c("dve", lambda e: e.scalar_tensor_tensor(out=a1[:], in0=a1[:], scalar=0.0, in1=a1[:], op0=ALU.is_lt, op1=ALU.add),
                    [ba1], [ba1])
                P.c("act", lambda e, Tt=Tt: e.activation(out=Tt[:], in_=a1[:], func=AF.Sin, bias=npi[:], scale=2 * PI), [ba1, bnpi], [bT])
            P.c("dve", lambda e: e.tensor_scalar_mul(out=St[:], in0=St[:], scalar1=cf[0:32, F_SGN:F_SGN + 1]), [bS], [bS])
            tile_units = []
            for (c0, kind, info) in panels:
                if not own and (kind == "g" or info[0] in ("Qd", "Qf")):
                    continue
                grp = (info[1] // 4 if kind == "qk" else info[1] // 512) if info[0] in ("Kd", "Vd") else 2
                first_used = True
                if kind == "v":
                    for tb in range(8):
                        if not own and grp < 2 and (ti == 0 or tb < (7 if grp == 0 else 4)):
                            continue
                        tile_units.append((c0, kind, info, tb, 0, first_used))
                        first_used = False
                else:
                    for j in range(4):
                        for half in range(2):
                            if not own and grp < 2 and (ti == 0 or half == 0):
                                continue
                            tile_units.append((c0, kind, info, j, half, first_used))
                            first_used = False
            stB = {}

            def stageA(idx, t0=t0):
                nonlocal u
                c0, kind, info, j, half, firstu = tile_units[idx]
                if firstu:
                    sl, bsl = slots.next()
                    wv = sl[:, :].rearrange("p (k n) -> p k n", k=16)
                    P.dma("pool", wv, Wv[:, :, c0:c0 + 512], bsl, True)
                    stageA.cur = (wv, bsl)
                wv, bsl = stageA.cur
                bk = ps[u % 4]
                u += 1
                if kind == "v":
                    tb = j

                    def mm(e):
                        for k in range(16):
                            ins = e.matmul(bk[0][:], hT[:, k, tb * 128:tb * 128 + 128], wv[:, k, :], start=(k == 0), stop=(k == 15))
                        return ins
                    P.c("pe", mm, [bsl, bhT], [bk[1]])
                    vt, bvt = vst.next()
                    P.c("act", lambda e: e.activation(out=vt[:], in_=bk[0][:], func=AF.Copy), [bk[1]], [bvt])
                    P.dma("sp", S[info[0]][t0 + tb * 128:t0 + tb * 128 + 128, info[1]:info[1] + 512], vt[:], bvt, False)
                    return
                hs = slice(half * 512, half * 512 + 512)

                def mm(e):
                    for k in range(16):
                        ins = e.matmul(bk[0][:], wv[:, k, j * 128:j * 128 + 128], hT[:, k, hs], start=(k == 0), stop=(k == 15))
                    return ins
                P.c("pe", mm, [bsl, bhT], [bk[1]])
                if kind == "g":
                    gt, bgt = gst.next()
                    P.c("act", lambda e: e.activation(out=gt[:], in_=bk[0][:], func=AF.Sigmoid), [bk[1]], [bgt])
                    tt = t0 - 2048 + half * 512
                    P.dma("sp", S["SG"][info[0] + j, :, tt:tt + 512], gt[:], bgt, False)
                    return
                name, ch0, gi = info
                st, bs = sq.next()
                qt, bq = qg.next()
                P.c("act", lambda e: e.activation(out=st[:], in_=bk[0][:], func=AF.Square), [bk[1]], [bs])
                P.c("act", lambda e: e.activation(out=qt[:], in_=bk[0][:], func=AF.Copy, scale=cf[:, F_HG + gi:F_HG + gi + 1]),
                    [bk[1]], [bq])
                stB[idx] = (st, bs, qt, bq, u)

            def stageB(idx, t0=t0):
                if idx not in stB:
                    return
                c0, kind, info, j, half, firstu = tile_units[idx]
                st, bs, qt, bq, uu = stB.pop(idx)
                name, ch0, gi = info
                hs = slice(half * 512, half * 512 + 512)
                rt, brt = rs.next()
                ot, bo = qo.next()
                x1, bx1 = t1.next()
                x2, bx2 = t2.next()
                bss, bsw = ps[4 + (uu % 2)], ps[6 + (uu % 2)]
                P.c("pe", lambda e: e.matmul(bss[0][:], cb[:, C_INV128:C_INV128 + 128], st[:], start=True, stop=True), [bs], [bss[1]])
                P.c("pe", lambda e: e.matmul(bsw[0][:], cb[:, C_PSW:C_PSW + 128], qt[:], start=True, stop=True), [bq], [bsw[1]])
                P.c("act", lambda e: e.activation(out=rt[:], in_=bss[0][:], func=AF.Ln, bias=cf[:, F_EPS:F_EPS + 1], scale=1.0),
                    [bss[1]], [brt])
                P.c("act", lambda e: e.activation(out=rt[:], in_=rt[:], func=AF.Exp, scale=-0.5), [brt], [brt])
                P.c("dve", lambda e: e.tensor_tensor(out=x1[:], in0=qt[0:32, :], in1=Ct[:, hs], op=ALU.mult), [bq, bC], [bx1])
                P.c("dve", lambda e: e.tensor_tensor(out=ot[:], in0=qt[:], in1=rt[:], op=ALU.mult), [bq, brt], [bo])
                P.c("dve", lambda e: e.tensor_tensor(out=x2[:], in0=bsw[0][0:32, :], in1=St[:, hs], op=ALU.mult), [bsw[1], bS], [bx2])
                P.c("dve", lambda e: e.tensor_tensor(out=x1[:], in0=x1[:], in1=x2[:], op=ALU.add), [bx1, bx2], [bx1])
                P.c("dve", lambda e: e.tensor_tensor(out=ot[0:32, :], in0=x1[:], in1=rt[0:32, :], op=ALU.mult), [bx1, brt, bo], [bo])
                tt = (t0 - 2048 + half * 512) if name[0] == "Q" else (t0 + half * 512)
                P.dma("sp", S[name][ch0 + j, :, tt:tt + 512], ot[:], bo, False)

            nu = len(tile_units)
            stageA(0)
            for idx in range(nu):
                if idx + 1 < nu:
                    stageA(idx + 1)
                stageB(idx)
        P.run()


def dil_phase(nc, ps, K, S, LOOK=3):
    with ExitStack() as es:
        reset_bufs(ps)
        P = Prog(nc, es, "dl")
        cb = K["cb"]
        VgR = RR([P.sb(f"Vg{i}", [128, 32, 512], BF16) for i in range(2)])
        KT = RR([P.sb(f"KT{i}", [128, 4096], BF16) for i in range(2)])
        QT = RR([P.sb(f"QT{i}", [128, 2048], BF16) for i in range(2)])
        Ua = [P.sb(f"Ua{h}", [128, 2048], F32) for h in range(4)]
        Da = [P.sb(f"Da{h}", [128, 2048], F32) for h in range(4)]
        Pt = RR([P.sb(f"Pt{i}", [128, 256], BF16) for i in range(4)])
        Pm = RR([P.sb(f"Pm{i}", [128, 256], BF16) for i in range(4)])
        od = RR([P.sb(f"od{i}", [128, 2048], BF16) for i in range(2)])
        ones = cb[:, C_ONES:C_ONES + 128]
        flag = K["flag"][:, :]
        units = []
        for g, r in enumerate((1, 4, 16)):
            nmb = 32 // r
            for h in range(4):
                for c in range(r):
                    for mb in range(nmb // 2, nmb):
                        units.append((g, r, h, c, mb))
        st = {}
        cur = {"gh": None, "g": None}

        def load_v(g):
            r = (1, 4, 16)[g]
            Vt, bVt = VgR.next()
            for m2 in range(32 // r):
                src = S["Vd"][m2 * 128 * r:(m2 + 1) * 128 * r, g * 512:(g + 1) * 512].rearrange("(p r) n -> p r n", r=r)
                P.dma("sp", Vt[:, m2 * r:(m2 + 1) * r, :], src, bVt, True)
            return Vt, bVt

        def stage1(u):
            g, r, h, c, mb = units[u]
            if cur["gh"] != (g, h):
                cur["gh"] = (g, h)
                kt, bkt = KT.next()
                qt, bqt = QT.next()
                P.dma("sp", kt[:], S["Kd"][g * 4 + h], bkt, True)
                P.dma("sp", qt[:], S["Qd"][g * 4 + h], bqt, True)
                cur["kq"] = (kt, bkt, qt, bqt)
            kt, bkt, qt, bqt = cur["kq"]
            bS_ = ps[u % 4]
            q0 = mb * 128 * r + c - 2048
            qsl = slice(q0, q0 + 127 * r + 1, r)
            kp = slice((mb - 1) * 128 * r + c, (mb - 1) * 128 * r + c + 127 * r + 1, r)
            kc = slice(mb * 128 * r + c, mb * 128 * r + c + 127 * r + 1, r)

            def mm1(e):
                e.matmul(bS_[0][:, 0:128], kt[:, kp], qt[:, qsl], start=True, stop=True)
                return e.matmul(bS_[0][:, 128:256], kt[:, kc], qt[:, qsl], start=True, stop=True)
            P.c("pe", mm1, [bkt, bqt], [bS_[1]])
            pt, bpt = Pt.next()
            pm, bpm = Pm.next()
            P.c("act", lambda e: e.activation(out=pt[:], in_=bS_[0][:, 0:256], func=AF.Exp, scale=SCALE), [bS_[1]], [bpt])
            P.c("pool", lambda e: e.tensor_tensor(out=pm[:], in0=pt[:], in1=cb[:, C_MDIL:C_MDIL + 256], op=ALU.mult), [bpt], [bpm])
            st[u] = (pm, bpm, qsl)

        def stage2(u):
            g, r, h, c, mb = units[u]
            nmb = 32 // r
            if cur["g"] != g:
                cur["g"] = g
                if g == 0:
                    cur["v"] = load_v(0)
                else:
                    cur["v"] = cur["vn"]
                if g + 1 < 3:
                    cur["vn"] = load_v(g + 1)
            Vg, bVg = cur["v"]
            pm, bpm, qsl = st.pop(u)
            bU_ = ps[4 + u % 4]
            prev_pref = (mb - 1) < nmb // 2
            hc = slice(h * 128, h * 128 + 128)

            def mm2(e):
                e.matmul(bU_[0][:, 0:128], Vg[:, (mb - 1) * r + c, hc], pm[:, 0:128], start=True, stop=False)
                e.matmul(bU_[0][:, 0:128], Vg[:, mb * r + c, hc], pm[:, 128:256], start=False, stop=True)
                e.matmul(bU_[0][:, 128:256], flag if prev_pref else ones, pm[:, 0:128], start=True, stop=False)
                return e.matmul(bU_[0][:, 128:256], ones, pm[:, 128:256], start=False, stop=True)
            P.c("pe", mm2, [bpm, bVg], [bU_[1]])
            ua, bua = Ua[h]
            da, bda = Da[h]
            if g == 0:
                P.c("dve", lambda e: e.tensor_copy(out=ua[:, qsl], in_=bU_[0][:, 0:128]), [bU_[1]], [bua])
                P.c("dve", lambda e: e.tensor_copy(out=da[:, qsl], in_=bU_[0][:, 128:256]), [bU_[1]], [bda])
            else:
                P.c("dve", lambda e: e.tensor_tensor(out=ua[:, qsl], in0=ua[:, qsl], in1=bU_[0][:, 0:128], op=ALU.add), [bU_[1], bua], [bua])
                P.c("dve", lambda e: e.tensor_tensor(out=da[:, qsl], in0=da[:, qsl], in1=bU_[0][:, 128:256], op=ALU.add), [bU_[1], bda], [bda])

        n = len(units)
        for u in range(min(LOOK, n)):
            stage1(u)
        for u in range(n):
            if u + LOOK < n:
                stage1(u + LOOK)
            stage2(u)
        for h in range(4):
            ua, bua = Ua[h]
            da, bda = Da[h]
            ot, bo = od.next()
            P.c("dve", lambda e, da=da: e.reciprocal(out=da[:], in_=da[:]), [bda], [bda])
            P.c("dve", lambda e, ot=ot, ua=ua, da=da: e.tensor_tensor(out=ot[:], in0=ua[:], in1=da[:], op=ALU.mult), [bua, bda], [bo])
            P.dma("sp", S["Od"][h], ot[:], bo, False)
        P.run()


def diff_phase(nc, ps, K, S, LOOK=3):
    with ExitStack() as es:
        reset_bufs(ps)
        P = Prog(nc, es, "df")
        cb = K["cb"]
        KT = RR([P.sb(f"KT{i}", [128, 2, 4096], BF16) for i in range(2)])
        QT = RR([P.sb(f"QT{i}", [128, 2, 2048], BF16) for i in range(2)])
        Vh = RR([P.sb(f"Vh{i}", [128, 32, 256], BF16) for i in range(2)])
        Pt = RR([P.sb(f"Pt{i}", [128, 512], BF16) for i in range(5)])
        rd, brd = P.sb("rd", [128, 512], F32)
        o1, bo1 = P.sb("o1", [128, 512], F32)
        o2, bo2 = P.sb("o2", [128, 512], F32)
        sqt, bsq = P.sb("sq", [128, 512], BF16)
        rst, brs = P.sb("rs", [128, 256], F32)
        of = RR([P.sb(f"of{i}", [128, 512], BF16) for i in range(2)])
        ones = cb[:, C_ONES:C_ONES + 128]
        flag = K["flag"][:, :]
        nlam = K["nlam"]
        gs = K["gs"]
        units = []
        for h in range(4):
            for qi in range(8):
                nkb = 18 + 2 * qi
                for kb in range(nkb):
                    units.append((h, qi, kb, nkb))
        st = {}
        cur = {"h": None}
        bU = (ps[2], ps[3])
        bD = ps[4]
        bN = ps[5]

        def stage1(u):
            h, qi, kb, nkb = units[u]
            if cur["h"] != h:
                cur["h"] = h
                kt, bkt = KT.next()
                qt, bqt = QT.next()
                vh, bvh = Vh.next()
                for cpt in range(2):
                    P.dma("sp", kt[:, cpt, :], S["Kf"][h * 2 + cpt], bkt, True)
                    P.dma("sp", qt[:, cpt, :], S["Qf"][h * 2 + cpt], bqt, True)
                P.dma("sp", vh[:], S["Vf"][:, h * 256:(h + 1) * 256].rearrange("(kb p) n -> p kb n", p=128), bvh, True)
                cur["kqv"] = (kt, bkt, qt, bqt, vh, bvh)
            kt, bkt, qt, bqt, vh, bvh = cur["kqv"]
            bS_ = ps[(0, 1, 6, 7)[u % 4]]
            qs = slice(qi * 256, qi * 256 + 256)
            ks = slice(kb * 128, kb * 128 + 128)

            def mm1(e):
                e.matmul(bS_[0][:, 0:256], kt[:, 0, ks], qt[:, 0, qs], start=True, stop=True)
                return e.matmul(bS_[0][:, 256:512], kt[:, 1, ks], qt[:, 1, qs], start=True, stop=True)
            P.c("pe", mm1, [bkt, bqt], [bS_[1]])
            pt, bpt = Pt.next()
            P.c("act", lambda e: e.activation(out=pt[:], in_=bS_[0][:], func=AF.Exp, scale=SCALE), [bS_[1]], [bpt])
            if kb >= nkb - 2:
                mc = C_M1 if kb == nkb - 2 else C_M2
                P.c("pool", lambda e: e.tensor_tensor(out=pt[:], in0=pt[:], in1=cb[:, mc:mc + 512], op=ALU.mult), [bpt], [bpt])
            st[u] = (pt, bpt, vh, bvh)

        def stage2(u):
            h, qi, kb, nkb = units[u]
            pt, bpt, vh, bvh = st.pop(u)
            first, last = kb == 0, kb == nkb - 1

            def mm2(e):
                for cpt in range(2):
                    for dvc in range(2):
                        e.matmul(bU[cpt][0][:, dvc * 256:dvc * 256 + 256], vh[:, kb, dvc * 128:dvc * 128 + 128],
                                 pt[:, cpt * 256:cpt * 256 + 256], start=(first and dvc == 0), stop=last, skip_group_check=True)
                for cpt in range(2):
                    ins = e.matmul(bD[0][:, cpt * 256:cpt * 256 + 256], flag if kb < 16 else ones,
                                   pt[:, cpt * 256:cpt * 256 + 256], start=(first and cpt == 0), stop=last, skip_group_check=True)
                return ins
            P.c("pe", mm2, [bpt, bvh], [bU[0][1], bU[1][1], bD[1]])
            if not last:
                return
            P.c("dve", lambda e: e.reciprocal(out=rd[:], in_=bD[0][:]), [bD[1]], [brd])
            for dvc in range(2):
                ds = slice(dvc * 256, dvc * 256 + 256)
                P.c("dve", lambda e, ds=ds: e.tensor_tensor(out=o1[:, ds], in0=bU[0][0][:, ds], in1=rd[:, 0:256], op=ALU.mult),
                    [bU[0][1], brd], [bo1])
                P.c("dve", lambda e, ds=ds: e.tensor_tensor(out=o2[:, ds], in0=bU[1][0][:, ds], in1=rd[:, 256:512], op=ALU.mult),
                    [bU[1][1], brd], [bo2])
            P.c("dve", lambda e: e.scalar_tensor_tensor(out=o1[:], in0=o2[:], scalar=nlam[:, 0:1], in1=o1[:], op0=ALU.mult, op1=ALU.add),
                [bo1, bo2], [bo1])
            pending.append((3, lambda: epi2(h, qi)))

        def epi2(h, qi):
            P.c("act", lambda e: e.activation(out=sqt[:], in_=o1[:], func=AF.Square), [bo1], [bsq])

            def mm3(e):
                e.matmul(bN[0][:, 0:256], cb[:, C_INV256:C_INV256 + 128], sqt[:, 0:256], start=True, stop=False)
                return e.matmul(bN[0][:, 0:256], cb[:, C_INV256:C_INV256 + 128], sqt[:, 256:512], start=False, stop=True)
            P.c("pe", mm3, [bsq], [bN[1]])
            P.c("act", lambda e: e.activation(out=rst[:], in_=bN[0][:, 0:256], func=AF.Sqrt, bias=K["cf"][:, F_EPS:F_EPS + 1], scale=1.0),
                [bN[1]], [brs])
            P.c("dve", lambda e: e.reciprocal(out=rst[:], in_=rst[:]), [brs], [brs])
            ot, bot = of.next()
            for dvc in range(2):
                ds = slice(dvc * 256, dvc * 256 + 256)
                P.c("dve", lambda e, ds=ds, dvc=dvc: e.scalar_tensor_tensor(out=ot[:, ds], in0=o1[:, ds], scalar=gs[:, dvc:dvc + 1],
                                                                           in1=rst[:], op0=ALU.mult, op1=ALU.mult), [bo1, brs], [bot])
            for dvc in range(2):
                P.dma("sp", S["Of"][h * 2 + dvc, :, qi * 256:qi * 256 + 256], ot[:, dvc * 256:dvc * 256 + 256], bot, False)

        pending = []
        n = len(units)
        for u in range(min(LOOK, n)):
            stage1(u)
        for u in range(n):
            if u + LOOK < n:
                stage1(u + LOOK)
            stage2(u)
            for item in list(pending):
                pending.remove(item)
                if item[0] == 0:
                    item[1]()
                else:
                    pending.append((item[0] - 1, item[1]))
        for item in pending:
            item[1]()
        P.run()


def out_phase(nc, ps, K, S, WA, WB, WO):
    with ExitStack() as es:
        reset_bufs(ps)
        P = Prog(nc, es, "op")
        wa, bwa = P.sb("wa", [128, 4, 2048], BF16)
        wb, bwb = P.sb("wb", [128, 8, 2048], BF16)
        wo, bwo = P.sb("wo", [128, 16, 2048], BF16)
        odt = RR([P.sb(f"od{i}", [128, 4, 512], BF16) for i in range(2)])
        oft = RR([P.sb(f"of{i}", [128, 8, 512], BF16) for i in range(2)])
        yT = RR([P.sb(f"y{i}", [128, 16, 512], BF16) for i in range(1)])
        sgd = RR([P.sb(f"sgd{i}", [128, 512], F32) for i in range(2)])
        sgf = RR([P.sb(f"sgf{i}", [128, 512], F32) for i in range(2)])
        ta = RR([P.sb(f"ta{i}", [128, 512], F32) for i in range(2)])
        tb_ = RR([P.sb(f"tb{i}", [128, 512], F32) for i in range(2)])
        xin = RR([P.sb(f"xin{i}", [128, 512], F32) for i in range(2)])
        xout = RR([P.sb(f"xo{i}", [128, 512], F32) for i in range(2)])
        P.dma("pool", wa[:], WA.rearrange("(k p) n -> p k n", p=128), bwa, True)
        for i in range(2):
            P.dma("pool", wb[:, i * 4:i * 4 + 4, :], WB.rearrange("(k p) n -> p k n", p=128)[:, i * 4:i * 4 + 4, :], bwb, True)
        for i in range(4):
            P.dma("pool", wo[:, i * 4:i * 4 + 4, :], WO.rearrange("(k p) n -> p k n", p=128)[:, i * 4:i * 4 + 4, :], bwo, True)
        u = 0

        def load_o(ti):
            ts_ = slice(ti * 512, ti * 512 + 512)
            od, bod = odt.next()
            of, bof = oft.next()
            for k in range(4):
                P.dma("sp", od[:, k, :], S["Od"][k, :, ts_], bod, True)
            for k in range(8):
                P.dma("sp", of[:, k, :], S["Of"][k, :, ts_], bof, True)
            return od, bod, of, bof
        nxt = load_o(0)
        for ti in range(4):
            ts = slice(ti * 512, ti * 512 + 512)
            od, bod, of, bof = nxt
            y, by = yT.next()
            for dc in range(16):
                bA, bB = ps[(u % 2) * 2], ps[(u % 2) * 2 + 1]
                u += 1
                cs = slice(dc * 128, dc * 128 + 128)

                def mm(e, od=od, of=of, cs=cs, bA=bA, bB=bB):
                    for k in range(4):
                        e.matmul(bA[0][:], wa[:, k, cs], od[:, k, :], start=(k == 0), stop=(k == 3))
                    for k in range(8):
                        ins = e.matmul(bB[0][:], wb[:, k, cs], of[:, k, :], start=(k == 0), stop=(k == 7))
                    return ins
                P.c("pe", mm, [bwa, bwb, bod, bof], [bA[1], bB[1]])
                g1, bg1 = sgd.next()
                g2, bg2 = sgf.next()
                a_, ba_ = ta.next()
                b_, bb_ = tb_.next()
                P.dma("sp", g1[:], S["SG"][dc, :, ts], bg1, True)
                P.dma("sp", g2[:], S["SG"][16 + dc, :, ts], bg2, True)
                P.c("dve", lambda e, a_=a_, g1=g1, bA=bA: e.tensor_tensor(out=a_[:], in0=g1[:], in1=bA[0][:], op=ALU.mult), [bg1, bA[1]], [ba_])
                P.c("dve", lambda e, b_=b_, g2=g2, bB=bB: e.tensor_tensor(out=b_[:], in0=g2[:], in1=bB[0][:], op=ALU.mult), [bg2, bB[1]], [bb_])
                P.c("dve", lambda e, y=y, dc=dc, a_=a_, b_=b_: e.tensor_tensor(out=y[:, dc, :], in0=a_[:], in1=b_[:], op=ALU.add),
                    [ba_, bb_], [by])
            if ti + 1 < 4:
                nxt = load_o(ti + 1)
            for dc2 in range(16):
                bC_ = ps[4 + dc2 % 4]
                cs = slice(dc2 * 128, dc2 * 128 + 128)

                def mm(e, y=y, cs=cs, bC_=bC_):
                    for k in range(16):
                        ins = e.matmul(bC_[0][:], wo[:, k, cs], y[:, k, :], start=(k == 0), stop=(k == 15))
                    return ins
                P.c("pe", mm, [bwo, by], [bC_[1]])
                xt, bx = xin.next()
                xo, bxo = xout.next()
                P.dma("sp", xt[:], S["x1T"][dc2, :, 2048 + ti * 512:2048 + ti * 512 + 512], bx, True)
                P.c("dve", lambda e, xo=xo, xt=xt, bC_=bC_: e.tensor_tensor(out=xo[:], in0=bC_[0][:], in1=xt[:], op=ALU.add), [bC_[1], bx], [bxo])
                P.dma("sp", S["x2T"][dc2, :, ts], xo[:], bxo, False)
        P.run()


IN_SHAPES = {
    "xT": ([16, 128, TFR], F32), "pos": ([32, TFR], I32), "cbf": ([128, NCB], F32), "cf32": ([128, NCF], F32),
    "flag": ([128, 128], F32), "lamv": ([128, 4, 128], F32),
    "ffn1_wg": ([D, DFF], F32), "ffn1_wu": ([D, DFF], F32), "ffn1_wd": ([DFF, D], F32),
    "ffn2_wg": ([D, DFF], F32), "ffn2_wu": ([D, DFF], F32), "ffn2_wd": ([DFF, D], F32),
    "w_in": ([D, DIN], F32), "w_a": ([512, D], F32), "w_b": ([1024, D], F32), "w_o": ([D, D], F32),
}


class LazyIns(dict):
    def __init__(self, nc):
        super().__init__()
        self.nc = nc

    def __missing__(self, name):
        shape, dt = IN_SHAPES[name]
        ap = self.nc.dram_tensor(name, shape, dt, kind="ExternalInput").ap()
        self[name] = ap
        return ap


def build(debug=False, upto=7, dump=()):
    nc = bass.Bass("TRN2", target_bir_lowering=False)
    ins = LazyIns(nc)
    kind = "ExternalOutput" if debug else "Internal"
    S = {}

    def scr(name, shape, dt):
        S[name] = nc.dram_tensor(name, shape, dt, kind=("ExternalOutput" if name in dump else "Internal")).ap()
    scr("x1T", [16, 128, TFR], F32)
    scr("Qd", [12, 128, TOWN], BF16)
    scr("Kd", [12, 128, TFR], BF16)
    scr("Vd", [TFR, 1536], BF16)
    scr("Qf", [8, 128, TOWN], BF16)
    scr("Kf", [8, 128, TFR], BF16)
    scr("Vf", [TFR, 1024], BF16)
    scr("SG", [32, 128, TOWN], F32)
    scr("Od", [4, 128, TOWN], BF16)
    scr("Of", [8, 128, TOWN], BF16)
    scr("x2T", [16, 128, TOWN], F32)
    outT = nc.dram_tensor("outT", [16, 128, TOWN], F32, kind="ExternalOutput").ap()
    with ExitStack() as es:
        ps = []
        for i in range(8):
            t = es.enter_context(nc.psum_tensor(f"ps{i}", [128, 512], F32))
            ps.append((t, Buf(f"ps{i}")))
        K = {}
        K["cb"] = es.enter_context(nc.sbuf_tensor("K_cb", [128, NCB], BF16))
        K["cf"] = es.enter_context(nc.sbuf_tensor("K_cf", [128, NCF], F32))
        K["flag"] = es.enter_context(nc.sbuf_tensor("K_flag", [128, 128], BF16))
        K["nlam"] = es.enter_context(nc.sbuf_tensor("K_nlam", [128, 1], F32))
        K["gs"] = es.enter_context(nc.sbuf_tensor("K_gs", [128, 2], F32))
        setup_phase(nc, K, ins)
        if upto >= 2:
            ffn_phase(nc, ps, K, ins["xT"], S["x1T"], [0, 1024, 2048, 3072], F_G1, ins["ffn1_wg"], ins["ffn1_wu"], ins["ffn1_wd"], "f1")
        if upto >= 3:
            proj_phase(nc, ps, K, S["x1T"], ins["w_in"], ins["pos"], S)
        if upto >= 4:
            dil_phase(nc, ps, K, S)
        if upto >= 5:
            diff_phase(nc, ps, K, S)
        if upto >= 6:
            out_phase(nc, ps, K, S, ins["w_a"], ins["w_b"], ins["w_o"])
        if upto >= 7:
            ffn_phase(nc, ps, K, S["x2T"], outT, [0, 1024], F_G2, ins["ffn2_wg"], ins["ffn2_wu"], ins["ffn2_wd"], "f2")
    return nc, list(ins.keys())


def host_consts():
    cb = np.zeros((128, NCB), np.float32)
    cb[:, C_INV2048:C_INV2048 + 128] = 1.0 / 2048
    cb[:, C_INV128:C_INV128 + 128] = 1.0 / 128
    cb[:, C_INV256:C_INV256 + 128] = 1.0 / 256
    cb[:, C_ONES:C_ONES + 128] = 1.0
    for m in range(16):
        cb[m + 16, C_PSW + m] = 1.0
        cb[m, C_PSW + m + 16] = 1.0
    ik = np.arange(128)[:, None]
    iq = np.arange(128)[None, :]
    cb[:, C_MDIL:C_MDIL + 128] = (ik >= iq)
    cb[:, C_MDIL + 128:C_MDIL + 256] = (ik <= iq)
    tri = (ik <= iq).astype(np.float32)
    m1 = np.concatenate([tri, np.ones((128, 128), np.float32)], 1)
    m2 = np.concatenate([np.zeros((128, 128), np.float32), tri], 1)
    cb[:, C_M1:C_M1 + 512] = np.concatenate([m1, m1], 1)
    cb[:, C_M2:C_M2 + 512] = np.concatenate([m2, m2], 1)
    return cb


def make_in_maps(x, positions, ffn1_norm, ffn1_w_gate, ffn1_w_up, ffn1_w_down, mix_norm, w_in,
                 dil_q_norm, dil_k_norm, diff_q_norm, diff_k_norm, diff_lq1, diff_lk1, diff_lq2, diff_lk2,
                 diff_subln, w_dil_branch, w_diff_branch, w_out, ffn2_norm, ffn2_w_gate, ffn2_w_up, ffn2_w_down):
    f = lambda a: np.ascontiguousarray(np.asarray(a, dtype=np.float32))
    x = np.asarray(x, np.float32)
    positions = np.asarray(positions, np.int32)
    cb = host_consts()
    cf = np.zeros((128, NCF), np.float32)
    cf[:, F_G1:F_G1 + 16] = f(ffn1_norm)[0].reshape(16, 128).T
    cf[:, F_GM:F_GM + 16] = f(mix_norm)[0].reshape(16, 128).T
    cf[:, F_G2:F_G2 + 16] = f(ffn2_norm)[0].reshape(16, 128).T
    cf[:, F_HG + 0] = f(dil_q_norm)[0]
    cf[:, F_HG + 1] = f(dil_k_norm)[0]
    cf[:, F_HG + 2] = f(diff_q_norm)[0]
    cf[:, F_HG + 3] = f(diff_k_norm)[0]
    cf[:, F_SUB:F_SUB + 2] = f(diff_subln)[0].reshape(2, 128).T
    inv = (np.float32(500000.0) ** (-np.arange(0, 32, 2, dtype=np.float32) / np.float32(32))).astype(np.float32)
    cf[0:16, F_INV] = inv
    cf[16:32, F_INV] = inv
    cf[0:16, F_SGN] = -1.0
    cf[16:32, F_SGN] = 1.0
    cf[:, F_EPS] = EPS
    lamv = np.stack([f(diff_lq1)[0], f(diff_lk1)[0], f(diff_lq2)[0], f(diff_lk2)[0]], 0)
    lamv = np.ascontiguousarray(np.broadcast_to(lamv[None], (128, 4, 128)))
    shared = {
        "cbf": cb, "cf32": cf, "lamv": lamv,
        "ffn1_wg": f(ffn1_w_gate)[0], "ffn1_wu": f(ffn1_w_up)[0], "ffn1_wd": f(ffn1_w_down)[0],
        "ffn2_wg": f(ffn2_w_gate)[0], "ffn2_wu": f(ffn2_w_up)[0], "ffn2_wd": f(ffn2_w_down)[0],
        "w_in": f(w_in)[0], "w_a": f(w_dil_branch)[0], "w_b": f(w_diff_branch)[0], "w_o": f(w_out)[0],
    }
    maps = []
    for core in range(8):
        b, half = core // 2, core % 2
        xT = np.zeros((D, TFR), np.float32)
        pos = np.zeros((TFR,), np.int32)
        xT[:, 2048:] = x[b, half * 2048:(half + 1) * 2048, :].T
        pos[2048:] = positions[b, half * 2048:(half + 1) * 2048]
        if half == 1:
            xT[:, :2048] = x[b, 0:2048, :].T
            pos[:2048] = positions[b, 0:2048]
        m = dict(shared)
        m["xT"] = np.ascontiguousarray(xT.reshape(16, 128, TFR))
        m["pos"] = np.ascontiguousarray(np.broadcast_to(pos[None], (32, TFR)))
        m["flag"] = np.full((128, 128), float(half), np.float32)
        maps.append(m)
    return maps


def kernel(**inputs):
    maps = make_in_maps(**inputs)
    nc, names = build()
    maps = [{k: m[k] for k in names} for m in maps]
    res = run_bass_kernel_spmd(nc, maps, core_ids=list(range(8)))
    out = np.empty((4, 4096, D), np.float32)
    for core in range(8):
        b, half = core // 2, core % 2
        oT = np.asarray(res.results[core]["outT"], np.float32).reshape(D, TOWN)
        out[b, half * 2048:(half + 1) * 2048, :] = oT.T
    return out
```
